# Optimizing a Trainium2 kernel written in Bass

```python
import math
import jax, jax.numpy as jnp
from jax import lax
import numpy as np

D_MODEL = 2048
BATCH = 2
SEQ = 4096
DEPTH = 4

GRID_W = 64
N_BRANCH = 4
BRANCH_W = D_MODEL // N_BRANCH
SHORT_CONV = 3
HEAD_DIM = 64
N_Q_HEADS = BRANCH_W // HEAD_DIM
N_KV_HEADS = 2
KV_W = N_KV_HEADS * HEAD_DIM
Q_BLOCK = 128
ROPE_THETA = 10000.0
RWKV_N = 64
RWKV_H = BRANCH_W // RWKV_N
DECAY_LORA = 96
ICLR_LORA = 96
LONG_CONV = 31
RMS_EPS = 1e-6
GN_EPS = 64e-5
LN_EPS = 1e-5

SPLIT_SIZES = (
    BRANCH_W, BRANCH_W, BRANCH_W, BRANCH_W,
    BRANCH_W, KV_W, KV_W, BRANCH_W,
    BRANCH_W, BRANCH_W, BRANCH_W, BRANCH_W,
    BRANCH_W, BRANCH_W, BRANCH_W,
    N_BRANCH * D_MODEL,
)
P_IN = 13 * BRANCH_W + 2 * KV_W + N_BRANCH * D_MODEL

kernel_name = "bidir_parallel_hybrid_conv_gqa_rwkv7_conformer"


def rms_norm(x, g, eps=RMS_EPS):
    xf = x.astype(jnp.float32)
    y = xf * lax.rsqrt(jnp.mean(xf * xf, axis=-1, keepdims=True) + eps)
    return (y * g.astype(jnp.float32)).astype(x.dtype)


def split_cols(proj):
    out, start = [], 0
    for n in SPLIT_SIZES:
        out.append(proj[..., start:start + n])
        start += n
    return out


def depthwise_conv(u, w):
    k = w.shape[0]
    return lax.conv_general_dilated(
        u, w[:, None, :].astype(u.dtype), window_strides=(1,),
        padding=[(k // 2, k // 2)], dimension_numbers=("NWC", "WIO", "NWC"),
        feature_group_count=u.shape[-1])


def axial_angles(seq_len):
    rows = seq_len // GRID_W
    row = jnp.repeat(jnp.arange(rows), GRID_W).astype(jnp.float32)
    col = jnp.tile(jnp.arange(GRID_W), rows).astype(jnp.float32)
    n = HEAD_DIM // 4
    inv = ROPE_THETA ** (-jnp.arange(n, dtype=jnp.float32) / n)
    return row[:, None] * inv, col[:, None] * inv


def rope_pair(x, ang):
    x1, x2 = jnp.split(x, 2, axis=-1)
    c = jnp.cos(ang)[:, None, :]
    s = jnp.sin(ang)[:, None, :]
    return jnp.concatenate([x1 * c - x2 * s, x2 * c + x1 * s], axis=-1)


def axial_rope(x, ang_row, ang_col):
    xf = x.astype(jnp.float32)
    xr, xc = jnp.split(xf, 2, axis=-1)
    return jnp.concatenate([rope_pair(xr, ang_row), rope_pair(xc, ang_col)], axis=-1).astype(x.dtype)


def gqa_attention(q, k, v):
    b, s, _, d = q.shape
    g = N_Q_HEADS // N_KV_HEADS
    nb = s // Q_BLOCK
    qb = q.reshape(b, nb, Q_BLOCK, N_KV_HEADS, g, d).transpose(1, 0, 2, 3, 4, 5)
    kf = k.astype(jnp.float32)
    vf = v.astype(jnp.float32)
    scale = 1.0 / math.sqrt(d)

    def block(qi):
        sc = jnp.einsum('bqhgd,bkhd->bhgqk', qi.astype(jnp.float32), kf) * scale
        p = jax.nn.softmax(sc, axis=-1)
        return jnp.einsum('bhgqk,bkhd->bqhgd', p, vf).astype(q.dtype)

    o = lax.map(block, qb)
    return o.transpose(1, 0, 2, 3, 4, 5).reshape(b, s, N_Q_HEADS * d)


def to_dirs(u):
    return jnp.stack([u, jnp.flip(u, axis=1)])


def dirs_flip(u2):
    return jnp.stack([u2[0], jnp.flip(u2[1], axis=1)])


def shift_prev(u2):
    return jnp.pad(u2, ((0, 0), (0, 0), (1, 0), (0, 0)))[:, :, :-1]


def rwkv7_bidir(h, r, k, v, w0, w_lora_a, w_lora_b, a0, a_lora_a, a_lora_b,
                mu_rkv, k_k, k_a, r_k, gn_g, gn_b):
    dt = r.dtype
    f32 = jnp.float32
    b, s, c = r.shape
    hf = h.astype(f32)
    r2, k2, v2 = to_dirs(r.astype(f32)), to_dirs(k.astype(f32)), to_dirs(v.astype(f32))
    mu = mu_rkv.astype(f32)
    r2 = r2 + (shift_prev(r2) - r2) * mu[:, 0][:, None, None, :]
    k2 = k2 + (shift_prev(k2) - k2) * mu[:, 1][:, None, None, :]
    v2 = v2 + (shift_prev(v2) - v2) * mu[:, 2][:, None, None, :]
    wl = w0.astype(f32)[:, None, None, :] + jnp.einsum(
        'ebsr,erc->ebsc', jnp.tanh(jnp.einsum('bsd,edr->ebsr', hf, w_lora_a.astype(f32))),
        w_lora_b.astype(f32))
    wl = dirs_flip(wl)
    decay = jnp.exp(-jnp.exp(-jax.nn.softplus(-wl) - 0.5))
    a = jax.nn.sigmoid(a0.astype(f32)[:, None, None, :] + jnp.einsum(
        'ebsr,erc->ebsc', jnp.einsum('bsd,edr->ebsr', hf, a_lora_a.astype(f32)),
        a_lora_b.astype(f32)))
    a = dirs_flip(a)

    hd = lambda u: u.reshape(2, b, s, RWKV_H, RWKV_N)
    r2, k2, v2, decay, a = hd(r2), hd(k2), hd(v2), hd(decay), hd(a)
    k_k_h = k_k.astype(f32).reshape(RWKV_H, RWKV_N)
    k_a_h = k_a.astype(f32).reshape(RWKV_H, RWKV_N)
    kk = k2 * k_k_h
    kk = kk / jnp.maximum(jnp.sqrt(jnp.sum(kk * kk, axis=-1, keepdims=True)), 1e-12)
    kt = k2 * (1.0 + (a - 1.0) * k_a_h)

    tm = lambda u: jnp.moveaxis(u, 2, 0)

    def step(state, inp):
        r_t, w_t, k_t, v_t, kk_t, a_t = inp
        sa = jnp.einsum('ebhvk,ebhk->ebhv', state, -kk_t)
        state = (state * w_t[..., None, :] + sa[..., :, None] * (kk_t * a_t)[..., None, :]
                 + v_t[..., :, None] * k_t[..., None, :])
        return state, jnp.einsum('ebhvk,ebhk->ebhv', state, r_t)

    s0 = jnp.zeros((2, b, RWKV_H, RWKV_N, RWKV_N), f32)
    _, y = lax.scan(step, s0, (tm(r2), tm(decay), tm(kt), tm(v2), tm(kk), tm(a)))
    y = jnp.moveaxis(y, 0, 2)
    mean = jnp.mean(y, axis=-1, keepdims=True)
    var = jnp.mean(jnp.square(y - mean), axis=-1, keepdims=True)
    yn = ((y - mean) * lax.rsqrt(var + GN_EPS)).reshape(2, b, s, c)
    yn = yn * gn_g.astype(f32) + gn_b.astype(f32)
    bonus = jnp.sum(r2 * kt * r_k.astype(f32), axis=-1, keepdims=True) * v2
    out = dirs_flip(yn + bonus.reshape(2, b, s, c))
    return (out[0] + out[1]).astype(dt)


def hybrid_layer(x, ang_row, ang_col, norm_g, w_in, b_gate, conv_a_w, q_norm_g, k_norm_g,
                 w0, w_lora_a, w_lora_b, a0, a_lora_a, a_lora_b, mu_rkv, k_k, k_a, r_k,
                 gn_g, gn_b, dw_w, dw_b, ln_g, ln_b, w_branch, w_out):
    b, s, _ = x.shape
    h = rms_norm(x, norm_g)
    proj = h @ w_in
    (a_bg, a_cg, a_x, a_z, q, k, v, b_z, c_r, c_k, c_v, c_z,
     d_val, d_gate, d_z, gates) = split_cols(proj)

    ya = a_bg * depthwise_conv(a_cg * a_x, conv_a_w)
    ya = ya * jax.nn.silu(a_z)

    qh = rms_norm(q.reshape(b, s, N_Q_HEADS, HEAD_DIM), q_norm_g)
    kh = rms_norm(k.reshape(b, s, N_KV_HEADS, HEAD_DIM), k_norm_g)
    qh = axial_rope(qh, ang_row, ang_col)
    kh = axial_rope(kh, ang_row, ang_col)
    yb = gqa_attention(qh, kh, v.reshape(b, s, N_KV_HEADS, HEAD_DIM))
    yb = yb * jax.nn.silu(b_z)

    yc = rwkv7_bidir(h, c_r, c_k, c_v, w0, w_lora_a, w_lora_b, a0, a_lora_a, a_lora_b,
                     mu_rkv, k_k, k_a, r_k, gn_g, gn_b)
    yc = yc * jax.nn.silu(c_z)

    u = d_val * jax.nn.sigmoid(d_gate)
    u = depthwise_conv(u, dw_w) + dw_b
    uf = u.astype(jnp.float32)
    mean = jnp.mean(uf, axis=-1, keepdims=True)
    var = jnp.mean(jnp.square(uf - mean), axis=-1, keepdims=True)
    un = ((uf - mean) * lax.rsqrt(var + LN_EPS) * ln_g.astype(jnp.float32)
          + ln_b.astype(jnp.float32)).astype(u.dtype)
    yd = jax.nn.silu(un) * jax.nn.silu(d_z)

    ys = jnp.stack([ya, yb, yc, yd], axis=2)
    bp = jnp.einsum('bskc,kcd->bskd', ys, w_branch)
    g = jax.nn.sigmoid(gates.reshape(b, s, N_BRANCH, D_MODEL) + b_gate)
    merged = jnp.sum(g * bp, axis=2)
    return x + merged @ w_out


def setup_inputs(seed: int = 0) -> dict:
    key = jax.random.key(seed)
    ks = jax.random.split(key, 32)
    L, D, W = DEPTH, D_MODEL, BRANCH_W
    nrm = lambda i, shape, scale: jax.random.normal(ks[i], shape, jnp.float32) * scale
    uni = lambda i, shape, lo, hi: jax.random.uniform(ks[i], shape, jnp.float32, lo, hi)
    return {
        "x": nrm(0, (BATCH, SEQ, D), 1.0),
        "norm_g": 1.0 + nrm(1, (L, D), 0.02),
        "w_in": nrm(2, (L, D, P_IN), D ** -0.5),
        "b_gate": nrm(3, (L, N_BRANCH, D), 0.1),
        "conv_a_w": nrm(4, (L, SHORT_CONV, W), SHORT_CONV ** -0.5),
        "q_norm_g": 1.0 + nrm(5, (L, HEAD_DIM), 0.02),
        "k_norm_g": 1.0 + nrm(6, (L, HEAD_DIM), 0.02),
        "w0": uni(7, (L, 2, W), -5.0, 1.0),
        "w_lora_a": nrm(8, (L, 2, D, DECAY_LORA), D ** -0.5),
        "w_lora_b": nrm(9, (L, 2, DECAY_LORA, W), 0.1 * DECAY_LORA ** -0.5),
        "a0": nrm(10, (L, 2, W), 0.1),
        "a_lora_a": nrm(11, (L, 2, D, ICLR_LORA), D ** -0.5),
        "a_lora_b": nrm(12, (L, 2, ICLR_LORA, W), 0.1 * ICLR_LORA ** -0.5),
        "mu_rkv": uni(13, (L, 2, 3, W), 0.0, 1.0),
        "k_k": 0.85 + nrm(14, (L, W), 0.05),
        "k_a": 1.0 + nrm(15, (L, W), 0.05),
        "r_k": nrm(16, (L, RWKV_H, RWKV_N), 0.1),
        "gn_g": 1.0 + nrm(17, (L, W), 0.02),
        "gn_b": nrm(18, (L, W), 0.02),
        "dw_w": nrm(19, (L, LONG_CONV, W), LONG_CONV ** -0.5),
        "dw_b": nrm(20, (L, W), 0.02),
        "ln_g": 1.0 + nrm(21, (L, W), 0.02),
        "ln_b": nrm(22, (L, W), 0.02),
        "w_branch": nrm(23, (L, N_BRANCH, W, D), W ** -0.5),
        "w_out": nrm(24, (L, D, D), 0.5 * D ** -0.5),
        "final_norm_g": 1.0 + nrm(25, (D,), 0.02),
    }


def reference(x, norm_g, w_in, b_gate, conv_a_w, q_norm_g, k_norm_g, w0, w_lora_a, w_lora_b,
              a0, a_lora_a, a_lora_b, mu_rkv, k_k, k_a, r_k, gn_g, gn_b, dw_w, dw_b,
              ln_g, ln_b, w_branch, w_out, final_norm_g):
    ang_row, ang_col = axial_angles(x.shape[1])
    for l in range(DEPTH):
        x = hybrid_layer(x, ang_row, ang_col, norm_g[l], w_in[l], b_gate[l], conv_a_w[l],
                         q_norm_g[l], k_norm_g[l], w0[l], w_lora_a[l], w_lora_b[l], a0[l],
                         a_lora_a[l], a_lora_b[l], mu_rkv[l], k_k[l], k_a[l], r_k[l],
                         gn_g[l], gn_b[l], dw_w[l], dw_b[l], ln_g[l], ln_b[l],
                         w_branch[l], w_out[l])
    return rms_norm(x, final_norm_g)
```

```python
import numpy as np
import concourse.bass as bass
import concourse.mybir as mybir
from concourse.bass_utils import run_bass_kernel_spmd

F32 = mybir.dt.float32
BF16 = mybir.dt.bfloat16
I32 = mybir.dt.int32
ALU = mybir.AluOpType
AF = mybir.ActivationFunctionType
AX = mybir.AxisListType

SEM_CAP = 4000
N_DMA_SEMS = 24


class V:
    __slots__ = ("t", "ap", "key")

    def __init__(self, t, ap, key):
        self.t, self.ap, self.key = t, ap, key

    def rr(self, pat, **kw):
        return V(self.t, self.ap.rearrange(pat, **kw), self.key)

    def bc(self, shape):
        return V(self.t, self.ap.to_broadcast(list(shape)), self.key)

    def __getitem__(self, idx):
        return V(self.t, self.ap[idx], self.key)

    def us(self, axis):
        return V(self.t, self.ap.unsqueeze(axis), self.key)

    def bitcast(self, dt):
        return V(self.t, self.ap.bitcast(dt), self.key)


class T:
    def __init__(self, h, name):
        self.h, self.name = h, name
        self.recs = {}
        self.is_psum = False

    def __getitem__(self, idx):
        return V(self, self.h[idx], None)

    def k(self, key):
        return _TK(self, key)


class _TK:
    def __init__(self, t, key):
        self.t, self.key = t, key

    def __getitem__(self, idx):
        return V(self.t, self.t.h[idx], self.key)


class _Scope:
    def __init__(self, p):
        self.p = p

    def __enter__(self):
        self.n = len(self.p._ctx)
        return self

    def __exit__(self, *a):
        self.p.barrier()
        while len(self.p._ctx) > self.n:
            self.p._ctx.pop().__exit__(None, None, None)


class _Rec:
    __slots__ = ("w", "r")

    def __init__(self):
        self.w, self.r = None, []


OUT_NAMES = ("out", "accum_out", "ap")


class Prog:
    ENG = ("pe", "act", "dve", "pool", "sp")

    def __init__(self):
        self.nc = bass.Bass("TRN2", target_bir_lowering=False)
        nc = self.nc
        self.eng = {"pe": nc.tensor, "act": nc.scalar, "dve": nc.vector, "pool": nc.gpsimd, "sp": nc.sync}
        self.ops = []
        self._ctx = []
        self.n_names = 0
        self.out_dma_ops = []

    def _enter(self, cm):
        h = cm.__enter__()
        self._ctx.append(cm)
        return h

    def sb(self, name, shape, dt=F32):
        return T(self._enter(self.nc.sbuf_tensor(name, list(shape), dt)), name)

    def ps(self, name, shape, dt=F32):
        t = T(self._enter(self.nc.psum_tensor(name, list(shape), dt)), name)
        t.is_psum = True
        return t

    def dram(self, name, shape, dt=F32, kind="Internal"):
        return T(self.nc.dram_tensor(name, list(shape), dt, kind=kind).ap(), name)

    def sem(self, name):
        return self._enter(self.nc.semaphore(name))

    def I(self, eng, method, acc=False, **kw):
        reads, writes = [], []
        for k, v in kw.items():
            if isinstance(v, V):
                (writes if k in OUT_NAMES else reads).append(v)
        if acc:
            for v in list(writes):
                reads.append(v)
        is_dma = method == "dma_start"
        op = dict(eng=eng, method=method, kw=kw, reads=reads, writes=writes, dma=is_dma,
                  idx=len(self.ops), mark=False, deps=[])
        self.ops.append(op)
        return op

    def barrier(self):
        last = {}
        dmas = []
        for op in self.ops:
            if op["dma"]:
                dmas.append(op["idx"])
            elif op["method"] is not None:
                last[op["eng"]] = op["idx"]
        deps = list(last.values()) + dmas[-3 * N_DMA_SEMS:]
        for e in self.ENG:
            op = dict(eng=e, method=None, kw={}, reads=[], writes=[], dma=False,
                      idx=len(self.ops), mark=False, deps=[], xdeps=[d for d in deps])
            self.ops.append(op)

    def scope(self):
        return _Scope(self)

    def dma(self, q, out, in_, final=False, **kw):
        op = self.I(q, "dma_start", out=out, in_=in_, **kw)
        if final:
            self.out_dma_ops.append(op)
        return op

    @staticmethod
    def _recs(v, create):
        t = v.t
        if v.key is None:
            ks = list(t.recs.keys())
            if None not in t.recs and create:
                t.recs[None] = _Rec()
                ks.append(None)
            return [t.recs[k] for k in ks]
        out = []
        if v.key not in t.recs and create:
            t.recs[v.key] = _Rec()
        if v.key in t.recs:
            out.append(t.recs[v.key])
        if None in t.recs:
            out.append(t.recs[None])
        return out

    def finalize(self):
        ops = self.ops
        fin = dict(eng="pool", method=None, kw={}, reads=[], writes=[], dma=False, idx=len(ops),
                   mark=False, deps=[o["idx"] for o in self.out_dma_ops])
        for op in ops:
            raw, other = set(), set()
            for v in op["reads"]:
                for rec in self._recs(v, False):
                    if rec.w is not None:
                        raw.add(rec.w)
                    if v.t.is_psum:
                        other.update(rec.r)
            for v in op["writes"]:
                for rec in self._recs(v, False):
                    if rec.w is not None:
                        other.add(rec.w)
                    other.update(rec.r)
            deps = []
            for d in raw | other:
                if d == op["idx"]:
                    continue
                dop = ops[d]
                if dop["eng"] == op["eng"] and not dop["dma"] and not op["dma"]:
                    if op["eng"] == "pe":
                        continue
                    if d not in raw:
                        continue
                deps.append(d)
            op["deps"] = deps + [d for d in op.get("xdeps", []) if not (ops[d]["eng"] == op["eng"] and not ops[d]["dma"])]
            for v in op["reads"]:
                if v.key is None:
                    if None not in v.t.recs:
                        v.t.recs[None] = _Rec()
                    for rec in v.t.recs.values():
                        rec.r.append(op["idx"])
                else:
                    if v.key not in v.t.recs:
                        v.t.recs[v.key] = _Rec()
                        if None in v.t.recs:
                            v.t.recs[v.key].w = v.t.recs[None].w
                            v.t.recs[v.key].r = list(v.t.recs[None].r)
                    v.t.recs[v.key].r.append(op["idx"])
            for v in op["writes"]:
                if v.key is None:
                    if None not in v.t.recs:
                        v.t.recs[None] = _Rec()
                    for rec in v.t.recs.values():
                        rec.w = op["idx"]
                        rec.r = []
                else:
                    if v.key not in v.t.recs:
                        v.t.recs[v.key] = _Rec()
                    rec = v.t.recs[v.key]
                    rec.w = op["idx"]
                    rec.r = []
        allops = ops + [fin]
        for op in allops:
            for d in op["deps"]:
                ops[d]["mark"] = True
        for op in ops:
            if op["dma"]:
                op["mark"] = True
        nc = self.nc
        eng_sems = {e: [] for e in self.ENG}
        eng_cnt = {e: 0 for e in self.ENG}
        dma_sems = [self.sem(f"dq{i}") for i in range(N_DMA_SEMS)]
        dma_val = [0] * N_DMA_SEMS
        dma_rr = 0
        for op in ops:
            if not op["mark"]:
                continue
            if op["dma"]:
                s = dma_rr
                dma_rr = (dma_rr + 1) % N_DMA_SEMS
                op["prev_ev"] = (("d", s), dma_val[s])
                dma_val[s] += 16
                op["ev"] = (("d", s), dma_val[s])
                op["inc"] = (dma_sems[s], 16)
            else:
                e = op["eng"]
                if not eng_sems[e] or eng_cnt[e] >= SEM_CAP:
                    eng_sems[e].append(self.sem(f"s_{e}{len(eng_sems[e])}"))
                    eng_cnt[e] = 0
                eng_cnt[e] += 1
                si = len(eng_sems[e]) - 1
                op["ev"] = ((e, si), eng_cnt[e])
                op["inc"] = (eng_sems[e][si], 1)

        def semobj(sid):
            return dma_sems[sid[1]] if sid[0] == "d" else eng_sems[sid[0]][sid[1]]

        waited = {e: {} for e in self.ENG}
        n_wait = 0
        for op in allops:
            e = op["eng"]
            E = self.eng[e]
            evs = [ops[d]["ev"] for d in op["deps"]]
            if op["dma"]:
                evs.append(op["prev_ev"])
            need = {}
            for sid, val in evs:
                if val <= 0:
                    continue
                if waited[e].get(sid, 0) >= val:
                    continue
                need[sid] = max(need.get(sid, 0), val)
            for sid, val in need.items():
                E.wait_ge(semobj(sid), val)
                waited[e][sid] = val
                n_wait += 1
            if op["method"] is None:
                continue
            kw = {k: (v.ap if isinstance(v, V) else v) for k, v in op["kw"].items()}
            ins = getattr(E, op["method"])(**kw)
            if op["mark"]:
                ins.then_inc(*op["inc"])
        self.stats = dict(n_ops=len(ops), n_wait=n_wait,
                          n_sems=N_DMA_SEMS + sum(len(v) for v in eng_sems.values()))
        return nc

    def close(self):
        for cm in reversed(self._ctx):
            cm.__exit__(None, None, None)
        self._ctx = []


NT = 1024
HALO = 16
NX = NT + 2 * HALO
BLKS = [(0, 512), (512, 512), (1024, 32)]
RMS_EPS = 1e-6
LN_EPS = 1e-5
GN_EPS = 64e-5

VF_NORMG = 0
VF_CONVA = 16
VF_DWW = 28
VF_DWB = 152
VF_LNG = 156
VF_LNB = 160
VF_N = 164


def load_w(p, wt, wsrc, c0, ncols, q="pool"):
    for cg in range(4):
        src = V(wsrc, wsrc.h[cg * 512:(cg + 1) * 512, c0:c0 + ncols].rearrange("(c p) n -> p c n", p=128), None)
        p.dma(q, wt[:, cg * 4:(cg + 1) * 4, 0:ncols], src)


def rsqrt(p, out, in_, scale, eps):
    p.I("act", "activation", out=out, in_=in_, func=AF.Sqrt, bias=eps, scale=scale)
    p.I("dve", "reciprocal", out=out, in_=out)


def rmsnorm_hT(p, xT, vecF, hT, ones, pss, ncols=NX, blks=BLKS):
    with p.scope():
        xs = p.sb("xs", [128, 16, ncols], F32)
        sq = [p.sb(f"sq{i}", [128, 512], F32) for i in range(2)]
        rstd = p.sb("rstd", [128, 512], F32)
        for cg in range(4):
            src = V(xT, xT.h[cg * 512:(cg + 1) * 512, :].rearrange("(c p) n -> p c n", p=128), None)
            p.dma("sp", xs[:, cg * 4:(cg + 1) * 4, :], src)
        for bi, (b0, bn) in enumerate(blks):
            ps = pss[bi % 2]
            for d in range(16):
                s = sq[d % 2]
                p.I("act", "activation", out=s[:, 0:bn], in_=xs[:, d, b0:b0 + bn], func=AF.Square)
                p.I("pe", "matmul", acc=(d > 0), out=ps[:, 0:bn], lhsT=ones[:, :], rhs=s[:, 0:bn],
                    start=(d == 0), stop=(d == 15))
            rsqrt(p, rstd[:, 0:bn], ps[:, 0:bn], 1.0 / 2048, RMS_EPS)
            for d in range(16):
                e = "dve"
                p.I(e, "scalar_tensor_tensor", out=hT[:, d, b0:b0 + bn], in0=xs[:, d, b0:b0 + bn],
                    scalar=vecF[:, VF_NORMG + d:VF_NORMG + d + 1], in1=rstd[:, 0:bn], op0=ALU.mult, op1=ALU.mult)


def gemm_F(p, hT, wt, j, ps, b0, bn):
    for d in range(16):
        p.I("pe", "matmul", acc=(d > 0), out=ps[:, 0:bn], lhsT=wt[:, d, j * 128:(j + 1) * 128],
            rhs=hT[:, d, b0:b0 + bn], start=(d == 0), stop=(d == 15))


def branch_AD(p, hT, w_in, vecF, ones, wts, pss, yaT_d, ydT_d, bzT_d):
    with p.scope():
        cgx = p.sb("cgx", [128, 4, NX], F32)
        ga = p.sb("ga", [128, 4, NX], F32)
        tmp = [p.sb(f"tmpF{i}", [128, 512], F32) for i in range(2)]
        wi = 0
        pi = 0
        for g in range(4):
            wt = wts[wi % 2]; wi += 1
            load_w(p, wt, w_in, g * 512, 512)
            for j in range(4):
                for (b0, bn) in BLKS:
                    ps = pss[pi % 2]; pi += 1
                    gemm_F(p, hT, wt, j, ps, b0, bn)
                    if g == 0:
                        p.I("act", "activation", out=ga[:, j, b0:b0 + bn], in_=ps[:, 0:bn], func=AF.Copy)
                    elif g == 1:
                        p.I("act", "activation", out=cgx[:, j, b0:b0 + bn], in_=ps[:, 0:bn], func=AF.Copy)
                    elif g == 2:
                        p.I("dve", "tensor_tensor", out=cgx[:, j, b0:b0 + bn], in0=ps[:, 0:bn],
                            in1=cgx[:, j, b0:b0 + bn], op=ALU.mult)
                    else:
                        t = tmp[pi % 2]
                        p.I("act", "activation", out=t[:, 0:bn], in_=ps[:, 0:bn], func=AF.Silu)
                        p.I("dve", "tensor_tensor", out=ga[:, j, b0:b0 + bn], in0=t[:, 0:bn],
                            in1=ga[:, j, b0:b0 + bn], op=ALU.mult)
        ya = p.sb("ya", [128, 4, NT], F32)
        for j in range(4):
            e = "dve"
            w = lambda k: vecF[:, VF_CONVA + j * 3 + k:VF_CONVA + j * 3 + k + 1]
            p.I(e, "tensor_scalar", out=ya[:, j, :], in0=cgx[:, j, HALO - 1:HALO - 1 + NT], scalar1=w(0), scalar2=None,
                op0=ALU.mult)
            for k in (1, 2):
                p.I(e, "scalar_tensor_tensor", out=ya[:, j, :], in0=cgx[:, j, HALO - 1 + k:HALO - 1 + k + NT],
                    scalar=w(k), in1=ya[:, j, :], op0=ALU.mult, op1=ALU.add)
            p.I(e, "tensor_tensor", out=ya[:, j, :], in0=ya[:, j, :], in1=ga[:, j, HALO:HALO + NT], op=ALU.mult)
            p.dma("sp", yaT_d[j * 128:(j + 1) * 128, :], ya[:, j, :], final=True)
    with p.scope():
        u = p.sb("u", [128, 4, NX], F32)
        sz = p.sb("sz", [128, 4, NT], F32)
        acc = p.sb("acc", [128, 4, NT], F32)
        tmp = [p.sb(f"tmpD{i}", [128, 512], F32) for i in range(2)]
        wi = 0
        pi = 0
        D0 = 2048 + 1280 + 2048
        for g in range(2):
            wt = wts[wi % 2]; wi += 1
            load_w(p, wt, w_in, D0 + g * 512, 512)
            for j in range(4):
                for (b0, bn) in BLKS:
                    ps = pss[pi % 2]; pi += 1
                    gemm_F(p, hT, wt, j, ps, b0, bn)
                    if g == 0:
                        p.I("act", "activation", out=u[:, j, b0:b0 + bn], in_=ps[:, 0:bn], func=AF.Copy)
                    else:
                        t = tmp[pi % 2]
                        p.I("act", "activation", out=t[:, 0:bn], in_=ps[:, 0:bn], func=AF.Sigmoid)
                        p.I("dve", "tensor_tensor", out=u[:, j, b0:b0 + bn], in0=t[:, 0:bn],
                            in1=u[:, j, b0:b0 + bn], op=ALU.mult)
        wt = wts[wi % 2]; wi += 1
        load_w(p, wt, w_in, D0 + 1024, 512)
        for j in range(4):
            for tb in range(2):
                ps = pss[pi % 2]; pi += 1
                gemm_F(p, hT, wt, j, ps, HALO + tb * 512, 512)
                p.I("act", "activation", out=sz[:, j, tb * 512:(tb + 1) * 512], in_=ps[:, :], func=AF.Silu)
        for j in range(4):
            e = "dve"
            w = lambda k: vecF[:, VF_DWW + j * 31 + k:VF_DWW + j * 31 + k + 1]
            p.I(e, "tensor_scalar", out=acc[:, j, :], in0=u[:, j, HALO - 15:HALO - 15 + NT], scalar1=w(0),
                scalar2=vecF[:, VF_DWB + j:VF_DWB + j + 1], op0=ALU.mult, op1=ALU.add)
            for k in range(1, 31):
                p.I(e, "scalar_tensor_tensor", out=acc[:, j, :], in0=u[:, j, HALO - 15 + k:HALO - 15 + k + NT],
                    scalar=w(k), in1=acc[:, j, :], op0=ALU.mult, op1=ALU.add)
        mean = p.sb("mean", [128, 512], F32)
        rs = p.sb("rs", [128, 512], F32)
        for tb in range(2):
            sl = slice(tb * 512, (tb + 1) * 512)
            ps1, ps2 = pss[0], pss[1]
            for j in range(4):
                p.I("pe", "matmul", acc=(j > 0), out=ps1[:, :], lhsT=ones[:, :], rhs=acc[:, j, sl], start=(j == 0), stop=(j == 3))
            for j in range(4):
                t = tmp[j % 2]
                p.I("act", "activation", out=t[:, :], in_=acc[:, j, sl], func=AF.Square)
                p.I("pe", "matmul", acc=(j > 0), out=ps2[:, :], lhsT=ones[:, :], rhs=t[:, :], start=(j == 0), stop=(j == 3))
            p.I("dve", "tensor_scalar", out=mean[:, :], in0=ps1[:, :], scalar1=1.0 / 512, scalar2=None, op0=ALU.mult)
            p.I("dve", "tensor_tensor", out=rs[:, :], in0=mean[:, :], in1=mean[:, :], op=ALU.mult)
            p.I("dve", "scalar_tensor_tensor", out=rs[:, :], in0=ps2[:, :], scalar=1.0 / 512, in1=rs[:, :],
                op0=ALU.mult, op1=ALU.subtract)
            rsqrt(p, rs[:, :], rs[:, :], 1.0, LN_EPS)
            for j in range(4):
                e = "dve" if j % 2 == 0 else "pool"
                p.I(e, "tensor_tensor", out=acc[:, j, sl], in0=acc[:, j, sl], in1=mean[:, :], op=ALU.subtract)
                p.I(e, "tensor_tensor", out=acc[:, j, sl], in0=acc[:, j, sl], in1=rs[:, :], op=ALU.mult)
                p.I("act", "activation", out=acc[:, j, sl], in_=acc[:, j, sl], func=AF.Silu,
                    bias=vecF[:, VF_LNB + j:VF_LNB + j + 1], scale=vecF[:, VF_LNG + j:VF_LNG + j + 1])
                p.I(e, "tensor_tensor", out=acc[:, j, sl], in0=acc[:, j, sl], in1=sz[:, j, sl], op=ALU.mult)
        for j in range(4):
            p.dma("sp", ydT_d[j * 128:(j + 1) * 128, :], acc[:, j, :], final=True)
        wt = wts[wi % 2]; wi += 1
        load_w(p, wt, w_in, 2048 + 768, 512)
        for j in range(4):
            for tb in range(2):
                ps = pss[pi % 2]; pi += 1
                gemm_F(p, hT, wt, j, ps, HALO + tb * 512, 512)
                p.I("act", "activation", out=sz[:, j, tb * 512:(tb + 1) * 512], in_=ps[:, :], func=AF.Silu)
            p.dma("sp", bzT_d[j * 128:(j + 1) * 128, :], sz[:, j, :], final=True)


def gemm_T(p, hT, wt, ncols, ps, tok0):
    for d in range(16):
        p.I("pe", "matmul", acc=(d > 0), out=ps[:, 0:ncols], lhsT=hT[:, d, tok0:tok0 + 128],
            rhs=wt[:, d, 0:ncols], start=(d == 0), stop=(d == 15))


VT_MU = 0
VT_W0 = 3072
VT_A0 = 4096
VT_KK = 5120
VT_KA = 5632
VT_RK = 6144
VT_GNG = 6656
VT_GNB = 7168
VT_QG = 7680
VT_KG = 7744
VT_N = 7808


def qk_norm_rope(p, src, nh, g_off, vecT, rope_tt, ident, pst, outT, scale, wk):
    n = nh * 64
    sq, ss, t1, t2, qn, qr = wk["sq"], wk["ss"], wk["t1"], wk["t2"], wk["qn"], wk["qr"]
    p.I("act", "activation", out=sq[:, 0:n], in_=src, func=AF.Square)
    p.I("dve", "tensor_reduce", out=ss[:, 0:nh], in_=sq[:, 0:n].rr("p (h k) -> p h k", k=64), axis=AX.X, op=ALU.add)
    rsqrt(p, ss[:, 0:nh], ss[:, 0:nh], 1.0 / 64, RMS_EPS)
    if scale != 1.0:
        p.I("dve", "tensor_scalar", out=ss[:, 0:nh], in0=ss[:, 0:nh], scalar1=scale, scalar2=None, op0=ALU.mult)
    v3 = lambda t: t[:, 0:n].rr("p (h k) -> p h k", k=64)
    p.I("dve", "tensor_tensor", out=v3(qn), in0=src.rr("p (h k) -> p h k", k=64),
        in1=ss[:, 0:nh].us(2).bc([128, nh, 64]), op=ALU.mult)
    p.I("dve", "tensor_tensor", out=v3(qn), in0=v3(qn),
        in1=vecT[:, g_off:g_off + 64].us(1).bc([128, nh, 64]), op=ALU.mult)
    v5 = lambda t: t[:, 0:n].rr("p (h a b k) -> p h a b k", a=2, b=2, k=16)
    x0 = v5(qn)[:, :, :, 0, :]
    x1 = v5(qn)[:, :, :, 1, :]
    o0 = v5(qr)[:, :, :, 0, :]
    o1 = v5(qr)[:, :, :, 1, :]
    cs = rope_tt[:, 0:32].rr("p (a k) -> p a k", k=16).us(1).bc([128, nh, 2, 16])
    sn = rope_tt[:, 32:64].rr("p (a k) -> p a k", k=16).us(1).bc([128, nh, 2, 16])
    h4 = lambda t: t[:, 0:n // 2].rr("p (h a k) -> p h a k", a=2, k=16)
    p.I("dve", "tensor_tensor", out=h4(t1), in0=x0, in1=cs, op=ALU.mult)
    p.I("pool", "tensor_tensor", out=h4(t2), in0=x1, in1=sn, op=ALU.mult)
    p.I("dve", "tensor_tensor", out=o0, in0=h4(t1), in1=h4(t2), op=ALU.subtract)
    p.I("pool", "tensor_tensor", out=h4(t2), in0=x1, in1=cs, op=ALU.mult)
    p.I("dve", "tensor_tensor", out=h4(t1), in0=x0, in1=sn, op=ALU.mult)
    p.I("dve", "tensor_tensor", out=o1, in0=h4(t1), in1=h4(t2), op=ALU.add)
    for h0 in range(0, nh, 4):
        hn = min(4, nh - h0)
        for h in range(h0, h0 + hn):
            p.I("pe", "transpose", out=pst[0:64, (h - h0) * 128:(h - h0 + 1) * 128], in_=qr[:, h * 64:(h + 1) * 64],
                identity=ident)
        p.I("act", "activation", out=outT[:, h0:h0 + hn, :], in_=pst[0:64, 0:hn * 128].rr("p (h t) -> p h t", t=128),
            func=AF.Copy)


def attn_prep(p, hT, w_in, vecT, rope, ident, wts, pss, qT_d, kT_d, v_d):
    with p.scope():
        wk = dict(sq=p.sb("aq_sq", [128, 512], F32), ss=p.sb("aq_ss", [128, 8], F32),
                  t1=p.sb("aq_t1", [128, 256], F32), t2=p.sb("aq_t2", [128, 256], F32),
                  qn=p.sb("aq_qn", [128, 512], F32), qr=p.sb("aq_qr", [128, 512], F32))
        qs = p.sb("aq_qs", [128, 512], F32)
        qT = [p.sb(f"aq_qT{i}", [64, 8, 128], F32) for i in range(2)]
        kT = [p.sb(f"aq_kT{i}", [64, 2, 128], F32) for i in range(2)]
        wq, wkv = wts
        load_w(p, wq, w_in, 2048, 512)
        load_w(p, wkv, w_in, 2560, 256)
        for tt in range(8):
            tok0 = HALO + tt * 128
            ps = pss[tt % 2]
            gemm_T(p, hT, wq, 512, ps, tok0)
            p.I("act", "activation", out=qs[:, :], in_=ps[:, :], func=AF.Copy)
            qk_norm_rope(p, qs[:, :], 8, VT_QG, vecT, rope[:, tt, :], ident, pss[2 + tt % 2], qT[tt % 2], 0.125, wk)
            p.dma("sp", qT_d[:, :, tt * 128:(tt + 1) * 128], qT[tt % 2][:, :, :], final=True)
            ps = pss[4 + tt % 2]
            gemm_T(p, hT, wkv, 256, ps, tok0)
            p.I("act", "activation", out=qs[:, 0:256], in_=ps[:, 0:256], func=AF.Copy)
            p.dma("sp", v_d[tt * 128:(tt + 1) * 128, :], qs[:, 128:256], final=True)
            qk_norm_rope(p, qs[:, 0:128], 2, VT_KG, vecT, rope[:, tt, :], ident, pss[6 + tt % 2], kT[tt % 2], 1.0, wk)
            p.dma("sp", kT_d[:, :, tt * 128:(tt + 1) * 128], kT[tt % 2][:, :, :], final=True)


C_R0 = 2048 + 1280
NCONST = 16


def rwkv_gemms(p, hT, w_in, lora_a_d, wts, pss, pc_d, czs_d, th):
    with p.scope():
        ev = [p.sb(f"rg_ev{i}", [128, 512], F32) for i in range(3)]
        n = 0
        for q in range(4):
            wt = wts[q % 2]
            load_w(p, wt, w_in, C_R0 + q * 512, 512)
            for tt in range(8):
                for win, sh in enumerate((0, -1, 1)):
                    if q == 3 and win > 0:
                        continue
                    ps = pss[n % 4]
                    e = ev[n % 3]
                    n += 1
                    gemm_T(p, hT, wt, 512, ps, HALO + tt * 128 + sh)
                    if q == 3:
                        p.I("act", "activation", out=e[:, :], in_=ps[:, :], func=AF.Silu)
                        p.dma("sp", czs_d[tt * 128:(tt + 1) * 128, :], e[:, :], final=True)
                    else:
                        if n % 2:
                            p.I("act", "activation", out=e[:, :], in_=ps[:, :], func=AF.Copy)
                        else:
                            p.I("dve", "tensor_copy", out=e[:, :], in_=ps[:, :])
                        p.dma("sp", pc_d[q * 3 + win, tt * 128:(tt + 1) * 128, :], e[:, :])
        wt = wts[0]
        load_w(p, wt, lora_a_d, 0, 384)
        for lo in range(4):
            for tb in range(2):
                ps = pss[4 + (lo * 2 + tb) % 2]
                for d in range(16):
                    p.I("pe", "matmul", acc=(d > 0), out=ps[0:96, :], lhsT=wt[:, d, lo * 96:(lo + 1) * 96],
                        rhs=hT[:, d, HALO + tb * 512:HALO + (tb + 1) * 512], start=(d == 0), stop=(d == 15))
                p.I("act", "activation", out=th[:, lo, tb * 512:(tb + 1) * 512], in_=ps[0:96, :],
                    func=(AF.Tanh if lo < 2 else AF.Copy))


def rwkv_pass1(p, pc_d, lora_b_d, th, vecT, consts, pss, bonus_d, v2_d, QT_d, Yl_d, GH_d, segT_d):
    ident = consts[:, 0, :]
    ones = consts[:, 5, :]
    with p.scope():
        lb = p.sb("rw_lb", [96, 4, 512], F32)
        p.dma("sp", lb[:, :, :], lora_b_d[:, :, :])
        S = p.sb("rw_S", [128, 17, 1024], F32)
        names = ["r2", "k2", "v2", "lw", "aa", "kk", "kt", "bb", "cum", "wt", "Bt", "Kt", "Rt", "Bh", "Kh", "t1", "t2"]
        sl = {n: i for i, n in enumerate(names)}
        W = lambda n: S.k(n)[:, sl[n], :]
        We = lambda n, e: S.k(n)[:, sl[n], e * 512:(e + 1) * 512]
        X = p.sb("rw_X", [128, 16, 128], F32)
        rkv = p.sb("rw_rkv", [128, 9, 512], F32)
        ssq = p.sb("rw_ssq", [128, 16], F32)
        NU = 4
        T5 = [p.sb(f"rw_T5_{i}", [128, 5, 128], F32) for i in range(NU)]
        NP = [p.sb(f"rw_NP_{i}", [128, 2, 2, 128], F32) for i in range(NU)]
        TR = [p.sb(f"rw_TR_{i}", [128, 4, 128], F32) for i in range(NU)]
        for t_ in TR:
            p.I("pool", "memset", ap=t_[:, :, :], constant=0.0)
        OUTQ = [p.sb(f"rw_oq_{i}", [64, 128], F32) for i in range(NU)]
        OUTY = [p.sb(f"rw_oy_{i}", [128, 64], F32) for i in range(NU)]
        OUTG = [p.sb(f"rw_og_{i}", [64, 128], F32) for i in range(NU)]
        dgW = p.sb("rw_dgW", [64, 16, 64], F32)
        omka = p.sb("rw_omka", [128, 512], F32)
        vt = lambda off, n=512: vecT[:, off:off + n]
        p.I("dve", "tensor_scalar", out=omka[:, :], in0=vt(VT_KA), scalar1=-1.0, scalar2=1.0, op0=ALU.mult, op1=ALU.add)
        import os
        for tt in range(8):
            tsl = slice(tt * 128, (tt + 1) * 128)
            p.dma("sp", rkv[:, :, :], V(pc_d.t if isinstance(pc_d, V) else pc_d, pc_d.h[0:9, tsl, :].rearrange("q t c -> t q c"), None))
            for e in range(2):
                for qi, nm in enumerate(("r2", "k2", "v2")):
                    cur = rkv[:, qi * 3, :]
                    sh = rkv[:, qi * 3 + 1 + e, :]
                    eng = "dve" if (qi + e) % 2 == 0 else "pool"
                    p.I(eng, "tensor_tensor", out=We(nm, e), in0=sh, in1=cur, op=ALU.subtract)
                    p.I(eng, "tensor_tensor", out=We(nm, e), in0=We(nm, e), in1=vt(VT_MU + (e * 3 + qi) * 512), op=ALU.mult)
                    p.I(eng, "tensor_tensor", out=We(nm, e), in0=We(nm, e), in1=cur, op=ALU.add)
            for e in range(2):
                for kind in range(2):
                    lo = kind * 2 + e
                    ps = pss[lo % 2]
                    p.I("pe", "matmul", out=ps[:, :], lhsT=th[:, lo, tsl], rhs=lb[:, lo, :], start=True, stop=True)
                    dst = We("lw" if kind == 0 else "aa", e)
                    p.I("dve", "tensor_tensor", out=dst, in0=ps[:, :], in1=vt((VT_W0 if kind == 0 else VT_A0) + e * 512), op=ALU.add)
                    p.I("act", "activation", out=dst, in_=dst, func=AF.Sigmoid)
            p.I("pool", "tensor_scalar", out=W("lw"), in0=W("lw"), scalar1=-0.6065306597126334, scalar2=None, op0=ALU.mult)
            for e in range(2):
                p.I("dve", "tensor_tensor", out=We("kk", e), in0=We("k2", e), in1=vt(VT_KK), op=ALU.mult)
                p.I("pool", "tensor_tensor", out=We("kt", e), in0=We("aa", e), in1=vt(VT_KA), op=ALU.mult)
                p.I("pool", "tensor_tensor", out=We("kt", e), in0=We("kt", e), in1=omka[:, :], op=ALU.add)
                p.I("pool", "tensor_tensor", out=We("kt", e), in0=We("kt", e), in1=We("k2", e), op=ALU.mult)
            p.I("act", "activation", out=W("t1"), in_=W("kk"), func=AF.Square)
            p.I("dve", "tensor_reduce", out=ssq[:, :], in_=W("t1").rr("p (u k) -> p u k", k=64), axis=AX.X, op=ALU.add)
            p.I("dve", "tensor_scalar", out=ssq[:, :], in0=ssq[:, :], scalar1=1e-24, scalar2=None, op0=ALU.max)
            rsqrt(p, ssq[:, :], ssq[:, :], 1.0, 0.0)
            p.I("dve", "tensor_tensor", out=W("kk").rr("p (u k) -> p u k", k=64), in0=W("kk").rr("p (u k) -> p u k", k=64),
                in1=ssq[:, :].us(2).bc([128, 16, 64]), op=ALU.mult)
            p.I("pool", "tensor_tensor", out=W("bb"), in0=W("kk"), in1=W("aa"), op=ALU.mult)
            for e in range(2):
                p.I("dve", "tensor_tensor", out=We("t1", e), in0=We("r2", e), in1=We("kt", e), op=ALU.mult)
                p.I("dve", "tensor_tensor", out=We("t1", e), in0=We("t1", e), in1=vt(VT_RK), op=ALU.mult)
            p.I("dve", "tensor_reduce", out=ssq[:, :], in_=W("t1").rr("p (u k) -> p u k", k=64), axis=AX.X, op=ALU.add)
            p.I("dve", "tensor_tensor", out=W("t2").rr("p (u k) -> p u k", k=64), in0=W("v2").rr("p (u k) -> p u k", k=64),
                in1=ssq[:, :].us(2).bc([128, 16, 64]), op=ALU.mult)
            for e in range(2):
                p.dma("sp", bonus_d[e, tsl, :], We("t2", e), final=True)
            for e in range(2):
                psc, pst_ = pss[2 + e], pss[4 + e]
                p.I("pe", "matmul", out=psc[:, :], lhsT=consts[:, 2 if e == 0 else 1, :], rhs=We("lw", e), start=True, stop=True)
                p.I("pe", "matmul", out=pst_[:, :], lhsT=ones, rhs=We("lw", e), start=True, stop=True)
                p.I("act", "activation", out=We("cum", e), in_=psc[:, :], func=AF.Copy)
                p.I("act", "activation", out=We("wt", e), in_=psc[:, :], func=AF.Exp)
                p.I("dve", "tensor_tensor", out=We("Rt", e), in0=We("r2", e), in1=We("wt", e), op=ALU.mult)
                p.I("act", "activation", out=We("wt", e), in_=psc[:, :], func=AF.Exp, scale=-1.0)
                p.I("dve", "tensor_tensor", out=We("Bt", e), in0=We("bb", e), in1=We("wt", e), op=ALU.mult)
                p.I("pool", "tensor_tensor", out=We("Kt", e), in0=We("kt", e), in1=We("wt", e), op=ALU.mult)
                p.I("dve", "tensor_tensor", out=We("t1", e), in0=pst_[:, :], in1=We("cum", e), op=ALU.subtract)
                p.I("act", "activation", out=We("t1", e), in_=We("t1", e), func=AF.Exp)
                p.I("dve", "tensor_tensor", out=We("Bh", e), in0=We("bb", e), in1=We("t1", e), op=ALU.mult)
                p.I("pool", "tensor_tensor", out=We("Kh", e), in0=We("kt", e), in1=We("t1", e), op=ALU.mult)
                p.I("dve", "tensor_tensor", out=We("t2", e), in0=We("cum", e), in1=We("lw", e), op=ALU.subtract)
                p.I("act", "activation", out=We("t2", e), in_=We("t2", e), func=AF.Exp)
                p.I("dve", "scalar_tensor_tensor", out=X[:, e * 8:(e + 1) * 8, 0:64], in0=We("kk", e).rr("p (u k) -> p u k", k=64),
                    scalar=-1.0, in1=We("t2", e).rr("p (u k) -> p u k", k=64), op0=ALU.mult, op1=ALU.mult)
                p.I("act", "activation", out=We("t2", e)[0:64, :], in_=pst_[0:64, :], func=AF.Exp)
                p.I("dve", "tensor_tensor", out=dgW[:, e * 8:(e + 1) * 8, :], in0=We("t2", e)[0:64, :].rr("p (u k) -> p u k", k=64),
                    in1=consts[0:64, 0, 0:64].us(1).bc([64, 8, 64]), op=ALU.mult)
            def unit(u, slot):
                e, h = divmod(u, 8)
                cs = slice(u * 64, (u + 1) * 64)
                t5, npw, tr = T5[slot], NP[slot], TR[slot]
                psa, psb = pss[slot * 2], pss[slot * 2 + 1]
                Vv = W("v2")[:, cs]
                for i, src in enumerate((X.k(u)[:, u, 0:64], W("Bt")[:, cs], W("Kt")[:, cs], W("Rt")[:, cs])):
                    p.I("pe", "transpose", out=psa[0:64, i * 128:(i + 1) * 128], in_=src, identity=ident)
                p.I("act", "activation", out=tr[0:64, :, :], in_=psa[0:64, :].rr("p (i t) -> p i t", t=128), func=AF.Copy)
                yield
                AT, BT, KT, RT = (tr[:, i, :] for i in range(4))
                mm = lambda out, l, r: p.I("pe", "matmul", out=out, lhsT=l, rhs=r, start=True, stop=True)
                mm(psb[:, 0:128], AT, BT)
                mm(psb[:, 128:256], BT, AT)
                mm(psb[:, 256:384], KT, AT)
                p.I("dve", "tensor_tensor", out=t5[:, 0:3, :], in0=psb[:, 0:384].rr("p (i t) -> p i t", t=128),
                    in1=consts[:, 6 + e * 5:9 + e * 5, :], op=ALU.mult)
                mm(psa[:, 0:128], KT, RT)
                mm(psa[:, 128:256], BT, RT)
                p.I("dve", "tensor_tensor", out=t5[:, 3:5, :], in0=psa[:, 0:256].rr("p (i t) -> p i t", t=128),
                    in1=consts[:, 9 + e * 5:11 + e * 5, :], op=ALU.mult)
                yield
                mm(psb[:, 0:64], t5[:, 2, :], Vv)
                p.I("act", "activation", out=X.k(u)[:, u, 64:128], in_=psb[:, 0:64], func=AF.Copy)
                p.I("pool", "tensor_copy", out=npw[:, 0, :, :], in_=t5[:, 0:2, :])
                yield
                for lvl in range(7):
                    cur = npw[:, lvl % 2, :, :]
                    nxt = npw[:, (lvl + 1) % 2, :, :]
                    ps = psa if lvl % 2 == 0 else psb
                    mm(ps[:, 0:128], cur[:, 1, :], X.k(u)[:, u, :])
                    if lvl < 6:
                        mm(ps[:, 128:256], cur[:, 1, :], cur[:, 0, :])
                        mm(ps[:, 256:384], cur[:, 0, :], cur[:, 1, :])
                    p.I("dve", "tensor_tensor", out=X.k(u)[:, u, :], in0=ps[:, 0:128], in1=X.k(u)[:, u, :], op=ALU.add)
                    if lvl < 6:
                        p.I("act", "activation", out=nxt, in_=ps[:, 128:384].rr("p (i t) -> p i t", t=128), func=AF.Copy)
                    yield
                P_, U_ = X.k(u)[:, u, 0:64], X.k(u)[:, u, 64:128]
                mm(psa[0:64, 0:128], P_, t5[:, 4, :])
                p.I("pe", "matmul", out=psa[:, 128:192], lhsT=t5[:, 3, :], rhs=Vv, start=True, stop=False)
                p.I("pe", "matmul", acc=True, out=psa[:, 128:192], lhsT=t5[:, 4, :], rhs=U_, start=False, stop=True)
                mm(psa[0:64, 192:256], P_, W("Bh")[:, cs])
                p.I("pe", "matmul", out=psa[0:64, 256:320], lhsT=W("Bh")[:, cs], rhs=U_, start=True, stop=False)
                p.I("pe", "matmul", acc=True, out=psa[0:64, 256:320], lhsT=W("Kh")[:, cs], rhs=Vv, start=False, stop=True)
                oq, oy, og = OUTQ[slot], OUTY[slot], OUTG[slot]
                p.I("dve", "tensor_tensor", out=oq[:, :], in0=psa[0:64, 0:128], in1=tr[0:64, 3, :], op=ALU.add)
                p.I("act", "activation", out=oy[:, :], in_=psa[:, 128:192], func=AF.Copy)
                p.I("dve", "tensor_tensor", out=og[:, 0:64], in0=psa[0:64, 192:256], in1=dgW[:, u, :], op=ALU.add)
                p.I("act", "activation", out=og[:, 64:128], in_=psa[0:64, 256:320], func=AF.Copy)
                p.dma("sp", QT_d[u, tt, :, :], oq[:, :], final=True)
                p.dma("sp", Yl_d[u, tt, :, :], oy[:, :], final=True)
                p.dma("sp", GH_d[u, tt, :, :], og[:, :], final=True)
                yield

            for u0 in range(0, 16, NU):
                gens = [unit(u0 + i, i) for i in range(NU)]
                while gens:
                    for g in list(gens):
                        try:
                            next(g)
                        except StopIteration:
                            gens.remove(g)
    import os
    if 1:
      with p.scope():
        st = [p.sb(f"rw_st{i}", [128, 16, 128], F32) for i in range(2)]
        gh = [p.sb(f"rw_gh{i}", [128, 8, 128], F32) for i in range(2)]
        for t_ in st + gh:
            p.I("pool", "memset", ap=t_[:, :, :], constant=0.0)
        p.I("dve", "tensor_copy", out=st[0][0:64, :, 64:128], in_=consts[0:64, 0, 0:64].us(1).bc([64, 16, 64]))
        for e in range(2):
            for step in range(8):
                c = step if e == 0 else 7 - step
                g = gh[step % 2]
                src = V(GH_d, GH_d.h[e * 8:(e + 1) * 8, c, :, :].rearrange("u k n -> k u n"), None)
                p.dma("sp", g[0:64, :, :], src)
                a, b = st[step % 2], st[(step + 1) % 2]
                for half in range(2):
                    ps = pss[half]
                    for i in range(4):
                        u = e * 8 + half * 4 + i
                        p.I("pe", "matmul", out=ps[0:64, i * 128:(i + 1) * 128], lhsT=g[:, half * 4 + i, 0:64], rhs=a[:, u, :],
                            start=True, stop=True)
                    us = slice(e * 8 + half * 4, e * 8 + half * 4 + 4)
                    pv = ps[0:64, :].rr("p (i t) -> p i t", t=128)
                    p.I("dve", "tensor_tensor", out=b[0:64, us, 0:64], in0=pv[:, :, 0:64], in1=g[0:64, half * 4:half * 4 + 4, 64:128], op=ALU.add)
                    p.I("act", "activation", out=b[0:64, us, 64:128], in_=pv[:, :, 64:128], func=AF.Copy)
        fin = st[0]
        for half in range(4):
            ps = pss[2 + half % 2]
            for i in range(4):
                u = half * 4 + i
                p.I("pe", "transpose", out=ps[0:64, i * 128:(i + 1) * 128], in_=fin[:, u, 64:128], identity=consts[:, 0, :])
            p.I("act", "activation", out=st[1][0:64, half * 4:half * 4 + 4, 64:128], in_=ps[0:64, :].rr("p (i t) -> p i t", t=128)[:, :, 0:64], func=AF.Copy)
            p.I("dve", "tensor_copy", out=st[1][0:64, half * 4:half * 4 + 4, 0:64], in_=fin[0:64, half * 4:half * 4 + 4, 0:64])
        p.dma("sp", V(segT_d, segT_d.h.rearrange("u k n -> k u n"), None), st[1][0:64, :, :], final=True)


def build_stage_A(p, parts=('AD', 'attn', 'rgemm', 'rw1')):
    xT = p.dram("xT", [2048, NX], F32, kind="ExternalInput")
    w_in = p.dram("w_inA", [2048, 6912], F32, kind="ExternalInput")
    lora_a = p.dram("lora_a", [2048, 384], F32, kind="ExternalInput")
    lora_b = p.dram("lora_b", [96, 4, 512], F32, kind="ExternalInput")
    vecF_d = p.dram("vecF", [128, VF_N], F32, kind="ExternalInput")
    vecT_d = p.dram("vecT", [128, VT_N], F32, kind="ExternalInput")
    rope_d = p.dram("rope", [128, 8, 64], F32, kind="ExternalInput")
    consts_d = p.dram("consts", [128, NCONST, 128], F32, kind="ExternalInput")
    yaT = p.dram("yaT", [512, NT], F32, kind="ExternalOutput")
    ydT = p.dram("ydT", [512, NT], F32, kind="ExternalOutput")
    bzT = p.dram("bzT", [512, NT], F32, kind="ExternalOutput")
    qT = p.dram("qT", [64, 8, NT], F32, kind="ExternalOutput")
    kT = p.dram("kT", [64, 2, NT], F32, kind="ExternalOutput")
    vv = p.dram("vv", [NT, 128], F32, kind="ExternalOutput")
    czs = p.dram("czs", [NT, 512], F32, kind="ExternalOutput")
    bonus = p.dram("bonus", [2, NT, 512], F32, kind="ExternalOutput")
    QT = p.dram("rQT", [16, 8, 64, 128], F32, kind="ExternalOutput")
    Yl = p.dram("rYl", [16, 8, 128, 64], F32, kind="ExternalOutput")
    GH = p.dram("rGH", [16, 8, 64, 128], F32, kind="ExternalOutput")
    segT = p.dram("rsegT", [16, 64, 128], F32, kind="ExternalOutput")
    pc = p.dram("pc", [9, NT, 512], F32, kind="Internal")
    vecF = p.sb("vecF_s", [128, VF_N], F32)
    vecT = p.sb("vecT_s", [128, VT_N], F32)
    rope = p.sb("rope_s", [128, 8, 64], F32)
    consts = p.sb("consts_s", [128, NCONST, 128], F32)
    th = p.sb("th_s", [96, 4, NT], F32)
    pss = [p.ps(f"ps{i}", [128, 512], F32) for i in range(8)]
    p.dma("sp", vecF[:, :], vecF_d[:, :])
    p.dma("sp", vecT[:, :], vecT_d[:, :])
    p.dma("sp", rope[:, :, :], rope_d[:, :, :])
    p.dma("sp", consts[:, :, :], consts_d[:, :, :])
    ones = consts[:, 5, :]
    with p.scope():
        hT = p.sb("hT", [128, 16, NX], BF16)
        wts = [p.sb(f"wt{i}", [128, 16, 512], BF16) for i in range(2)]
        rmsnorm_hT(p, xT, vecF, hT, ones, pss)
        if 'AD' in parts:
            branch_AD(p, hT, w_in, vecF, ones, wts, pss, yaT, ydT, bzT)
        if 'attn' in parts:
            attn_prep(p, hT, w_in, vecT, rope, consts[:, 0, :], wts, pss, qT, kT, vv)
        if 'rgemm' in parts:
            rwkv_gemms(p, hT, w_in, lora_a, wts, pss, pc, czs, th)
    if 'rw1' in parts:
        rwkv_pass1(p, pc, lora_b, th, vecT, consts, pss, bonus, None, QT, Yl, GH, segT)


G0 = 6912


def load_cast(p, dst, src_t, src_ap, q="pool"):
    p.dma(q, dst, V(src_t, src_ap, None))


def attention(p, qT_d, kTf_d, vf_d, bzT_d, consts, pss, yT):
    ones_bf = None
    with p.scope():
        kT = p.sb("at_kT", [64, 2, 4096], BF16)
        qT = p.sb("at_qT", [64, 8, NT], BF16)
        vz = p.sb("at_vz", [128, 32, 2, 192], BF16)
        onesb = p.sb("at_ones", [128, 128], BF16)
        bz = p.sb("at_bz", [128, 4, NT], F32)
        PT = [p.sb(f"at_PT{i}", [128, 512], BF16) for i in range(3)]
        rden = p.sb("at_rden", [128, 512], F32)
        p.I("pool", "memset", ap=vz[:, :, :, :], constant=0.0)
        p.I("pool", "memset", ap=onesb[:, :], constant=1.0)
        for g in range(2):
            p.dma("pool", kT[:, g, :], kTf_d[:, g, :])
        for h0 in range(0, 8, 2):
            p.dma("pool", qT[:, h0:h0 + 2, :], qT_d[:, h0:h0 + 2, :])
        for g in range(2):
            for t4 in range(4):
                src = V(vf_d, vf_d.h[t4 * 1024:(t4 + 1) * 1024, g * 64:(g + 1) * 64].rearrange("(t p) d -> p t d", p=128), None)
                p.dma("pool", vz[:, t4 * 8:(t4 + 1) * 8, g, 64:128], src)
        p.dma("sp", bz[:, :, :], V(bzT_d, bzT_d.h.rearrange("(j p) t -> p j t", p=128), None))
        n = 0
        for h in range(8):
            g = h // 4
            po = (h % 2) * 64
            lo = 64 if h % 2 == 0 else 0
            for qb in range(2):
                qs = slice(qb * 512, (qb + 1) * 512)
                pso, psd = pss[4 + (n % 2)], pss[6 + (n % 2)]
                n += 1
                for kt in range(32):
                    pss_ = pss[kt % 4]
                    pt = PT[kt % 3]
                    p.I("pe", "matmul", out=pss_[:, :], lhsT=kT[:, g, kt * 128:(kt + 1) * 128], rhs=qT[:, h, qs],
                        start=True, stop=True)
                    p.I("act", "activation", out=pt[:, :], in_=pss_[:, :], func=AF.Exp)
                    p.I("pe", "matmul", acc=(kt > 0), out=pso[:, :], lhsT=vz[:, kt, g, lo:lo + 128], rhs=pt[:, :],
                        start=(kt == 0), stop=(kt == 31))
                    p.I("pe", "matmul", acc=(kt > 0), out=psd[:, :], lhsT=onesb[:, :], rhs=pt[:, :],
                        start=(kt == 0), stop=(kt == 31))
                ps_ = slice(po, po + 64)
                p.I("dve", "reciprocal", out=rden[ps_, :], in_=psd[ps_, :])
                p.I("dve", "tensor_tensor", out=rden[ps_, :], in0=rden[ps_, :], in1=bz[ps_, h // 2, qs], op=ALU.mult)
                p.I("dve", "tensor_tensor", out=yT[ps_, 1, h // 2, qs], in0=pso[ps_, :], in1=rden[ps_, :], op=ALU.mult)


def rwkv_pass2(p, QT_d, Yl_d, GH_d, segs_d, bonus_d, czs_d, vecT, consts, pss, yT):
    ident = consts[:, 0, :]
    with p.scope():
        ST = [p.sb(f"r2_ST{i}", [128, 16, 64], F32) for i in range(2)]
        sg = p.sb("r2_sg", [128, 16, 128], F32)
        qt = [p.sb(f"r2_qt{i}", [128, 16, 128], F32) for i in range(2)]
        gh = [p.sb(f"r2_gh{i}", [128, 16, 128], F32) for i in range(2)]
        yl = [p.sb(f"r2_yl{i}", [128, 16, 64], F32) for i in range(2)]
        yn = p.sb("r2_yn", [128, 8, 2, 512], F32)
        for t_ in ST + qt + gh + [sg]:
            p.I("pool", "memset", ap=t_[:, :, :], constant=0.0)
        cur = 0
        for s in range(3):
            p.dma("sp", sg[0:64, :, :], V(segs_d, segs_d.h[s].rearrange("u k n -> k u n"), None))
            a, b = ST[cur], ST[1 - cur]
            for half in range(2):
                ps = pss[half]
                for i in range(8):
                    u = half * 8 + i
                    p.I("pe", "matmul", out=ps[0:64, i * 64:(i + 1) * 64], lhsT=sg[:, u, 64:128], rhs=a[:, u, :],
                        start=True, stop=True)
                us = slice(half * 8, half * 8 + 8)
                p.I("dve", "tensor_tensor", out=b[0:64, us, :], in0=ps[0:64, :].rr("p (i t) -> p i t", t=64),
                    in1=sg[0:64, us, 0:64], op=ALU.add)
            cur = 1 - cur
        for s in range(8):
            q_, g_, y_ = qt[s % 2], gh[s % 2], yl[s % 2]
            for e in range(2):
                c = s if e == 0 else 7 - s
                us = slice(e * 8, e * 8 + 8)
                p.dma("sp", q_[0:64, us, :], V(QT_d, QT_d.h[e * 8:(e + 1) * 8, c].rearrange("u k n -> k u n"), None))
                p.dma("sp", g_[0:64, us, :], V(GH_d, GH_d.h[e * 8:(e + 1) * 8, c].rearrange("u k n -> k u n"), None))
                p.dma("sp", y_[:, us, :], V(Yl_d, Yl_d.h[e * 8:(e + 1) * 8, c].rearrange("u t n -> t u n"), None))
            a, b = ST[cur], ST[1 - cur]
            for e in range(2):
                c = s if e == 0 else 7 - s
                psy, pss_ = pss[2 + e], pss[4 + e]
                for i in range(8):
                    u = e * 8 + i
                    p.I("pe", "matmul", out=psy[:, i * 64:(i + 1) * 64], lhsT=q_[:, u, :], rhs=a[:, u, :], start=True, stop=True)
                    p.I("pe", "matmul", out=pss_[0:64, i * 64:(i + 1) * 64], lhsT=g_[:, u, 0:64], rhs=a[:, u, :], start=True, stop=True)
                us = slice(e * 8, e * 8 + 8)
                p.I("dve", "tensor_tensor", out=yn[:, c, e, :].rr("p (u k) -> p u k", k=64), in0=psy[:, :].rr("p (u k) -> p u k", k=64),
                    in1=y_[:, us, :], op=ALU.add)
                p.I("dve", "tensor_tensor", out=b[0:64, us, :], in0=pss_[0:64, :].rr("p (i t) -> p i t", t=64),
                    in1=g_[0:64, us, 64:128], op=ALU.add)
            cur = 1 - cur
        st1 = p.sb("r2_s1", [128, 16], F32)
        st2 = p.sb("r2_s2", [128, 16], F32)
        sq = p.sb("r2_sq", [128, 2, 512], F32)
        bon = [p.sb(f"r2_bon{i}", [128, 2, 512], F32) for i in range(2)]
        cz = [p.sb(f"r2_cz{i}", [128, 512], F32) for i in range(2)]
        yc = [p.sb(f"r2_yc{i}", [128, 512], F32) for i in range(2)]
        for c in range(8):
            tsl = slice(c * 128, (c + 1) * 128)
            bo, cz_, yc_ = bon[c % 2], cz[c % 2], yc[c % 2]
            p.dma("sp", bo[:, :, :], V(bonus_d, bonus_d.h[:, tsl, :].rearrange("e t c -> t e c"), None))
            p.dma("sp", cz_[:, :], czs_d[tsl, :])
            y2 = yn[:, c, :, :]
            y3 = y2.rr("p e (h k) -> p (e h) k", k=64)
            p.I("dve", "tensor_reduce", out=st1[:, :], in_=y3, axis=AX.X, op=ALU.add)
            p.I("dve", "tensor_scalar", out=st1[:, :], in0=st1[:, :], scalar1=1.0 / 64, scalar2=None, op0=ALU.mult)
            p.I("dve", "tensor_tensor", out=y3, in0=y3, in1=st1[:, :].us(2).bc([128, 16, 64]), op=ALU.subtract)
            p.I("act", "activation", out=sq[:, :, :], in_=y2, func=AF.Square)
            p.I("dve", "tensor_reduce", out=st2[:, :], in_=sq[:, :, :].rr("p e (h k) -> p (e h) k", k=64), axis=AX.X, op=ALU.add)
            rsqrt(p, st2[:, :], st2[:, :], 1.0 / 64, GN_EPS)
            p.I("dve", "tensor_tensor", out=y3, in0=y3, in1=st2[:, :].us(2).bc([128, 16, 64]), op=ALU.mult)
            p.I("pool", "tensor_tensor", out=y2, in0=y2, in1=vecT[:, VT_GNG:VT_GNG + 512].us(1).bc([128, 2, 512]), op=ALU.mult)
            p.I("pool", "tensor_tensor", out=y2, in0=y2, in1=vecT[:, VT_GNB:VT_GNB + 512].us(1).bc([128, 2, 512]), op=ALU.add)
            p.I("dve", "tensor_tensor", out=y2, in0=y2, in1=bo[:, :, :], op=ALU.add)
            p.I("dve", "tensor_tensor", out=yc_[:, :], in0=yn[:, c, 0, :], in1=yn[:, c, 1, :], op=ALU.add)
            p.I("pool", "tensor_tensor", out=yc_[:, :], in0=yc_[:, :], in1=cz_[:, :], op=ALU.mult)
            ps = pss[6 + c % 2]
            for j in range(4):
                p.I("pe", "transpose", out=ps[:, j * 128:(j + 1) * 128], in_=yc_[:, j * 128:(j + 1) * 128], identity=ident)
            p.I("act", "activation", out=yT[:, 2, :, tsl], in_=ps[:, :].rr("p (j t) -> p j t", t=128), func=AF.Copy)


def merge_out(p, xT_d, w_g_d, w_br_d, w_out_d, vecG, hT, wts, pss, yT, out_d):
    with p.scope():
        mT = p.sb("mo_mT", [128, 16, NT], BF16)
        mg = p.sb("mo_mg", [128, 4, NT], F32)
        wbr = [p.sb(f"mo_wbr{i}", [128, 4, 512], BF16) for i in range(2)]
        gt = [p.sb(f"mo_g{i}", [128, 512], F32) for i in range(3)]
        n = 0
        for dg in range(4):
            for br in range(4):
                wt = wts[n % 2]
                wb = wbr[n % 2]
                n += 1
                load_w(p, wt, w_g_d, br * 2048 + dg * 512, 512)
                p.dma("pool", wb[:, :, :], V(w_br_d, w_br_d.h[br, :, dg * 512:(dg + 1) * 512].rearrange("(j p) n -> p j n", p=128), None))
                for dci in range(4):
                    dc = dg * 4 + dci
                    for tb in range(2):
                        tsl = slice(tb * 512, (tb + 1) * 512)
                        psg, psb = pss[(dci * 2 + tb) % 4], pss[4 + (dci * 2 + tb) % 4]
                        gemm_F(p, hT, wt, dci, psg, HALO + tb * 512, 512)
                        for j in range(4):
                            p.I("pe", "matmul", acc=(j > 0), out=psb[:, :], lhsT=wb[:, j, dci * 128:(dci + 1) * 128],
                                rhs=yT[:, br, j, tsl], start=(j == 0), stop=(j == 3))
                        g = gt[(dci * 2 + tb) % 3]
                        p.I("act", "activation", out=g[:, :], in_=psg[:, :], func=AF.Sigmoid,
                            bias=vecG[:, br * 16 + dc:br * 16 + dc + 1], scale=1.0)
                        if br == 0:
                            p.I("dve", "tensor_tensor", out=mg[:, dci, tsl], in0=psb[:, :], in1=g[:, :], op=ALU.mult)
                        else:
                            p.I("dve", "tensor_tensor", out=g[:, :], in0=psb[:, :], in1=g[:, :], op=ALU.mult)
                            if br < 3:
                                p.I("pool", "tensor_tensor", out=mg[:, dci, tsl], in0=mg[:, dci, tsl], in1=g[:, :], op=ALU.add)
                            else:
                                p.I("pool", "tensor_tensor", out=mT[:, dc, tsl], in0=mg[:, dci, tsl], in1=g[:, :], op=ALU.add)
        xr = [p.sb(f"mo_xr{i}", [128, 512], F32) for i in range(2)]
        ot = [p.sb(f"mo_ot{i}", [128, 512], F32) for i in range(2)]
        n = 0
        for og in range(4):
            wt = wts[og % 2]
            load_w(p, wt, w_out_d, og * 512, 512)
            for oci in range(4):
                oc = og * 4 + oci
                for tb in range(2):
                    tsl = slice(tb * 512, (tb + 1) * 512)
                    ps = pss[n % 4]
                    x_, o_ = xr[n % 2], ot[n % 2]
                    n += 1
                    p.dma("sp", x_[:, :], xT_d[oc * 128:(oc + 1) * 128, HALO + tb * 512:HALO + (tb + 1) * 512])
                    for d in range(16):
                        p.I("pe", "matmul", acc=(d > 0), out=ps[:, :], lhsT=wt[:, d, oci * 128:(oci + 1) * 128],
                            rhs=mT[:, d, tsl], start=(d == 0), stop=(d == 15))
                    p.I("dve", "tensor_tensor", out=o_[:, :], in0=ps[:, :], in1=x_[:, :], op=ALU.add)
                    p.dma("sp", out_d[oc * 128:(oc + 1) * 128, tsl], o_[:, :], final=True)


def build_stage_B(p, parts=("attn", "rw2", "merge")):
    xT = p.dram("xT", [2048, NX], F32, kind="ExternalInput")
    w_g = p.dram("w_g", [2048, 8192], F32, kind="ExternalInput")
    w_br = p.dram("w_br", [4, 512, 2048], F32, kind="ExternalInput")
    w_out = p.dram("w_out", [2048, 2048], F32, kind="ExternalInput")
    vecF_d = p.dram("vecF", [128, VF_N], F32, kind="ExternalInput")
    vecT_d = p.dram("vecT", [128, VT_N], F32, kind="ExternalInput")
    vecG_d = p.dram("vecG", [128, 64], F32, kind="ExternalInput")
    consts_d = p.dram("consts", [128, NCONST, 128], F32, kind="ExternalInput")
    yaT = p.dram("yaT", [512, NT], F32, kind="ExternalInput")
    ydT = p.dram("ydT", [512, NT], F32, kind="ExternalInput")
    bzT = p.dram("bzT", [512, NT], F32, kind="ExternalInput")
    qT = p.dram("qT", [64, 8, NT], F32, kind="ExternalInput")
    kTf = p.dram("kTf", [64, 2, 4096], F32, kind="ExternalInput")
    vf = p.dram("vf", [4096, 128], F32, kind="ExternalInput")
    czs = p.dram("czs", [NT, 512], F32, kind="ExternalInput")
    bonus = p.dram("bonus", [2, NT, 512], F32, kind="ExternalInput")
    QT = p.dram("rQT", [16, 8, 64, 128], F32, kind="ExternalInput")
    Yl = p.dram("rYl", [16, 8, 128, 64], F32, kind="ExternalInput")
    GH = p.dram("rGH", [16, 8, 64, 128], F32, kind="ExternalInput")
    segs = p.dram("segs", [3, 16, 64, 128], F32, kind="ExternalInput")
    out = p.dram("xoT", [2048, NT], F32, kind="ExternalOutput")
    vecF = p.sb("vecF_s", [128, VF_N], F32)
    vecT = p.sb("vecT_s", [128, VT_N], F32)
    vecG = p.sb("vecG_s", [128, 64], F32)
    consts = p.sb("consts_s", [128, NCONST, 128], F32)
    yT = p.sb("yT_s", [128, 4, 4, NT], BF16)
    pss = [p.ps(f"ps{i}", [128, 512], F32) for i in range(8)]
    p.dma("sp", vecF[:, :], vecF_d[:, :])
    p.dma("sp", vecT[:, :], vecT_d[:, :])
    p.dma("sp", vecG[:, :], vecG_d[:, :])
    p.dma("sp", consts[:, :, :], consts_d[:, :, :])
    p.dma("pool", yT[:, 0, :, :], V(yaT, yaT.h.rearrange("(j p) t -> p j t", p=128), None))
    p.dma("pool", yT[:, 3, :, :], V(ydT, ydT.h.rearrange("(j p) t -> p j t", p=128), None))
    if "attn" in parts:
        attention(p, qT, kTf, vf, bzT, consts, pss, yT)
    if "rw2" in parts:
        rwkv_pass2(p, QT, Yl, GH, segs, bonus, czs, vecT, consts, pss, yT)
    if "merge" in parts:
        with p.scope():
            hT = p.sb("hT", [128, 16, NX], BF16)
            rmsnorm_hT(p, xT, vecF, hT, consts[:, 5, :], pss)
            wts = [p.sb(f"wt{i}", [128, 16, 512], BF16) for i in range(2)]
            merge_out(p, xT, w_g, w_br, w_out, vecG, hT, wts, pss, yT, out)
    else:
        dbg = p.dram("dbg_yT", [128, 16, NT], F32, kind="ExternalOutput")
        p.dma("pool", dbg[:, :, :], yT[:, :, :, :].rr("p b j t -> p (b j) t"), final=True)


def build_stage_C(p):
    xT = p.dram("xT", [2048, NT], F32, kind="ExternalInput")
    vecF_d = p.dram("vecF", [128, VF_N], F32, kind="ExternalInput")
    out = p.dram("yT", [2048, NT], F32, kind="ExternalOutput")
    vecF = p.sb("vecF_s", [128, VF_N], F32)
    ones = p.sb("ones_s", [128, 128], F32)
    xs = p.sb("xs", [128, 16, NT], F32)
    sq = [p.sb(f"sq{i}", [128, 512], F32) for i in range(2)]
    rstd = p.sb("rstd", [128, 512], F32)
    ot = [p.sb(f"ot{i}", [128, 512], F32) for i in range(2)]
    pss = [p.ps(f"ps{i}", [128, 512], F32) for i in range(2)]
    p.dma("sp", vecF[:, :], vecF_d[:, :])
    p.I("pool", "memset", ap=ones[:, :], constant=1.0)
    for cg in range(4):
        src = V(xT, xT.h[cg * 512:(cg + 1) * 512, :].rearrange("(c p) n -> p c n", p=128), None)
        p.dma("sp", xs[:, cg * 4:(cg + 1) * 4, :], src)
    for tb in range(2):
        tsl = slice(tb * 512, (tb + 1) * 512)
        ps = pss[tb]
        for d in range(16):
            s = sq[d % 2]
            p.I("act", "activation", out=s[:, :], in_=xs[:, d, tsl], func=AF.Square)
            p.I("pe", "matmul", acc=(d > 0), out=ps[:, :], lhsT=ones[:, :], rhs=s[:, :], start=(d == 0), stop=(d == 15))
        rsqrt(p, rstd[:, :], ps[:, :], 1.0 / 2048, RMS_EPS)
        for d in range(16):
            o = ot[d % 2]
            p.I("dve", "scalar_tensor_tensor", out=o[:, :], in0=xs[:, d, tsl], scalar=vecF[:, VF_NORMG + d:VF_NORMG + d + 1],
                in1=rstd[:, :], op0=ALU.mult, op1=ALU.mult)
            p.dma("sp", out[d * 128:(d + 1) * 128, tsl], o[:, :], final=True)


def core_tok(c):
    return c // 4, (c % 4) * 1024


def make_xT(x, c, halo=16):
    b, t0 = core_tok(c)
    S = x.shape[1]
    out = np.zeros((2048, 1024 + 2 * halo), np.float32)
    lo, hi = t0 - halo, t0 + 1024 + halo
    slo, shi = max(lo, 0), min(hi, S)
    out[:, slo - lo: shi - lo] = x[b, slo:shi, :].T
    return out


def make_vecF(inp, l, final=False):
    v = np.zeros((128, 164), np.float32)
    if final:
        v[:, 0:16] = np.asarray(inp['final_norm_g'], np.float32).reshape(16, 128).T
        return v
    v[:, 0:16] = inp['norm_g'][l].reshape(16, 128).T
    v[:, 16:28] = inp['conv_a_w'][l].reshape(3, 4, 128).transpose(2, 1, 0).reshape(128, 12)
    v[:, 28:152] = inp['dw_w'][l].reshape(31, 4, 128).transpose(2, 1, 0).reshape(128, 124)
    v[:, 152:156] = inp['dw_b'][l].reshape(4, 128).T
    v[:, 156:160] = inp['ln_g'][l].reshape(4, 128).T
    v[:, 160:164] = inp['ln_b'][l].reshape(4, 128).T
    return v


def make_consts16():
    c = np.zeros((128, 16, 128), np.float32)
    i = np.arange(128)
    I = np.eye(128, dtype=np.float32)
    LI = (i[None, :] <= i[:, None]).astype(np.float32)
    UI = (i[None, :] >= i[:, None]).astype(np.float32)
    LS = (i[None, :] < i[:, None]).astype(np.float32)
    US = (i[None, :] > i[:, None]).astype(np.float32)
    for k, m in enumerate([I, LI, UI, LS, US, np.ones((128, 128), np.float32), LS, US, US, UI, UI, US, LS, LS, LI, LI]):
        c[:, k] = m
    return c


def make_vecT(inp, l):
    parts = [inp['mu_rkv'][l].reshape(-1), inp['w0'][l].reshape(-1), inp['a0'][l].reshape(-1), inp['k_k'][l], inp['k_a'][l],
             inp['r_k'][l].reshape(-1), inp['gn_g'][l], inp['gn_b'][l], inp['q_norm_g'][l], inp['k_norm_g'][l]]
    v = np.concatenate([np.asarray(x, np.float32).reshape(-1) for x in parts])
    return np.ascontiguousarray(np.broadcast_to(v[None, :], (128, v.size)))


def make_lora(inp, l):
    la = np.concatenate([inp['w_lora_a'][l][0], inp['w_lora_a'][l][1], inp['a_lora_a'][l][0], inp['a_lora_a'][l][1]], axis=1)
    lb = np.stack([inp['w_lora_b'][l][0], inp['w_lora_b'][l][1], inp['a_lora_b'][l][0], inp['a_lora_b'][l][1]], axis=1)
    return np.ascontiguousarray(la), np.ascontiguousarray(lb)


def make_rope(c):
    b, t0 = core_tok(c)
    t = t0 + np.arange(1024)
    row = (t // 64).astype(np.float32)
    col = (t % 64).astype(np.float32)
    inv = (10000.0 ** (-np.arange(16, dtype=np.float32) / 16)).astype(np.float32)
    ar = row[:, None] * inv
    ac = col[:, None] * inv
    tab = np.concatenate([np.cos(ar), np.cos(ac), np.sin(ar), np.sin(ac)], axis=1).astype(np.float32)
    return np.ascontiguousarray(tab.reshape(8, 128, 64).transpose(1, 0, 2))


def make_vecG(inp, l):
    bg = inp['b_gate'][l]
    return np.ascontiguousarray(bg.reshape(4, 16, 128).transpose(2, 0, 1).reshape(128, 64))


def stageA_inputs(inp, l, x, c, shared):
    d = dict(shared)
    d["xT"] = make_xT(x, c)
    d["rope"] = make_rope(c)
    return d


def stageB_inputs(inp, l, x, c, A, shared, xTs):
    m = c % 4
    base = (c // 4) * 4
    kTf = np.concatenate([A[base + i]['kT'] for i in range(4)], axis=2)
    vf = np.concatenate([A[base + i]['vv'] for i in range(4)], axis=0)
    segs = np.zeros((3, 16, 64, 128), np.float32)
    for s in range(3):
        src0 = m - 3 + s
        if src0 >= 0:
            segs[s, 0:8] = A[base + src0]['rsegT'][0:8]
        src1 = m + 3 - s
        if src1 <= 3:
            segs[s, 8:16] = A[base + src1]['rsegT'][8:16]
    r = A[c]
    d = dict(shared)
    d.update({"xT": xTs[c], "kTf": kTf, "vf": vf, "segs": segs})
    for k in ("yaT", "ydT", "bzT", "qT", "czs", "bonus", "rQT", "rYl", "rGH"):
        d[k] = r[k]
    return d


_PROGS = {}


def _prog(name, builder):
    if name not in _PROGS:
        p = Prog()
        builder(p)
        _PROGS[name] = p.finalize()
    return _PROGS[name]


def kernel(**inputs):
    inp = {k: np.asarray(v) for k, v in inputs.items()}
    x = np.ascontiguousarray(inp['x'], dtype=np.float32)
    ncA = _prog("A", build_stage_A)
    ncB = _prog("B", build_stage_B)
    ncC = _prog("C", build_stage_C)
    cores = list(range(8))
    consts = make_consts16()
    for l in range(4):
        la, lb = make_lora(inp, l)
        vecF, vecT = make_vecF(inp, l), make_vecT(inp, l)
        sharedA = {"w_inA": np.ascontiguousarray(inp['w_in'][l][:, :6912]), "lora_a": la, "lora_b": lb,
                   "vecF": vecF, "vecT": vecT, "consts": consts}
        xTs = [make_xT(x, c) for c in cores]
        in_maps = []
        for c in cores:
            d = dict(sharedA)
            d["xT"] = xTs[c]
            d["rope"] = make_rope(c)
            in_maps.append(d)
        A = run_bass_kernel_spmd(ncA, in_maps, core_ids=cores).results
        sharedB = {"w_g": np.ascontiguousarray(inp['w_in'][l][:, 6912:]), "w_br": np.ascontiguousarray(inp['w_branch'][l]),
                   "w_out": np.ascontiguousarray(inp['w_out'][l]), "vecF": vecF, "vecT": vecT, "vecG": make_vecG(inp, l),
                   "consts": consts}
        in_maps = [stageB_inputs(inp, l, x, c, A, sharedB, xTs) for c in cores]
        B = run_bass_kernel_spmd(ncB, in_maps, core_ids=cores).results
        xn = np.empty_like(x)
        for c in cores:
            b, t0 = core_tok(c)
            xn[b, t0:t0 + 1024, :] = B[c]['xoT'].T
        x = xn
    vecF = make_vecF(inp, 0, final=True)
    in_maps = []
    for c in cores:
        b, t0 = core_tok(c)
        in_maps.append({"xT": np.ascontiguousarray(x[b, t0:t0 + 1024, :].T), "vecF": vecF})
    C = run_bass_kernel_spmd(ncC, in_maps, core_ids=cores).results
    out = np.empty_like(x)
    for c in cores:
        b, t0 = core_tok(c)
        out[b, t0:t0 + 1024, :] = C[c]['yT'].T
    return out
```

```python
import numpy as np
import concourse.bass as bass
import concourse.mybir as mybir
from concourse.bass_utils import run_bass_kernel_spmd

F32 = mybir.dt.float32
BF16 = mybir.dt.bfloat16
I32 = mybir.dt.int32
ALU = mybir.AluOpType
AF = mybir.ActivationFunctionType
AX = mybir.AxisListType

SEM_CAP = 4000
N_DMA_SEMS = 24


class V:
    __slots__ = ("t", "ap", "key")

    def __init__(self, t, ap, key):
        self.t, self.ap, self.key = t, ap, key

    def rr(self, pat, **kw):
        return V(self.t, self.ap.rearrange(pat, **kw), self.key)

    def bc(self, shape):
        return V(self.t, self.ap.to_broadcast(list(shape)), self.key)

    def __getitem__(self, idx):
        return V(self.t, self.ap[idx], self.key)

    def us(self, axis):
        return V(self.t, self.ap.unsqueeze(axis), self.key)

    def bitcast(self, dt):
        return V(self.t, self.ap.bitcast(dt), self.key)


class T:
    def __init__(self, h, name):
        self.h, self.name = h, name
        self.recs = {}
        self.is_psum = False

    def __getitem__(self, idx):
        return V(self, self.h[idx], None)

    def k(self, key):
        return _TK(self, key)

    def view(self, ap, name=None):
        t = T(ap, name or self.name)
        t.recs = self.recs
        t.is_psum = self.is_psum
        return t


class _TK:
    def __init__(self, t, key):
        self.t, self.key = t, key

    def __getitem__(self, idx):
        return V(self.t, self.t.h[idx], self.key)


class _Scope:
    def __init__(self, p):
        self.p = p

    def __enter__(self):
        self.n = len(self.p._ctx)
        return self

    def __exit__(self, *a):
        self.p.barrier()
        while len(self.p._ctx) > self.n:
            self.p._ctx.pop().__exit__(None, None, None)


class _Rec:
    __slots__ = ("w", "r")

    def __init__(self):
        self.w, self.r = None, []


OUT_NAMES = ("out", "accum_out", "ap")


class Prog:
    ENG = ("pe", "act", "dve", "pool", "sp")

    def __init__(self):
        self.nc = bass.Bass("TRN2", target_bir_lowering=False)
        nc = self.nc
        self.eng = {"pe": nc.tensor, "act": nc.scalar, "dve": nc.vector, "pool": nc.gpsimd, "sp": nc.sync}
        self.ops = []
        self._ctx = []
        self.n_names = 0
        self.out_dma_ops = []

    def _enter(self, cm):
        h = cm.__enter__()
        self._ctx.append(cm)
        return h

    def _uniq(self, name):
        self.n_names += 1
        return f"{name}_{self.n_names}"

    def sb(self, name, shape, dt=F32):
        name = self._uniq(name)
        return T(self._enter(self.nc.sbuf_tensor(name, list(shape), dt)), name)

    def ps(self, name, shape, dt=F32):
        name = self._uniq(name)
        t = T(self._enter(self.nc.psum_tensor(name, list(shape), dt)), name)
        t.is_psum = True
        return t

    def dram(self, name, shape, dt=F32, kind="Internal"):
        return T(self.nc.dram_tensor(name, list(shape), dt, kind=kind).ap(), name)

    def sem(self, name):
        return self._enter(self.nc.semaphore(name))

    def I(self, eng, method, acc=False, **kw):
        reads, writes = [], []
        for k, v in kw.items():
            if isinstance(v, V):
                (writes if k in OUT_NAMES else reads).append(v)
            elif k == "ins" and isinstance(v, list):
                reads.extend(v)
            elif k == "outs" and isinstance(v, list):
                writes.extend(v)
        if acc:
            for v in list(writes):
                reads.append(v)
        is_dma = method in ("dma_start", "collective_compute")
        op = dict(eng=eng, method=method, kw=kw, reads=reads, writes=writes, dma=is_dma,
                  idx=len(self.ops), mark=False, deps=[])
        self.ops.append(op)
        return op

    def barrier(self):
        last = {}
        dmas = []
        for op in self.ops:
            if op["dma"]:
                dmas.append(op["idx"])
            elif op["method"] is not None:
                last[op["eng"]] = op["idx"]
        deps = [self.ops[i] for i in (list(last.values()) + dmas[-6 * N_DMA_SEMS:])]
        for e in self.ENG:
            op = dict(eng=e, method=None, kw={}, reads=[], writes=[], dma=False,
                      idx=len(self.ops), mark=False, deps=[], xdeps=[d for d in deps])
            self.ops.append(op)

    def mark(self):
        return len(self.ops)

    def hoist(self, start, to):
        moved = self.ops[start:]
        del self.ops[start:]
        self.ops[to:to] = moved

    def scope(self):
        return _Scope(self)

    def dma(self, q, out, in_, final=False, **kw):
        op = self.I(q, "dma_start", out=out, in_=in_, **kw)
        if final:
            self.out_dma_ops.append(op)
        return op

    @staticmethod
    def _recs(v, create):
        t = v.t
        if v.key is None:
            ks = list(t.recs.keys())
            if None not in t.recs and create:
                t.recs[None] = _Rec()
                ks.append(None)
            return [t.recs[k] for k in ks]
        out = []
        if v.key not in t.recs and create:
            t.recs[v.key] = _Rec()
        if v.key in t.recs:
            out.append(t.recs[v.key])
        if None in t.recs:
            out.append(t.recs[None])
        return out

    def finalize(self):
        ops = self.ops
        for i, op in enumerate(ops):
            op["idx"] = i
        fin = dict(eng="pool", method=None, kw={}, reads=[], writes=[], dma=False, idx=len(ops),
                   mark=False, deps=[o["idx"] for o in self.out_dma_ops])
        for op in ops:
            raw, other = set(), set()
            for v in op["reads"]:
                for rec in self._recs(v, False):
                    if rec.w is not None:
                        raw.add(rec.w)
                    if v.t.is_psum:
                        other.update(rec.r)
            for v in op["writes"]:
                for rec in self._recs(v, False):
                    if rec.w is not None:
                        other.add(rec.w)
                    other.update(rec.r)
            deps = []
            for d in raw | other:
                if d == op["idx"]:
                    continue
                dop = ops[d]
                if dop["eng"] == op["eng"] and not dop["dma"] and not op["dma"]:
                    if op["eng"] == "pe":
                        continue
                    if d not in raw:
                        continue
                deps.append(d)
            op["deps"] = deps + [d["idx"] for d in op.get("xdeps", []) if not (d["eng"] == op["eng"] and not d["dma"]) and d["idx"] < op["idx"]]
            for v in op["reads"]:
                if v.key is None:
                    if None not in v.t.recs:
                        v.t.recs[None] = _Rec()
                    for rec in v.t.recs.values():
                        rec.r.append(op["idx"])
                else:
                    if v.key not in v.t.recs:
                        v.t.recs[v.key] = _Rec()
                        if None in v.t.recs:
                            v.t.recs[v.key].w = v.t.recs[None].w
                            v.t.recs[v.key].r = list(v.t.recs[None].r)
                    v.t.recs[v.key].r.append(op["idx"])
            for v in op["writes"]:
                if v.key is None:
                    if None not in v.t.recs:
                        v.t.recs[None] = _Rec()
                    for rec in v.t.recs.values():
                        rec.w = op["idx"]
                        rec.r = []
                else:
                    if v.key not in v.t.recs:
                        v.t.recs[v.key] = _Rec()
                    rec = v.t.recs[v.key]
                    rec.w = op["idx"]
                    rec.r = []
        allops = ops + [fin]
        for op in allops:
            for d in op["deps"]:
                ops[d]["mark"] = True
        for op in ops:
            if op["dma"]:
                op["mark"] = True
        nc = self.nc
        eng_sems = {e: [] for e in self.ENG}
        eng_cnt = {e: 0 for e in self.ENG}
        dma_sems = [self.sem(f"dq{i}") for i in range(N_DMA_SEMS)]
        dma_val = [0] * N_DMA_SEMS
        dma_rr = 0
        cc_sem, cc_val = [None], [0]
        for op in ops:
            if not op["mark"]:
                continue
            if op["method"] == "collective_compute":
                if cc_sem[0] is None:
                    cc_sem[0] = self.sem("ccsem")
                op["prev_ev"] = (("c", 0), cc_val[0])
                cc_val[0] += 1
                op["ev"] = (("c", 0), cc_val[0])
                op["inc"] = (cc_sem[0], 1)
            elif op["dma"]:
                s = dma_rr
                dma_rr = (dma_rr + 1) % N_DMA_SEMS
                op["prev_ev"] = (("d", s), dma_val[s])
                dma_val[s] += 16
                op["ev"] = (("d", s), dma_val[s])
                op["inc"] = (dma_sems[s], 16)
            else:
                e = op["eng"]
                if not eng_sems[e] or eng_cnt[e] >= SEM_CAP:
                    eng_sems[e].append(self.sem(f"s_{e}{len(eng_sems[e])}"))
                    eng_cnt[e] = 0
                eng_cnt[e] += 1
                si = len(eng_sems[e]) - 1
                op["ev"] = ((e, si), eng_cnt[e])
                op["inc"] = (eng_sems[e][si], 1)

        def semobj(sid):
            if sid[0] == "c":
                return cc_sem[0]
            return dma_sems[sid[1]] if sid[0] == "d" else eng_sems[sid[0]][sid[1]]

        waited = {e: {} for e in self.ENG}
        n_wait = 0
        for op in allops:
            e = op["eng"]
            E = self.eng[e]
            evs = [ops[d]["ev"] for d in op["deps"]]
            if op["dma"]:
                evs.append(op["prev_ev"])
            need = {}
            for sid, val in evs:
                if val <= 0:
                    continue
                if waited[e].get(sid, 0) >= val:
                    continue
                need[sid] = max(need.get(sid, 0), val)
            for sid, val in need.items():
                E.wait_ge(semobj(sid), val)
                waited[e][sid] = val
                n_wait += 1
            if op["method"] is None:
                continue
            kw = {k: (v.ap if isinstance(v, V) else ([x.ap for x in v] if k in ("ins", "outs") else v)) for k, v in op["kw"].items()}
            ins = getattr(E, op["method"])(**kw)
            if op["mark"]:
                ins.then_inc(*op["inc"])
        self.stats = dict(n_ops=len(ops), n_wait=n_wait,
                          n_sems=N_DMA_SEMS + sum(len(v) for v in eng_sems.values()))
        return nc

    def close(self):
        for cm in reversed(self._ctx):
            cm.__exit__(None, None, None)
        self._ctx = []


NT = 1024
HALO = 16
NX = NT + 2 * HALO
BLKS = [(0, 512), (512, 512), (1024, 32)]
RMS_EPS = 1e-6
LN_EPS = 1e-5
GN_EPS = 64e-5

VF_NORMG = 0
VF_CONVA = 16
VF_DWW = 28
VF_DWB = 152
VF_LNG = 156
VF_LNB = 160
VF_N = 164


def load_w(p, wt, wsrc, c0, ncols, q="pool"):
    for cg in range(4):
        src = V(wsrc, wsrc.h[cg * 512:(cg + 1) * 512, c0:c0 + ncols].rearrange("(c p) n -> p c n", p=128), None)
        p.dma(q, wt[:, cg * 4:(cg + 1) * 4, 0:ncols], src)


def rsqrt(p, out, in_, scale, eps):
    p.I("act", "activation", out=out, in_=in_, func=AF.Sqrt, bias=eps, scale=scale)
    p.I("dve", "reciprocal", out=out, in_=out)


def rmsnorm_hT(p, xT, vecF, hT, ones, pss, ncols=NX, blks=BLKS):
    with p.scope():
        xs = p.sb("xs", [128, 16, ncols], F32)
        sq = [p.sb(f"sq{i}", [128, 512], F32) for i in range(2)]
        rstd = p.sb("rstd", [128, 512], F32)
        for cg in range(4):
            src = V(xT, xT.h[cg * 512:(cg + 1) * 512, :].rearrange("(c p) n -> p c n", p=128), None)
            p.dma("sp", xs[:, cg * 4:(cg + 1) * 4, :], src)
        for bi, (b0, bn) in enumerate(blks):
            ps = pss[bi % 2]
            for d in range(16):
                s = sq[d % 2]
                p.I("act", "activation", out=s[:, 0:bn], in_=xs[:, d, b0:b0 + bn], func=AF.Square)
                p.I("pe", "matmul", acc=(d > 0), out=ps[:, 0:bn], lhsT=ones[:, :], rhs=s[:, 0:bn],
                    start=(d == 0), stop=(d == 15))
            rsqrt(p, rstd[:, 0:bn], ps[:, 0:bn], 1.0 / 2048, RMS_EPS)
            for d in range(16):
                e = "dve"
                p.I(e, "scalar_tensor_tensor", out=hT[:, d, b0:b0 + bn], in0=xs[:, d, b0:b0 + bn],
                    scalar=vecF[:, VF_NORMG + d:VF_NORMG + d + 1], in1=rstd[:, 0:bn], op0=ALU.mult, op1=ALU.mult)


def gemm_F(p, hT, wt, j, ps, b0, bn):
    for d in range(16):
        p.I("pe", "matmul", acc=(d > 0), out=ps[:, 0:bn], lhsT=wt[:, d, j * 128:(j + 1) * 128],
            rhs=hT[:, d, b0:b0 + bn], start=(d == 0), stop=(d == 15))


def branch_AD(p, hT, w_in, vecF, ones, wts, pss, yaT_d, ydT_d, bzT_d):
    with p.scope():
        cgx = p.sb("cgx", [128, 4, NX], F32)
        ga = p.sb("ga", [128, 4, NX], F32)
        tmp = [p.sb(f"tmpF{i}", [128, 512], F32) for i in range(2)]
        wi = 0
        pi = 0
        marks = []
        for g in range(4):
            wt = wts[wi % 2]; wi += 1
            m0 = p.mark()
            load_w(p, wt, w_in, g * 512, 512)
            if g >= 1:
                p.hoist(m0, marks[-1])
            marks.append(p.mark())
            for j in range(4):
                for (b0, bn) in BLKS:
                    ps = pss[pi % 2]; pi += 1
                    gemm_F(p, hT, wt, j, ps, b0, bn)
                    if g == 0:
                        p.I("act", "activation", out=ga[:, j, b0:b0 + bn], in_=ps[:, 0:bn], func=AF.Copy)
                    elif g == 1:
                        p.I("act", "activation", out=cgx[:, j, b0:b0 + bn], in_=ps[:, 0:bn], func=AF.Copy)
                    elif g == 2:
                        p.I("dve", "tensor_tensor", out=cgx[:, j, b0:b0 + bn], in0=ps[:, 0:bn],
                            in1=cgx[:, j, b0:b0 + bn], op=ALU.mult)
                    else:
                        t = tmp[pi % 2]
                        p.I("act", "activation", out=t[:, 0:bn], in_=ps[:, 0:bn], func=AF.Silu)
                        p.I("dve", "tensor_tensor", out=ga[:, j, b0:b0 + bn], in0=t[:, 0:bn],
                            in1=ga[:, j, b0:b0 + bn], op=ALU.mult)
        ya = p.sb("ya", [128, 4, NT], F32)
        for j in range(4):
            e = "dve"
            w = lambda k: vecF[:, VF_CONVA + j * 3 + k:VF_CONVA + j * 3 + k + 1]
            p.I(e, "tensor_scalar", out=ya[:, j, :], in0=cgx[:, j, HALO - 1:HALO - 1 + NT], scalar1=w(0), scalar2=None,
                op0=ALU.mult)
            for k in (1, 2):
                p.I(e, "scalar_tensor_tensor", out=ya[:, j, :], in0=cgx[:, j, HALO - 1 + k:HALO - 1 + k + NT],
                    scalar=w(k), in1=ya[:, j, :], op0=ALU.mult, op1=ALU.add)
            p.I(e, "tensor_tensor", out=ya[:, j, :], in0=ya[:, j, :], in1=ga[:, j, HALO:HALO + NT], op=ALU.mult)
            p.dma("sp", yaT_d[j * 128:(j + 1) * 128, :], ya[:, j, :], final=True)
    with p.scope():
        u = p.sb("u", [128, 4, NX], F32)
        sz = p.sb("sz", [128, 4, NT], F32)
        acc = p.sb("acc", [128, 4, NT], F32)
        tmp = [p.sb(f"tmpD{i}", [128, 512], F32) for i in range(2)]
        wi = 0
        pi = 0
        D0 = 2048 + 1280 + 2048
        marks = []
        for g in range(2):
            wt = wts[wi % 2]; wi += 1
            m0 = p.mark()
            load_w(p, wt, w_in, D0 + g * 512, 512)
            if g >= 1:
                p.hoist(m0, marks[-1])
            marks.append(p.mark())
            for j in range(4):
                for (b0, bn) in BLKS:
                    ps = pss[pi % 2]; pi += 1
                    gemm_F(p, hT, wt, j, ps, b0, bn)
                    if g == 0:
                        p.I("act", "activation", out=u[:, j, b0:b0 + bn], in_=ps[:, 0:bn], func=AF.Copy)
                    else:
                        t = tmp[pi % 2]
                        p.I("act", "activation", out=t[:, 0:bn], in_=ps[:, 0:bn], func=AF.Sigmoid)
                        p.I("dve", "tensor_tensor", out=u[:, j, b0:b0 + bn], in0=t[:, 0:bn],
                            in1=u[:, j, b0:b0 + bn], op=ALU.mult)
        wt = wts[wi % 2]; wi += 1
        m0 = p.mark()
        load_w(p, wt, w_in, D0 + 1024, 512)
        p.hoist(m0, marks[-1])
        for j in range(4):
            for tb in range(2):
                ps = pss[pi % 2]; pi += 1
                gemm_F(p, hT, wt, j, ps, HALO + tb * 512, 512)
                p.I("act", "activation", out=sz[:, j, tb * 512:(tb + 1) * 512], in_=ps[:, :], func=AF.Silu)
        for j in range(4):
            e = "dve"
            w = lambda k: vecF[:, VF_DWW + j * 31 + k:VF_DWW + j * 31 + k + 1]
            p.I(e, "tensor_scalar", out=acc[:, j, :], in0=u[:, j, HALO - 15:HALO - 15 + NT], scalar1=w(0),
                scalar2=vecF[:, VF_DWB + j:VF_DWB + j + 1], op0=ALU.mult, op1=ALU.add)
            for k in range(1, 31):
                p.I(e, "scalar_tensor_tensor", out=acc[:, j, :], in0=u[:, j, HALO - 15 + k:HALO - 15 + k + NT],
                    scalar=w(k), in1=acc[:, j, :], op0=ALU.mult, op1=ALU.add)
        mean = p.sb("mean", [128, 512], F32)
        rs = p.sb("rs", [128, 512], F32)
        for tb in range(2):
            sl = slice(tb * 512, (tb + 1) * 512)
            ps1, ps2 = pss[0], pss[1]
            for j in range(4):
                p.I("pe", "matmul", acc=(j > 0), out=ps1[:, :], lhsT=ones[:, :], rhs=acc[:, j, sl], start=(j == 0), stop=(j == 3))
            for j in range(4):
                t = tmp[j % 2]
                p.I("act", "activation", out=t[:, :], in_=acc[:, j, sl], func=AF.Square)
                p.I("pe", "matmul", acc=(j > 0), out=ps2[:, :], lhsT=ones[:, :], rhs=t[:, :], start=(j == 0), stop=(j == 3))
            p.I("dve", "tensor_scalar", out=mean[:, :], in0=ps1[:, :], scalar1=1.0 / 512, scalar2=None, op0=ALU.mult)
            p.I("dve", "tensor_tensor", out=rs[:, :], in0=mean[:, :], in1=mean[:, :], op=ALU.mult)
            p.I("dve", "scalar_tensor_tensor", out=rs[:, :], in0=ps2[:, :], scalar=1.0 / 512, in1=rs[:, :],
                op0=ALU.mult, op1=ALU.subtract)
            rsqrt(p, rs[:, :], rs[:, :], 1.0, LN_EPS)
            for j in range(4):
                e = "dve" if j % 2 == 0 else "pool"
                p.I(e, "tensor_tensor", out=acc[:, j, sl], in0=acc[:, j, sl], in1=mean[:, :], op=ALU.subtract)
                p.I(e, "tensor_tensor", out=acc[:, j, sl], in0=acc[:, j, sl], in1=rs[:, :], op=ALU.mult)
                p.I("act", "activation", out=acc[:, j, sl], in_=acc[:, j, sl], func=AF.Silu,
                    bias=vecF[:, VF_LNB + j:VF_LNB + j + 1], scale=vecF[:, VF_LNG + j:VF_LNG + j + 1])
                p.I(e, "tensor_tensor", out=acc[:, j, sl], in0=acc[:, j, sl], in1=sz[:, j, sl], op=ALU.mult)
        for j in range(4):
            p.dma("sp", ydT_d[j * 128:(j + 1) * 128, :], acc[:, j, :], final=True)
        wt = wts[wi % 2]; wi += 1
        load_w(p, wt, w_in, 2048 + 768, 512)
        for j in range(4):
            for tb in range(2):
                ps = pss[pi % 2]; pi += 1
                gemm_F(p, hT, wt, j, ps, HALO + tb * 512, 512)
                p.I("act", "activation", out=sz[:, j, tb * 512:(tb + 1) * 512], in_=ps[:, :], func=AF.Silu)
            p.dma("sp", bzT_d[j * 128:(j + 1) * 128, :], sz[:, j, :], final=True)


def gemm_T(p, hT, wt, ncols, ps, tok0):
    for d in range(16):
        p.I("pe", "matmul", acc=(d > 0), out=ps[:, 0:ncols], lhsT=hT[:, d, tok0:tok0 + 128],
            rhs=wt[:, d, 0:ncols], start=(d == 0), stop=(d == 15))


VT_MU = 0
VT_W0 = 3072
VT_A0 = 4096
VT_KK = 5120
VT_KA = 5632
VT_RK = 6144
VT_GNG = 6656
VT_GNB = 7168
VT_QG = 7680
VT_KG = 7744
VT_N = 7808


def qk_norm_rope(p, src, nh, g_off, vecT, rope_tt, ident, pst, outT, scale, wk):
    n = nh * 64
    sq, ss, t1, t2, qn, qr = wk["sq"], wk["ss"], wk["t1"], wk["t2"], wk["qn"], wk["qr"]
    p.I("act", "activation", out=sq[:, 0:n], in_=src, func=AF.Square)
    p.I("dve", "tensor_reduce", out=ss[:, 0:nh], in_=sq[:, 0:n].rr("p (h k) -> p h k", k=64), axis=AX.X, op=ALU.add)
    rsqrt(p, ss[:, 0:nh], ss[:, 0:nh], 1.0 / 64, RMS_EPS)
    if scale != 1.0:
        p.I("dve", "tensor_scalar", out=ss[:, 0:nh], in0=ss[:, 0:nh], scalar1=scale, scalar2=None, op0=ALU.mult)
    v3 = lambda t: t[:, 0:n].rr("p (h k) -> p h k", k=64)
    p.I("dve", "tensor_tensor", out=v3(qn), in0=src.rr("p (h k) -> p h k", k=64),
        in1=ss[:, 0:nh].us(2).bc([128, nh, 64]), op=ALU.mult)
    p.I("dve", "tensor_tensor", out=v3(qn), in0=v3(qn),
        in1=vecT[:, g_off:g_off + 64].us(1).bc([128, nh, 64]), op=ALU.mult)
    v5 = lambda t: t[:, 0:n].rr("p (h a b k) -> p h a b k", a=2, b=2, k=16)
    x0 = v5(qn)[:, :, :, 0, :]
    x1 = v5(qn)[:, :, :, 1, :]
    o0 = v5(qr)[:, :, :, 0, :]
    o1 = v5(qr)[:, :, :, 1, :]
    cs = rope_tt[:, 0:32].rr("p (a k) -> p a k", k=16).us(1).bc([128, nh, 2, 16])
    sn = rope_tt[:, 32:64].rr("p (a k) -> p a k", k=16).us(1).bc([128, nh, 2, 16])
    h4 = lambda t: t[:, 0:n // 2].rr("p (h a k) -> p h a k", a=2, k=16)
    p.I("dve", "tensor_tensor", out=h4(t1), in0=x0, in1=cs, op=ALU.mult)
    p.I("pool", "tensor_tensor", out=h4(t2), in0=x1, in1=sn, op=ALU.mult)
    p.I("dve", "tensor_tensor", out=o0, in0=h4(t1), in1=h4(t2), op=ALU.subtract)
    p.I("pool", "tensor_tensor", out=h4(t2), in0=x1, in1=cs, op=ALU.mult)
    p.I("dve", "tensor_tensor", out=h4(t1), in0=x0, in1=sn, op=ALU.mult)
    p.I("dve", "tensor_tensor", out=o1, in0=h4(t1), in1=h4(t2), op=ALU.add)
    for h0 in range(0, nh, 4):
        hn = min(4, nh - h0)
        for h in range(h0, h0 + hn):
            p.I("pe", "transpose", out=pst[0:64, (h - h0) * 128:(h - h0 + 1) * 128], in_=qr[:, h * 64:(h + 1) * 64],
                identity=ident)
        p.I("act", "activation", out=outT[:, h0:h0 + hn, :], in_=pst[0:64, 0:hn * 128].rr("p (h t) -> p h t", t=128),
            func=AF.Copy)


def attn_prep(p, hT, w_in, vecT, rope, ident, wts, pss, qT_d, kT_d, v_d):
    with p.scope():
        wk = dict(sq=p.sb("aq_sq", [128, 512], F32), ss=p.sb("aq_ss", [128, 8], F32),
                  t1=p.sb("aq_t1", [128, 256], F32), t2=p.sb("aq_t2", [128, 256], F32),
                  qn=p.sb("aq_qn", [128, 512], F32), qr=p.sb("aq_qr", [128, 512], F32))
        qs = p.sb("aq_qs", [128, 512], F32)
        qT = [p.sb(f"aq_qT{i}", [64, 8, 128], F32) for i in range(2)]
        kT = [p.sb(f"aq_kT{i}", [64, 2, 128], F32) for i in range(2)]
        wq, wkv = wts
        load_w(p, wq, w_in, 2048, 512)
        load_w(p, wkv, w_in, 2560, 256)
        for tt in range(8):
            tok0 = HALO + tt * 128
            ps = pss[tt % 2]
            gemm_T(p, hT, wq, 512, ps, tok0)
            p.I("act", "activation", out=qs[:, :], in_=ps[:, :], func=AF.Copy)
            qk_norm_rope(p, qs[:, :], 8, VT_QG, vecT, rope[:, tt, :], ident, pss[2 + tt % 2], qT[tt % 2], 0.125, wk)
            p.dma("sp", qT_d[:, :, tt * 128:(tt + 1) * 128], qT[tt % 2][:, :, :], final=True)
            ps = pss[4 + tt % 2]
            gemm_T(p, hT, wkv, 256, ps, tok0)
            p.I("act", "activation", out=qs[:, 0:256], in_=ps[:, 0:256], func=AF.Copy)
            p.dma("sp", v_d[tt * 128:(tt + 1) * 128, :], qs[:, 128:256], final=True)
            qk_norm_rope(p, qs[:, 0:128], 2, VT_KG, vecT, rope[:, tt, :], ident, pss[6 + tt % 2], kT[tt % 2], 1.0, wk)
            p.dma("sp", kT_d[:, :, tt * 128:(tt + 1) * 128], kT[tt % 2][:, :, :], final=True)


C_R0 = 2048 + 1280
NCONST = 16


def rwkv_gemms(p, hT, w_in, lora_a_d, wts, pss, pc_d, czs_d, th):
    with p.scope():
        ev = [p.sb(f"rg_ev{i}", [128, 512], F32) for i in range(3)]
        n = 0
        marks = []
        for q in range(4):
            wt = wts[q % 2]
            m0 = p.mark()
            load_w(p, wt, w_in, C_R0 + q * 512, 512)
            if q >= 1:
                p.hoist(m0, marks[-1])
            marks.append(p.mark())
            for tt in range(8):
                for win, sh in enumerate((0, -1, 1)):
                    if q == 3 and win > 0:
                        continue
                    ps = pss[n % 4]
                    e = ev[n % 3]
                    n += 1
                    gemm_T(p, hT, wt, 512, ps, HALO + tt * 128 + sh)
                    if q == 3:
                        p.I("act", "activation", out=e[:, :], in_=ps[:, :], func=AF.Silu)
                        p.dma("sp", czs_d[tt * 128:(tt + 1) * 128, :], e[:, :], final=True)
                    else:
                        if n % 2:
                            p.I("act", "activation", out=e[:, :], in_=ps[:, :], func=AF.Copy)
                        else:
                            p.I("dve", "tensor_copy", out=e[:, :], in_=ps[:, :])
                        p.dma("sp", pc_d[q * 3 + win, tt * 128:(tt + 1) * 128, :], e[:, :])
        wt = wts[0]
        m0 = p.mark()
        load_w(p, wt, lora_a_d, 0, 384)
        p.hoist(m0, marks[-1])
        for lo in range(4):
            for tb in range(2):
                ps = pss[4 + (lo * 2 + tb) % 2]
                for d in range(16):
                    p.I("pe", "matmul", acc=(d > 0), out=ps[0:96, :], lhsT=wt[:, d, lo * 96:(lo + 1) * 96],
                        rhs=hT[:, d, HALO + tb * 512:HALO + (tb + 1) * 512], start=(d == 0), stop=(d == 15))
                p.I("act", "activation", out=th[:, lo, tb * 512:(tb + 1) * 512], in_=ps[0:96, :],
                    func=(AF.Tanh if lo < 2 else AF.Copy))


def rwkv_pass1(p, pc_d, lora_b_d, th, vecT, consts, pss, bonus_d, v2_d, QT_d, Yl_d, GH_d, segT_d):
    ident = consts[:, 0, :]
    ones = consts[:, 5, :]
    with p.scope():
        lb = p.sb("rw_lb", [96, 4, 512], F32)
        p.dma("sp", lb[:, :, :], lora_b_d[:, :, :])
        S = p.sb("rw_S", [128, 17, 1024], F32)
        names = ["r2", "k2", "v2", "lw", "aa", "kk", "kt", "bb", "cum", "wt", "Bt", "Kt", "Rt", "Bh", "Kh", "t1", "t2"]
        sl = {n: i for i, n in enumerate(names)}
        W = lambda n: S.k(n)[:, sl[n], :]
        We = lambda n, e: S.k(n)[:, sl[n], e * 512:(e + 1) * 512]
        X = p.sb("rw_X", [128, 16, 128], F32)
        rkv = p.sb("rw_rkv", [128, 9, 512], F32)
        ssq = p.sb("rw_ssq", [128, 16], F32)
        NU = 4
        T5 = [p.sb(f"rw_T5_{i}", [128, 5, 128], F32) for i in range(NU)]
        NP = [p.sb(f"rw_NP_{i}", [128, 2, 2, 128], F32) for i in range(NU)]
        TR = [p.sb(f"rw_TR_{i}", [128, 4, 128], F32) for i in range(NU)]
        for t_ in TR:
            p.I("pool", "memset", ap=t_[:, :, :], constant=0.0)
        OUTQ = [p.sb(f"rw_oq_{i}", [64, 128], F32) for i in range(NU)]
        OUTY = [p.sb(f"rw_oy_{i}", [128, 64], F32) for i in range(NU)]
        OUTG = [p.sb(f"rw_og_{i}", [64, 128], F32) for i in range(NU)]
        dgW = p.sb("rw_dgW", [64, 16, 64], F32)
        omka = p.sb("rw_omka", [128, 512], F32)
        vt = lambda off, n=512: vecT[:, off:off + n]
        p.I("dve", "tensor_scalar", out=omka[:, :], in0=vt(VT_KA), scalar1=-1.0, scalar2=1.0, op0=ALU.mult, op1=ALU.add)
        import os
        for tt in range(8):
            tsl = slice(tt * 128, (tt + 1) * 128)
            p.dma("sp", rkv[:, :, :], V(pc_d.t if isinstance(pc_d, V) else pc_d, pc_d.h[0:9, tsl, :].rearrange("q t c -> t q c"), None))
            for e in range(2):
                for qi, nm in enumerate(("r2", "k2", "v2")):
                    cur = rkv[:, qi * 3, :]
                    sh = rkv[:, qi * 3 + 1 + e, :]
                    eng = "dve" if (qi + e) % 2 == 0 else "pool"
                    p.I(eng, "tensor_tensor", out=We(nm, e), in0=sh, in1=cur, op=ALU.subtract)
                    p.I(eng, "tensor_tensor", out=We(nm, e), in0=We(nm, e), in1=vt(VT_MU + (e * 3 + qi) * 512), op=ALU.mult)
                    p.I(eng, "tensor_tensor", out=We(nm, e), in0=We(nm, e), in1=cur, op=ALU.add)
            for e in range(2):
                for kind in range(2):
                    lo = kind * 2 + e
                    ps = pss[lo % 2]
                    p.I("pe", "matmul", out=ps[:, :], lhsT=th[:, lo, tsl], rhs=lb[:, lo, :], start=True, stop=True)
                    dst = We("lw" if kind == 0 else "aa", e)
                    p.I("dve", "tensor_tensor", out=dst, in0=ps[:, :], in1=vt((VT_W0 if kind == 0 else VT_A0) + e * 512), op=ALU.add)
                    p.I("act", "activation", out=dst, in_=dst, func=AF.Sigmoid)
            p.I("pool", "tensor_scalar", out=W("lw"), in0=W("lw"), scalar1=-0.6065306597126334, scalar2=None, op0=ALU.mult)
            for e in range(2):
                p.I("dve", "tensor_tensor", out=We("kk", e), in0=We("k2", e), in1=vt(VT_KK), op=ALU.mult)
                p.I("pool", "tensor_tensor", out=We("kt", e), in0=We("aa", e), in1=vt(VT_KA), op=ALU.mult)
                p.I("pool", "tensor_tensor", out=We("kt", e), in0=We("kt", e), in1=omka[:, :], op=ALU.add)
                p.I("pool", "tensor_tensor", out=We("kt", e), in0=We("kt", e), in1=We("k2", e), op=ALU.mult)
            p.I("act", "activation", out=W("t1"), in_=W("kk"), func=AF.Square)
            p.I("dve", "tensor_reduce", out=ssq[:, :], in_=W("t1").rr("p (u k) -> p u k", k=64), axis=AX.X, op=ALU.add)
            p.I("dve", "tensor_scalar", out=ssq[:, :], in0=ssq[:, :], scalar1=1e-24, scalar2=None, op0=ALU.max)
            rsqrt(p, ssq[:, :], ssq[:, :], 1.0, 0.0)
            p.I("dve", "tensor_tensor", out=W("kk").rr("p (u k) -> p u k", k=64), in0=W("kk").rr("p (u k) -> p u k", k=64),
                in1=ssq[:, :].us(2).bc([128, 16, 64]), op=ALU.mult)
            p.I("pool", "tensor_tensor", out=W("bb"), in0=W("kk"), in1=W("aa"), op=ALU.mult)
            for e in range(2):
                p.I("dve", "tensor_tensor", out=We("t1", e), in0=We("r2", e), in1=We("kt", e), op=ALU.mult)
                p.I("dve", "tensor_tensor", out=We("t1", e), in0=We("t1", e), in1=vt(VT_RK), op=ALU.mult)
            p.I("dve", "tensor_reduce", out=ssq[:, :], in_=W("t1").rr("p (u k) -> p u k", k=64), axis=AX.X, op=ALU.add)
            p.I("dve", "tensor_tensor", out=W("t2").rr("p (u k) -> p u k", k=64), in0=W("v2").rr("p (u k) -> p u k", k=64),
                in1=ssq[:, :].us(2).bc([128, 16, 64]), op=ALU.mult)
            for e in range(2):
                p.dma("sp", bonus_d[e, tsl, :], We("t2", e), final=True)
            for e in range(2):
                psc, pst_ = pss[2 + e], pss[4 + e]
                p.I("pe", "matmul", out=psc[:, :], lhsT=consts[:, 2 if e == 0 else 1, :], rhs=We("lw", e), start=True, stop=True)
                p.I("pe", "matmul", out=pst_[:, :], lhsT=ones, rhs=We("lw", e), start=True, stop=True)
                p.I("act", "activation", out=We("cum", e), in_=psc[:, :], func=AF.Copy)
                p.I("act", "activation", out=We("wt", e), in_=psc[:, :], func=AF.Exp)
                p.I("dve", "tensor_tensor", out=We("Rt", e), in0=We("r2", e), in1=We("wt", e), op=ALU.mult)
                p.I("act", "activation", out=We("wt", e), in_=psc[:, :], func=AF.Exp, scale=-1.0)
                p.I("dve", "tensor_tensor", out=We("Bt", e), in0=We("bb", e), in1=We("wt", e), op=ALU.mult)
                p.I("pool", "tensor_tensor", out=We("Kt", e), in0=We("kt", e), in1=We("wt", e), op=ALU.mult)
                p.I("dve", "tensor_tensor", out=We("t1", e), in0=pst_[:, :], in1=We("cum", e), op=ALU.subtract)
                p.I("act", "activation", out=We("t1", e), in_=We("t1", e), func=AF.Exp)
                p.I("dve", "tensor_tensor", out=We("Bh", e), in0=We("bb", e), in1=We("t1", e), op=ALU.mult)
                p.I("pool", "tensor_tensor", out=We("Kh", e), in0=We("kt", e), in1=We("t1", e), op=ALU.mult)
                p.I("dve", "tensor_tensor", out=We("t2", e), in0=We("cum", e), in1=We("lw", e), op=ALU.subtract)
                p.I("act", "activation", out=We("t2", e), in_=We("t2", e), func=AF.Exp)
                p.I("dve", "scalar_tensor_tensor", out=X[:, e * 8:(e + 1) * 8, 0:64], in0=We("kk", e).rr("p (u k) -> p u k", k=64),
                    scalar=-1.0, in1=We("t2", e).rr("p (u k) -> p u k", k=64), op0=ALU.mult, op1=ALU.mult)
                p.I("act", "activation", out=We("t2", e)[0:64, :], in_=pst_[0:64, :], func=AF.Exp)
                p.I("dve", "tensor_tensor", out=dgW[:, e * 8:(e + 1) * 8, :], in0=We("t2", e)[0:64, :].rr("p (u k) -> p u k", k=64),
                    in1=consts[0:64, 0, 0:64].us(1).bc([64, 8, 64]), op=ALU.mult)
            def unit(u, slot):
                e, h = divmod(u, 8)
                cs = slice(u * 64, (u + 1) * 64)
                t5, npw, tr = T5[slot], NP[slot], TR[slot]
                psa, psb = pss[slot * 2], pss[slot * 2 + 1]
                Vv = W("v2")[:, cs]
                for i, src in enumerate((X.k(u)[:, u, 0:64], W("Bt")[:, cs], W("Kt")[:, cs], W("Rt")[:, cs])):
                    p.I("pe", "transpose", out=psa[0:64, i * 128:(i + 1) * 128], in_=src, identity=ident)
                p.I("act", "activation", out=tr[0:64, :, :], in_=psa[0:64, :].rr("p (i t) -> p i t", t=128), func=AF.Copy)
                yield
                AT, BT, KT, RT = (tr[:, i, :] for i in range(4))
                mm = lambda out, l, r: p.I("pe", "matmul", out=out, lhsT=l, rhs=r, start=True, stop=True)
                mm(psb[:, 0:128], AT, BT)
                mm(psb[:, 128:256], BT, AT)
                mm(psb[:, 256:384], KT, AT)
                p.I("dve", "tensor_tensor", out=t5[:, 0:3, :], in0=psb[:, 0:384].rr("p (i t) -> p i t", t=128),
                    in1=consts[:, 6 + e * 5:9 + e * 5, :], op=ALU.mult)
                mm(psa[:, 0:128], KT, RT)
                mm(psa[:, 128:256], BT, RT)
                p.I("dve", "tensor_tensor", out=t5[:, 3:5, :], in0=psa[:, 0:256].rr("p (i t) -> p i t", t=128),
                    in1=consts[:, 9 + e * 5:11 + e * 5, :], op=ALU.mult)
                yield
                mm(psb[:, 0:64], t5[:, 2, :], Vv)
                p.I("act", "activation", out=X.k(u)[:, u, 64:128], in_=psb[:, 0:64], func=AF.Copy)
                p.I("pool", "tensor_copy", out=npw[:, 0, :, :], in_=t5[:, 0:2, :])
                yield
                for lvl in range(7):
                    cur = npw[:, lvl % 2, :, :]
                    nxt = npw[:, (lvl + 1) % 2, :, :]
                    ps = psa if lvl % 2 == 0 else psb
                    mm(ps[:, 0:128], cur[:, 1, :], X.k(u)[:, u, :])
                    if lvl < 6:
                        mm(ps[:, 128:256], cur[:, 1, :], cur[:, 0, :])
                        mm(ps[:, 256:384], cur[:, 0, :], cur[:, 1, :])
                    p.I("dve", "tensor_tensor", out=X.k(u)[:, u, :], in0=ps[:, 0:128], in1=X.k(u)[:, u, :], op=ALU.add)
                    if lvl < 6:
                        p.I("act", "activation", out=nxt, in_=ps[:, 128:384].rr("p (i t) -> p i t", t=128), func=AF.Copy)
                    yield
                P_, U_ = X.k(u)[:, u, 0:64], X.k(u)[:, u, 64:128]
                mm(psa[0:64, 0:128], P_, t5[:, 4, :])
                p.I("pe", "matmul", out=psa[:, 128:192], lhsT=t5[:, 3, :], rhs=Vv, start=True, stop=False)
                p.I("pe", "matmul", acc=True, out=psa[:, 128:192], lhsT=t5[:, 4, :], rhs=U_, start=False, stop=True)
                mm(psa[0:64, 192:256], P_, W("Bh")[:, cs])
                p.I("pe", "matmul", out=psa[0:64, 256:320], lhsT=W("Bh")[:, cs], rhs=U_, start=True, stop=False)
                p.I("pe", "matmul", acc=True, out=psa[0:64, 256:320], lhsT=W("Kh")[:, cs], rhs=Vv, start=False, stop=True)
                oq, oy, og = OUTQ[slot], OUTY[slot], OUTG[slot]
                p.I("dve", "tensor_tensor", out=oq[:, :], in0=psa[0:64, 0:128], in1=tr[0:64, 3, :], op=ALU.add)
                p.I("act", "activation", out=oy[:, :], in_=psa[:, 128:192], func=AF.Copy)
                p.I("dve", "tensor_tensor", out=og[:, 0:64], in0=psa[0:64, 192:256], in1=dgW[:, u, :], op=ALU.add)
                p.I("act", "activation", out=og[:, 64:128], in_=psa[0:64, 256:320], func=AF.Copy)
                p.dma("sp", QT_d[u, tt, :, :], oq[:, :], final=True)
                p.dma("sp", Yl_d[u, tt, :, :], oy[:, :], final=True)
                p.dma("sp", GH_d[u, tt, :, :], og[:, :], final=True)
                yield

            for u0 in range(0, 16, NU):
                gens = [unit(u0 + i, i) for i in range(NU)]
                while gens:
                    for g in list(gens):
                        try:
                            next(g)
                        except StopIteration:
                            gens.remove(g)
    import os
    if 1:
      with p.scope():
        st = [p.sb(f"rw_st{i}", [128, 16, 128], F32) for i in range(2)]
        gh = [p.sb(f"rw_gh{i}", [128, 8, 128], F32) for i in range(2)]
        for t_ in st + gh:
            p.I("pool", "memset", ap=t_[:, :, :], constant=0.0)
        p.I("dve", "tensor_copy", out=st[0][0:64, :, 64:128], in_=consts[0:64, 0, 0:64].us(1).bc([64, 16, 64]))
        for e in range(2):
            for step in range(8):
                c = step if e == 0 else 7 - step
                g = gh[step % 2]
                src = V(GH_d, GH_d.h[e * 8:(e + 1) * 8, c, :, :].rearrange("u k n -> k u n"), None)
                p.dma("sp", g[0:64, :, :], src)
                a, b = st[step % 2], st[(step + 1) % 2]
                for half in range(2):
                    ps = pss[half]
                    for i in range(4):
                        u = e * 8 + half * 4 + i
                        p.I("pe", "matmul", out=ps[0:64, i * 128:(i + 1) * 128], lhsT=g[:, half * 4 + i, 0:64], rhs=a[:, u, :],
                            start=True, stop=True)
                    us = slice(e * 8 + half * 4, e * 8 + half * 4 + 4)
                    pv = ps[0:64, :].rr("p (i t) -> p i t", t=128)
                    p.I("dve", "tensor_tensor", out=b[0:64, us, 0:64], in0=pv[:, :, 0:64], in1=g[0:64, half * 4:half * 4 + 4, 64:128], op=ALU.add)
                    p.I("act", "activation", out=b[0:64, us, 64:128], in_=pv[:, :, 64:128], func=AF.Copy)
        fin = st[0]
        for half in range(4):
            ps = pss[2 + half % 2]
            for i in range(4):
                u = half * 4 + i
                p.I("pe", "transpose", out=ps[0:64, i * 128:(i + 1) * 128], in_=fin[:, u, 64:128], identity=consts[:, 0, :])
            p.I("act", "activation", out=st[1][0:64, half * 4:half * 4 + 4, 64:128], in_=ps[0:64, :].rr("p (i t) -> p i t", t=128)[:, :, 0:64], func=AF.Copy)
            p.I("dve", "tensor_copy", out=st[1][0:64, half * 4:half * 4 + 4, 0:64], in_=fin[0:64, half * 4:half * 4 + 4, 0:64])
        p.dma("sp", V(segT_d, segT_d.h.rearrange("u k n -> k u n"), None), st[1][0:64, :, :], final=True)


def build_stage_A(p, parts=('AD', 'attn', 'rgemm', 'rw1')):
    xT = p.dram("xT", [2048, NX], F32, kind="ExternalInput")
    w_in = p.dram("w_inA", [2048, 6912], F32, kind="ExternalInput")
    lora_a = p.dram("lora_a", [2048, 384], F32, kind="ExternalInput")
    lora_b = p.dram("lora_b", [96, 4, 512], F32, kind="ExternalInput")
    vecF_d = p.dram("vecF", [128, VF_N], F32, kind="ExternalInput")
    vecT_d = p.dram("vecT", [128, VT_N], F32, kind="ExternalInput")
    rope_d = p.dram("rope", [128, 8, 64], F32, kind="ExternalInput")
    consts_d = p.dram("consts", [128, NCONST, 128], F32, kind="ExternalInput")
    yaT = p.dram("yaT", [512, NT], F32, kind="ExternalOutput")
    ydT = p.dram("ydT", [512, NT], F32, kind="ExternalOutput")
    bzT = p.dram("bzT", [512, NT], F32, kind="ExternalOutput")
    qT = p.dram("qT", [64, 8, NT], F32, kind="ExternalOutput")
    kT = p.dram("kT", [64, 2, NT], F32, kind="ExternalOutput")
    vv = p.dram("vv", [NT, 128], F32, kind="ExternalOutput")
    czs = p.dram("czs", [NT, 512], F32, kind="ExternalOutput")
    bonus = p.dram("bonus", [2, NT, 512], F32, kind="ExternalOutput")
    QT = p.dram("rQT", [16, 8, 64, 128], F32, kind="ExternalOutput")
    Yl = p.dram("rYl", [16, 8, 128, 64], F32, kind="ExternalOutput")
    GH = p.dram("rGH", [16, 8, 64, 128], F32, kind="ExternalOutput")
    segT = p.dram("rsegT", [16, 64, 128], F32, kind="ExternalOutput")
    pc = p.dram("pc", [9, NT, 512], F32, kind="Internal")
    vecF = p.sb("vecF_s", [128, VF_N], F32)
    vecT = p.sb("vecT_s", [128, VT_N], F32)
    rope = p.sb("rope_s", [128, 8, 64], F32)
    consts = p.sb("consts_s", [128, NCONST, 128], F32)
    th = p.sb("th_s", [96, 4, NT], F32)
    pss = [p.ps(f"ps{i}", [128, 512], F32) for i in range(8)]
    p.dma("sp", vecF[:, :], vecF_d[:, :])
    p.dma("sp", vecT[:, :], vecT_d[:, :])
    p.dma("sp", rope[:, :, :], rope_d[:, :, :])
    p.dma("sp", consts[:, :, :], consts_d[:, :, :])
    ones = consts[:, 5, :]
    with p.scope():
        hT = p.sb("hT", [128, 16, NX], BF16)
        wts = [p.sb(f"wt{i}", [128, 16, 512], BF16) for i in range(2)]
        rmsnorm_hT(p, xT, vecF, hT, ones, pss)
        if 'AD' in parts:
            branch_AD(p, hT, w_in, vecF, ones, wts, pss, yaT, ydT, bzT)
        if 'attn' in parts:
            attn_prep(p, hT, w_in, vecT, rope, consts[:, 0, :], wts, pss, qT, kT, vv)
        if 'rgemm' in parts:
            rwkv_gemms(p, hT, w_in, lora_a, wts, pss, pc, czs, th)
    if 'rw1' in parts:
        rwkv_pass1(p, pc, lora_b, th, vecT, consts, pss, bonus, None, QT, Yl, GH, segT)


G0 = 6912


def load_cast(p, dst, src_t, src_ap, q="pool"):
    p.dma(q, dst, V(src_t, src_ap, None))


def attention(p, qT_d, kTf_d, vf_d, bzT_d, consts, pss, yT, kv_src=None):
    ones_bf = None
    with p.scope():
        kT = p.sb("at_kT", [64, 2, 4096], BF16)
        qT = p.sb("at_qT", [64, 8, NT], BF16)
        vz = p.sb("at_vz", [128, 32, 2, 192], BF16)
        onesb = p.sb("at_ones", [128, 128], BF16)
        bz = p.sb("at_bz", [128, 4, NT], F32)
        PT = [p.sb(f"at_PT{i}", [128, 512], BF16) for i in range(3)]
        rden = p.sb("at_rden", [128, 512], F32)
        p.I("pool", "memset", ap=vz[:, :, :, :], constant=0.0)
        p.I("pool", "memset", ap=onesb[:, :], constant=1.0)
        if kv_src is None:
            k_src = lambda g, j: kTf_d[:, g, j * 1024:(j + 1) * 1024]
            v_src = lambda g, j: V(vf_d, vf_d.h[j * 1024:(j + 1) * 1024, g * 64:(g + 1) * 64].rearrange("(t p) d -> p t d", p=128), None)
        else:
            k_src, v_src = kv_src
        for g in range(2):
            for j in range(4):
                p.dma("pool", kT[:, g, j * 1024:(j + 1) * 1024], k_src(g, j))
        for h0 in range(0, 8, 2):
            p.dma("pool", qT[:, h0:h0 + 2, :], qT_d[:, h0:h0 + 2, :])
        for g in range(2):
            for t4 in range(4):
                p.dma("pool", vz[:, t4 * 8:(t4 + 1) * 8, g, 64:128], v_src(g, t4))
        p.dma("sp", bz[:, :, :], V(bzT_d, bzT_d.h.rearrange("(j p) t -> p j t", p=128), None))
        n = 0
        for h in range(8):
            g = h // 4
            po = (h % 2) * 64
            lo = 64 if h % 2 == 0 else 0
            for qb in range(2):
                qs = slice(qb * 512, (qb + 1) * 512)
                pso, psd = pss[4 + (n % 2)], pss[6 + (n % 2)]
                n += 1
                def score(kt):
                    p.I("pe", "matmul", out=pss[kt % 4][:, :], lhsT=kT[:, g, kt * 128:(kt + 1) * 128], rhs=qT[:, h, qs],
                        start=True, stop=True)
                    p.I("act", "activation", out=PT[kt % 3][:, :], in_=pss[kt % 4][:, :], func=AF.Exp)

                score(0)
                score(1)
                for kt in range(32):
                    pt = PT[kt % 3]
                    p.I("pe", "matmul", acc=(kt > 0), out=pso[:, :], lhsT=vz[:, kt, g, lo:lo + 128], rhs=pt[:, :],
                        start=(kt == 0), stop=(kt == 31))
                    p.I("pe", "matmul", acc=(kt > 0), out=psd[:, :], lhsT=onesb[:, :], rhs=pt[:, :],
                        start=(kt == 0), stop=(kt == 31))
                    if kt + 2 < 32:
                        score(kt + 2)
                ps_ = slice(po, po + 64)
                p.I("dve", "reciprocal", out=rden[ps_, :], in_=psd[ps_, :])
                p.I("dve", "tensor_tensor", out=rden[ps_, :], in0=rden[ps_, :], in1=bz[ps_, h // 2, qs], op=ALU.mult)
                p.I("dve", "tensor_tensor", out=yT[ps_, 1, h // 2, qs], in0=pso[ps_, :], in1=rden[ps_, :], op=ALU.mult)


def compose_host_slots(p, ST, sg, pss, segs_d):
    cur = 0
    for s in range(3):
        p.dma("sp", sg[0:64, :, :], V(segs_d, segs_d.h[s].rearrange("u k n -> k u n"), None))
        a, b = ST[cur], ST[1 - cur]
        for half in range(2):
            ps = pss[half]
            for i in range(8):
                u = half * 8 + i
                p.I("pe", "matmul", out=ps[0:64, i * 64:(i + 1) * 64], lhsT=sg[:, u, 64:128], rhs=a[:, u, :],
                    start=True, stop=True)
            us = slice(half * 8, half * 8 + 8)
            p.I("dve", "tensor_tensor", out=b[0:64, us, :], in0=ps[0:64, :].rr("p (i t) -> p i t", t=64),
                in1=sg[0:64, us, 0:64], op=ALU.add)
        cur = 1 - cur
    return cur


def rwkv_pass2(p, QT_d, Yl_d, GH_d, segs_d, bonus_d, czs_d, vecT, consts, pss, yT, compose=None):
    ident = consts[:, 0, :]
    with p.scope():
        ST = [p.sb(f"r2_ST{i}", [128, 16, 64], F32) for i in range(2)]
        sg = p.sb("r2_sg", [128, 16, 128], F32)
        qt = [p.sb(f"r2_qt{i}", [128, 16, 128], F32) for i in range(2)]
        gh = [p.sb(f"r2_gh{i}", [128, 16, 128], F32) for i in range(2)]
        yl = [p.sb(f"r2_yl{i}", [128, 16, 64], F32) for i in range(2)]
        yn = p.sb("r2_yn", [128, 8, 2, 512], F32)
        for t_ in ST + qt + gh + [sg]:
            p.I("pool", "memset", ap=t_[:, :, :], constant=0.0)
        if compose is not None:
            cur = compose(p, ST, sg, pss)
        else:
            cur = compose_host_slots(p, ST, sg, pss, segs_d)
        for s in range(8):
            q_, g_, y_ = qt[s % 2], gh[s % 2], yl[s % 2]
            for e in range(2):
                c = s if e == 0 else 7 - s
                us = slice(e * 8, e * 8 + 8)
                p.dma("sp", q_[0:64, us, :], V(QT_d, QT_d.h[e * 8:(e + 1) * 8, c].rearrange("u k n -> k u n"), None))
                p.dma("sp", g_[0:64, us, :], V(GH_d, GH_d.h[e * 8:(e + 1) * 8, c].rearrange("u k n -> k u n"), None))
                p.dma("sp", y_[:, us, :], V(Yl_d, Yl_d.h[e * 8:(e + 1) * 8, c].rearrange("u t n -> t u n"), None))
            a, b = ST[cur], ST[1 - cur]
            for e in range(2):
                c = s if e == 0 else 7 - s
                psy, pss_ = pss[2 + e], pss[4 + e]
                for i in range(8):
                    u = e * 8 + i
                    p.I("pe", "matmul", out=psy[:, i * 64:(i + 1) * 64], lhsT=q_[:, u, :], rhs=a[:, u, :], start=True, stop=True)
                    p.I("pe", "matmul", out=pss_[0:64, i * 64:(i + 1) * 64], lhsT=g_[:, u, 0:64], rhs=a[:, u, :], start=True, stop=True)
                us = slice(e * 8, e * 8 + 8)
                p.I("dve", "tensor_tensor", out=yn[:, c, e, :].rr("p (u k) -> p u k", k=64), in0=psy[:, :].rr("p (u k) -> p u k", k=64),
                    in1=y_[:, us, :], op=ALU.add)
                p.I("dve", "tensor_tensor", out=b[0:64, us, :], in0=pss_[0:64, :].rr("p (i t) -> p i t", t=64),
                    in1=g_[0:64, us, 64:128], op=ALU.add)
            cur = 1 - cur
        st1 = p.sb("r2_s1", [128, 16], F32)
        st2 = p.sb("r2_s2", [128, 16], F32)
        sq = p.sb("r2_sq", [128, 2, 512], F32)
        bon = [p.sb(f"r2_bon{i}", [128, 2, 512], F32) for i in range(2)]
        cz = [p.sb(f"r2_cz{i}", [128, 512], F32) for i in range(2)]
        yc = [p.sb(f"r2_yc{i}", [128, 512], F32) for i in range(2)]
        for c in range(8):
            tsl = slice(c * 128, (c + 1) * 128)
            bo, cz_, yc_ = bon[c % 2], cz[c % 2], yc[c % 2]
            p.dma("sp", bo[:, :, :], V(bonus_d, bonus_d.h[:, tsl, :].rearrange("e t c -> t e c"), None))
            p.dma("sp", cz_[:, :], czs_d[tsl, :])
            y2 = yn[:, c, :, :]
            y3 = y2.rr("p e (h k) -> p (e h) k", k=64)
            p.I("dve", "tensor_reduce", out=st1[:, :], in_=y3, axis=AX.X, op=ALU.add)
            p.I("dve", "tensor_scalar", out=st1[:, :], in0=st1[:, :], scalar1=1.0 / 64, scalar2=None, op0=ALU.mult)
            p.I("dve", "tensor_tensor", out=y3, in0=y3, in1=st1[:, :].us(2).bc([128, 16, 64]), op=ALU.subtract)
            p.I("act", "activation", out=sq[:, :, :], in_=y2, func=AF.Square)
            p.I("dve", "tensor_reduce", out=st2[:, :], in_=sq[:, :, :].rr("p e (h k) -> p (e h) k", k=64), axis=AX.X, op=ALU.add)
            rsqrt(p, st2[:, :], st2[:, :], 1.0 / 64, GN_EPS)
            p.I("dve", "tensor_tensor", out=y3, in0=y3, in1=st2[:, :].us(2).bc([128, 16, 64]), op=ALU.mult)
            p.I("pool", "tensor_tensor", out=y2, in0=y2, in1=vecT[:, VT_GNG:VT_GNG + 512].us(1).bc([128, 2, 512]), op=ALU.mult)
            p.I("pool", "tensor_tensor", out=y2, in0=y2, in1=vecT[:, VT_GNB:VT_GNB + 512].us(1).bc([128, 2, 512]), op=ALU.add)
            p.I("dve", "tensor_tensor", out=y2, in0=y2, in1=bo[:, :, :], op=ALU.add)
            p.I("dve", "tensor_tensor", out=yc_[:, :], in0=yn[:, c, 0, :], in1=yn[:, c, 1, :], op=ALU.add)
            p.I("pool", "tensor_tensor", out=yc_[:, :], in0=yc_[:, :], in1=cz_[:, :], op=ALU.mult)
            ps = pss[6 + c % 2]
            for j in range(4):
                p.I("pe", "transpose", out=ps[:, j * 128:(j + 1) * 128], in_=yc_[:, j * 128:(j + 1) * 128], identity=ident)
            p.I("act", "activation", out=yT[:, 2, :, tsl], in_=ps[:, :].rr("p (j t) -> p j t", t=128), func=AF.Copy)


def merge_out(p, xT_d, w_g_d, w_br_d, w_out_d, vecG, hT, wts, pss, yT, out_d, edge_d=None):
    with p.scope():
        mT = p.sb("mo_mT", [128, 16, NT], BF16)
        mg = p.sb("mo_mg", [128, 4, NT], F32)
        wbr = [p.sb(f"mo_wbr{i}", [128, 4, 512], BF16) for i in range(2)]
        gt = [p.sb(f"mo_g{i}", [128, 512], F32) for i in range(2)]
        n = 0
        marks = []
        for dg in range(4):
            for br in range(4):
                wt = wts[n % 2]
                wb = wbr[n % 2]
                n += 1
                m0 = p.mark()
                load_w(p, wt, w_g_d, br * 2048 + dg * 512, 512)
                p.dma("pool", wb[:, :, :], V(w_br_d, w_br_d.h[br, :, dg * 512:(dg + 1) * 512].rearrange("(j p) n -> p j n", p=128), None))
                if marks:
                    p.hoist(m0, marks[-1])
                marks.append(p.mark())
                for dci in range(4):
                    dc = dg * 4 + dci
                    for tb in range(2):
                        tsl = slice(tb * 512, (tb + 1) * 512)
                        psg, psb = pss[(dci * 2 + tb) % 4], pss[4 + (dci * 2 + tb) % 4]
                        gemm_F(p, hT, wt, dci, psg, HALO + tb * 512, 512)
                        for j in range(4):
                            p.I("pe", "matmul", acc=(j > 0), out=psb[:, :], lhsT=wb[:, j, dci * 128:(dci + 1) * 128],
                                rhs=yT[:, br, j, tsl], start=(j == 0), stop=(j == 3))
                        g = gt[(dci * 2 + tb) % 2]
                        p.I("act", "activation", out=g[:, :], in_=psg[:, :], func=AF.Sigmoid,
                            bias=vecG[:, br * 16 + dc:br * 16 + dc + 1], scale=1.0)
                        if br == 0:
                            p.I("dve", "tensor_tensor", out=mg[:, dci, tsl], in0=psb[:, :], in1=g[:, :], op=ALU.mult)
                        else:
                            p.I("dve", "tensor_tensor", out=g[:, :], in0=psb[:, :], in1=g[:, :], op=ALU.mult)
                            if br < 3:
                                p.I("pool", "tensor_tensor", out=mg[:, dci, tsl], in0=mg[:, dci, tsl], in1=g[:, :], op=ALU.add)
                            else:
                                p.I("pool", "tensor_tensor", out=mT[:, dc, tsl], in0=mg[:, dci, tsl], in1=g[:, :], op=ALU.add)
        xr = [p.sb(f"mo_xr{i}", [128, 512], F32) for i in range(2)]
        ot = [p.sb(f"mo_ot{i}", [128, 512], F32) for i in range(2)]
        n = 0
        for og in range(4):
            wt = wts[og % 2]
            m0 = p.mark()
            load_w(p, wt, w_out_d, og * 512, 512)
            p.hoist(m0, marks[-1])
            marks.append(p.mark())
            for oci in range(4):
                oc = og * 4 + oci
                for tb in range(2):
                    tsl = slice(tb * 512, (tb + 1) * 512)
                    ps = pss[n % 4]
                    x_, o_ = xr[n % 2], ot[n % 2]
                    n += 1
                    p.dma("sp", x_[:, :], xT_d[oc * 128:(oc + 1) * 128, HALO + tb * 512:HALO + (tb + 1) * 512])
                    for d in range(16):
                        p.I("pe", "matmul", acc=(d > 0), out=ps[:, :], lhsT=wt[:, d, oci * 128:(oci + 1) * 128],
                            rhs=mT[:, d, tsl], start=(d == 0), stop=(d == 15))
                    p.I("dve", "tensor_tensor", out=o_[:, :], in0=ps[:, :], in1=x_[:, :], op=ALU.add)
                    p.dma("sp", out_d[oc * 128:(oc + 1) * 128, tsl], o_[:, :], final=True)
                    if edge_d is not None:
                        if tb == 0:
                            p.dma("sp", edge_d[oc * 128:(oc + 1) * 128, 0:16], o_[:, 0:16])
                        else:
                            p.dma("sp", edge_d[oc * 128:(oc + 1) * 128, 16:32], o_[:, 496:512])


def build_stage_B(p, parts=("attn", "rw2", "merge")):
    xT = p.dram("xT", [2048, NX], F32, kind="ExternalInput")
    w_g = p.dram("w_g", [2048, 8192], F32, kind="ExternalInput")
    w_br = p.dram("w_br", [4, 512, 2048], F32, kind="ExternalInput")
    w_out = p.dram("w_out", [2048, 2048], F32, kind="ExternalInput")
    vecF_d = p.dram("vecF", [128, VF_N], F32, kind="ExternalInput")
    vecT_d = p.dram("vecT", [128, VT_N], F32, kind="ExternalInput")
    vecG_d = p.dram("vecG", [128, 64], F32, kind="ExternalInput")
    consts_d = p.dram("consts", [128, NCONST, 128], F32, kind="ExternalInput")
    yaT = p.dram("yaT", [512, NT], F32, kind="ExternalInput")
    ydT = p.dram("ydT", [512, NT], F32, kind="ExternalInput")
    bzT = p.dram("bzT", [512, NT], F32, kind="ExternalInput")
    qT = p.dram("qT", [64, 8, NT], F32, kind="ExternalInput")
    kTf = p.dram("kTf", [64, 2, 4096], F32, kind="ExternalInput")
    vf = p.dram("vf", [4096, 128], F32, kind="ExternalInput")
    czs = p.dram("czs", [NT, 512], F32, kind="ExternalInput")
    bonus = p.dram("bonus", [2, NT, 512], F32, kind="ExternalInput")
    QT = p.dram("rQT", [16, 8, 64, 128], F32, kind="ExternalInput")
    Yl = p.dram("rYl", [16, 8, 128, 64], F32, kind="ExternalInput")
    GH = p.dram("rGH", [16, 8, 64, 128], F32, kind="ExternalInput")
    segs = p.dram("segs", [3, 16, 64, 128], F32, kind="ExternalInput")
    out = p.dram("xoT", [2048, NT], F32, kind="ExternalOutput")
    vecF = p.sb("vecF_s", [128, VF_N], F32)
    vecT = p.sb("vecT_s", [128, VT_N], F32)
    vecG = p.sb("vecG_s", [128, 64], F32)
    consts = p.sb("consts_s", [128, NCONST, 128], F32)
    yT = p.sb("yT_s", [128, 4, 4, NT], BF16)
    pss = [p.ps(f"ps{i}", [128, 512], F32) for i in range(8)]
    p.dma("sp", vecF[:, :], vecF_d[:, :])
    p.dma("sp", vecT[:, :], vecT_d[:, :])
    p.dma("sp", vecG[:, :], vecG_d[:, :])
    p.dma("sp", consts[:, :, :], consts_d[:, :, :])
    p.dma("pool", yT[:, 0, :, :], V(yaT, yaT.h.rearrange("(j p) t -> p j t", p=128), None))
    p.dma("pool", yT[:, 3, :, :], V(ydT, ydT.h.rearrange("(j p) t -> p j t", p=128), None))
    if "attn" in parts:
        attention(p, qT, kTf, vf, bzT, consts, pss, yT)
    if "rw2" in parts:
        rwkv_pass2(p, QT, Yl, GH, segs, bonus, czs, vecT, consts, pss, yT)
    if "merge" in parts:
        with p.scope():
            hT = p.sb("hT", [128, 16, NX], BF16)
            rmsnorm_hT(p, xT, vecF, hT, consts[:, 5, :], pss)
            wts = [p.sb(f"wt{i}", [128, 16, 512], BF16) for i in range(2)]
            merge_out(p, xT, w_g, w_br, w_out, vecG, hT, wts, pss, yT, out)
    else:
        dbg = p.dram("dbg_yT", [128, 16, NT], F32, kind="ExternalOutput")
        p.dma("pool", dbg[:, :, :], yT[:, :, :, :].rr("p b j t -> p (b j) t"), final=True)


def build_stage_C(p):
    xT = p.dram("xT", [2048, NT], F32, kind="ExternalInput")
    vecF_d = p.dram("vecF", [128, VF_N], F32, kind="ExternalInput")
    out = p.dram("yT", [2048, NT], F32, kind="ExternalOutput")
    vecF = p.sb("vecF_s", [128, VF_N], F32)
    ones = p.sb("ones_s", [128, 128], F32)
    xs = p.sb("xs", [128, 16, NT], F32)
    sq = [p.sb(f"sq{i}", [128, 512], F32) for i in range(2)]
    rstd = p.sb("rstd", [128, 512], F32)
    ot = [p.sb(f"ot{i}", [128, 512], F32) for i in range(2)]
    pss = [p.ps(f"ps{i}", [128, 512], F32) for i in range(2)]
    p.dma("sp", vecF[:, :], vecF_d[:, :])
    p.I("pool", "memset", ap=ones[:, :], constant=1.0)
    for cg in range(4):
        src = V(xT, xT.h[cg * 512:(cg + 1) * 512, :].rearrange("(c p) n -> p c n", p=128), None)
        p.dma("sp", xs[:, cg * 4:(cg + 1) * 4, :], src)
    for tb in range(2):
        tsl = slice(tb * 512, (tb + 1) * 512)
        ps = pss[tb]
        for d in range(16):
            s = sq[d % 2]
            p.I("act", "activation", out=s[:, :], in_=xs[:, d, tsl], func=AF.Square)
            p.I("pe", "matmul", acc=(d > 0), out=ps[:, :], lhsT=ones[:, :], rhs=s[:, :], start=(d == 0), stop=(d == 15))
        rsqrt(p, rstd[:, :], ps[:, :], 1.0 / 2048, RMS_EPS)
        for d in range(16):
            o = ot[d % 2]
            p.I("dve", "scalar_tensor_tensor", out=o[:, :], in0=xs[:, d, tsl], scalar=vecF[:, VF_NORMG + d:VF_NORMG + d + 1],
                in1=rstd[:, :], op0=ALU.mult, op1=ALU.mult)
            p.dma("sp", out[d * 128:(d + 1) * 128, tsl], o[:, :], final=True)


def core_tok(c):
    return c // 4, (c % 4) * 1024


def make_xT(x, c, halo=16):
    b, t0 = core_tok(c)
    S = x.shape[1]
    out = np.zeros((2048, 1024 + 2 * halo), np.float32)
    lo, hi = t0 - halo, t0 + 1024 + halo
    slo, shi = max(lo, 0), min(hi, S)
    out[:, slo - lo: shi - lo] = x[b, slo:shi, :].T
    return out


def make_vecF(inp, l, final=False):
    v = np.zeros((128, 164), np.float32)
    if final:
        v[:, 0:16] = np.asarray(inp['final_norm_g'], np.float32).reshape(16, 128).T
        return v
    v[:, 0:16] = inp['norm_g'][l].reshape(16, 128).T
    v[:, 16:28] = inp['conv_a_w'][l].reshape(3, 4, 128).transpose(2, 1, 0).reshape(128, 12)
    v[:, 28:152] = inp['dw_w'][l].reshape(31, 4, 128).transpose(2, 1, 0).reshape(128, 124)
    v[:, 152:156] = inp['dw_b'][l].reshape(4, 128).T
    v[:, 156:160] = inp['ln_g'][l].reshape(4, 128).T
    v[:, 160:164] = inp['ln_b'][l].reshape(4, 128).T
    return v


def make_consts16():
    c = np.zeros((128, 16, 128), np.float32)
    i = np.arange(128)
    I = np.eye(128, dtype=np.float32)
    LI = (i[None, :] <= i[:, None]).astype(np.float32)
    UI = (i[None, :] >= i[:, None]).astype(np.float32)
    LS = (i[None, :] < i[:, None]).astype(np.float32)
    US = (i[None, :] > i[:, None]).astype(np.float32)
    for k, m in enumerate([I, LI, UI, LS, US, np.ones((128, 128), np.float32), LS, US, US, UI, UI, US, LS, LS, LI, LI]):
        c[:, k] = m
    return c


def make_vecT(inp, l):
    parts = [inp['mu_rkv'][l].reshape(-1), inp['w0'][l].reshape(-1), inp['a0'][l].reshape(-1), inp['k_k'][l], inp['k_a'][l],
             inp['r_k'][l].reshape(-1), inp['gn_g'][l], inp['gn_b'][l], inp['q_norm_g'][l], inp['k_norm_g'][l]]
    v = np.concatenate([np.asarray(x, np.float32).reshape(-1) for x in parts])
    return np.ascontiguousarray(np.broadcast_to(v[None, :], (128, v.size)))


def make_lora(inp, l):
    la = np.concatenate([inp['w_lora_a'][l][0], inp['w_lora_a'][l][1], inp['a_lora_a'][l][0], inp['a_lora_a'][l][1]], axis=1)
    lb = np.stack([inp['w_lora_b'][l][0], inp['w_lora_b'][l][1], inp['a_lora_b'][l][0], inp['a_lora_b'][l][1]], axis=1)
    return np.ascontiguousarray(la), np.ascontiguousarray(lb)


def make_rope(c):
    b, t0 = core_tok(c)
    t = t0 + np.arange(1024)
    row = (t // 64).astype(np.float32)
    col = (t % 64).astype(np.float32)
    inv = (10000.0 ** (-np.arange(16, dtype=np.float32) / 16)).astype(np.float32)
    ar = row[:, None] * inv
    ac = col[:, None] * inv
    tab = np.concatenate([np.cos(ar), np.cos(ac), np.sin(ar), np.sin(ac)], axis=1).astype(np.float32)
    return np.ascontiguousarray(tab.reshape(8, 128, 64).transpose(1, 0, 2))


def make_vecG(inp, l):
    bg = inp['b_gate'][l]
    return np.ascontiguousarray(bg.reshape(4, 16, 128).transpose(2, 0, 1).reshape(128, 64))


def stageA_inputs(inp, l, x, c, shared):
    d = dict(shared)
    d["xT"] = make_xT(x, c)
    d["rope"] = make_rope(c)
    return d


def stageB_inputs(inp, l, x, c, A, shared, xTs):
    m = c % 4
    base = (c // 4) * 4
    kTf = np.concatenate([A[base + i]['kT'] for i in range(4)], axis=2)
    vf = np.concatenate([A[base + i]['vv'] for i in range(4)], axis=0)
    segs = np.zeros((3, 16, 64, 128), np.float32)
    for s in range(3):
        src0 = m - 3 + s
        if src0 >= 0:
            segs[s, 0:8] = A[base + src0]['rsegT'][0:8]
        src1 = m + 3 - s
        if src1 <= 3:
            segs[s, 8:16] = A[base + src1]['rsegT'][8:16]
    r = A[c]
    d = dict(shared)
    d.update({"xT": xTs[c], "kTf": kTf, "vf": vf, "segs": segs})
    for k in ("yaT", "ydT", "bzT", "qT", "czs", "bonus", "rQT", "rYl", "rGH"):
        d[k] = r[k]
    return d


_PROGS = {}


def _prog(name, builder):
    if name not in _PROGS:
        p = Prog()
        builder(p)
        _PROGS[name] = p.finalize()
    return _PROGS[name]


def kernel_unfused(**inputs):
    inp = {k: np.asarray(v) for k, v in inputs.items()}
    x = np.ascontiguousarray(inp['x'], dtype=np.float32)
    ncA = _prog("A", build_stage_A)
    ncB = _prog("B", build_stage_B)
    ncC = _prog("C", build_stage_C)
    cores = list(range(8))
    consts = make_consts16()
    for l in range(4):
        la, lb = make_lora(inp, l)
        vecF, vecT = make_vecF(inp, l), make_vecT(inp, l)
        sharedA = {"w_inA": np.ascontiguousarray(inp['w_in'][l][:, :6912]), "lora_a": la, "lora_b": lb,
                   "vecF": vecF, "vecT": vecT, "consts": consts}
        xTs = [make_xT(x, c) for c in cores]
        in_maps = []
        for c in cores:
            d = dict(sharedA)
            d["xT"] = xTs[c]
            d["rope"] = make_rope(c)
            in_maps.append(d)
        A = run_bass_kernel_spmd(ncA, in_maps, core_ids=cores).results
        sharedB = {"w_g": np.ascontiguousarray(inp['w_in'][l][:, 6912:]), "w_br": np.ascontiguousarray(inp['w_branch'][l]),
                   "w_out": np.ascontiguousarray(inp['w_out'][l]), "vecF": vecF, "vecT": vecT, "vecG": make_vecG(inp, l),
                   "consts": consts}
        in_maps = [stageB_inputs(inp, l, x, c, A, sharedB, xTs) for c in cores]
        B = run_bass_kernel_spmd(ncB, in_maps, core_ids=cores).results
        xn = np.empty_like(x)
        for c in cores:
            b, t0 = core_tok(c)
            xn[b, t0:t0 + 1024, :] = B[c]['xoT'].T
        x = xn
    vecF = make_vecF(inp, 0, final=True)
    in_maps = []
    for c in cores:
        b, t0 = core_tok(c)
        in_maps.append({"xT": np.ascontiguousarray(x[b, t0:t0 + 1024, :].T), "vecF": vecF})
    C = run_bass_kernel_spmd(ncC, in_maps, core_ids=cores).results
    out = np.empty_like(x)
    for c in cores:
        b, t0 = core_tok(c)
        out[b, t0:t0 + 1024, :] = C[c]['yT'].T
    return out


GROUPS = [[0, 1, 2, 3], [4, 5, 6, 7]]
PKR = 3072


def build_fused(p, n_layers=4):
    L = n_layers
    x0T = p.dram("x0T", [2048, NX], F32, kind="ExternalInput")
    w_in_all = p.dram("w_in", [L, 2048, 15104], F32, kind="ExternalInput")
    lora_a_all = p.dram("lora_a", [L, 2048, 384], F32, kind="ExternalInput")
    lora_b_all = p.dram("lora_b", [L, 96, 4, 512], F32, kind="ExternalInput")
    w_br_all = p.dram("w_br", [L, 4, 512, 2048], F32, kind="ExternalInput")
    w_out_all = p.dram("w_out", [L, 2048, 2048], F32, kind="ExternalInput")
    vecF_all = p.dram("vecF", [L + 1, 128, VF_N], F32, kind="ExternalInput")
    vecT_all = p.dram("vecT", [L, 128, VT_N], F32, kind="ExternalInput")
    vecG_all = p.dram("vecG", [L, 128, 64], F32, kind="ExternalInput")
    rope_d = p.dram("rope", [128, 8, 64], F32, kind="ExternalInput")
    consts_d = p.dram("consts", [128, NCONST, 128], F32, kind="ExternalInput")
    sel_d = p.dram("sel", [128, 16], F32, kind="ExternalInput")
    yT_out = p.dram("yT", [2048, NT], F32, kind="ExternalOutput")
    xTb = [x0T, p.dram("xTs1", [2048, NX], F32), p.dram("xTs2", [2048, NX], F32)]
    yaT = p.dram("s_yaT", [512, NT], F32)
    ydT = p.dram("s_ydT", [512, NT], F32)
    bzT = p.dram("s_bzT", [512, NT], F32)
    qT = p.dram("s_qT", [64, 8, NT], F32)
    czs = p.dram("s_czs", [NT, 512], F32)
    bonus = p.dram("s_bonus", [2, NT, 512], F32)
    QT = p.dram("s_rQT", [16, 8, 64, 128], F32)
    Yl = p.dram("s_rYl", [16, 8, 128, 64], F32)
    GH = p.dram("s_rGH", [16, 8, 64, 128], F32)
    pc = p.dram("s_pc", [9, NT, 512], F32)
    pk_v = p.dram("s_pkv", [1024, 128], F32)
    pk_k = p.dram("s_pkk", [1024, 128], F32)
    pk_s = p.dram("s_pks", [1024, 128], F32)
    g_v = p.dram("s_gv", [4096, 128], F32)
    g_k = p.dram("s_gk", [4096, 128], F32)
    g_s = p.dram("s_gs", [4096, 128], F32)
    edge = p.dram("s_edge", [2048, 32], F32)
    edges = p.dram("s_edges", [4 * 2048, 32], F32)
    vv_v = pk_v
    kT_v = pk_k.view(pk_k.h.rearrange("(k g a) b -> k g (a b)", k=64, g=2, a=8))
    seg_v = pk_s.view(pk_s.h.rearrange("(u k) n -> u k n", u=16))
    vecF = p.sb("vecF_s", [128, VF_N], F32)
    vecT = p.sb("vecT_s", [128, VT_N], F32)
    vecG = p.sb("vecG_s", [128, 64], F32)
    rope = p.sb("rope_s", [128, 8, 64], F32)
    consts = p.sb("consts_s", [128, NCONST, 128], F32)
    sel = p.sb("sel_s", [128, 16], F32)
    pss = [p.ps(f"ps{i}", [128, 512], F32) for i in range(8)]
    p.dma("sp", rope[:, :, :], rope_d[:, :, :])
    p.dma("sp", consts[:, :, :], consts_d[:, :, :])
    p.dma("sp", sel[:, :], sel_d[:, :])
    ones = consts[:, 5, :]

    def k_src(g, j):
        return V(g_k, g_k.h[j * 1024:(j + 1) * 1024, :].rearrange("(k g a) b -> k g (a b)", k=64, g=2, a=8)[:, g, :], None)

    def v_src(g, j):
        return V(g_v, g_v.h[j * 1024:(j + 1) * 1024, g * 64:(g + 1) * 64].rearrange("(t p) d -> p t d", p=128), None)

    def seg_src(j, us):
        return V(g_s, g_s.h[j * 1024:(j + 1) * 1024, :].rearrange("(u k) n -> k u n", u=16)[:, us, :], None)

    def compose(p, ST, sg, pss):
        cur = 0
        with p.scope():
            tmp = p.sb("r2_ctmp", [128, 16, 64], F32)
            for s in range(4):
                p.dma("sp", sg[0:64, 0:8, :], seg_src(s, slice(0, 8)))
                p.dma("sp", sg[0:64, 8:16, :], seg_src(3 - s, slice(8, 16)))
                a, b = ST[cur], ST[1 - cur]
                for e in range(2):
                    ps = pss[e]
                    for i in range(8):
                        u = e * 8 + i
                        p.I("pe", "matmul", out=ps[0:64, i * 64:(i + 1) * 64], lhsT=sg[:, u, 64:128], rhs=a[:, u, :],
                            start=True, stop=True)
                    us = slice(e * 8, e * 8 + 8)
                    p.I("dve", "tensor_tensor", out=tmp[0:64, us, :], in0=ps[0:64, :].rr("p (i t) -> p i t", t=64),
                        in1=sg[0:64, us, 0:64], op=ALU.add)
                    p.I("pool", "tensor_tensor", out=tmp[0:64, us, :], in0=tmp[0:64, us, :], in1=a[0:64, us, :], op=ALU.subtract)
                    col = e * 4 + s
                    p.I("dve", "scalar_tensor_tensor", out=b[0:64, us, :], in0=tmp[0:64, us, :], scalar=sel[0:64, col:col + 1],
                        in1=a[0:64, us, :], op0=ALU.mult, op1=ALU.add)
                cur = 1 - cur
        return cur

    for l in range(L):
        xT_cur, xT_nxt = xTb[0 if l == 0 else 1 + (l - 1) % 2], xTb[1 + l % 2]
        w_in = w_in_all.view(w_in_all.h[l])
        w_g = w_in_all.view(w_in_all.h[l][:, 6912:])
        lora_a = lora_a_all.view(lora_a_all.h[l])
        lora_b = lora_b_all.view(lora_b_all.h[l])
        w_br = w_br_all.view(w_br_all.h[l])
        w_out = w_out_all.view(w_out_all.h[l])
        p.dma("sp", vecF[:, :], vecF_all[l, :, :])
        p.dma("sp", vecT[:, :], vecT_all[l, :, :])
        p.dma("sp", vecG[:, :], vecG_all[l, :, :])
        with p.scope():
            th = p.sb("th_s", [96, 4, NT], F32)
            with p.scope():
                hT = p.sb("hT", [128, 16, NX], BF16)
                wts = [p.sb(f"wt{i}", [128, 16, 512], BF16) for i in range(2)]
                rmsnorm_hT(p, xT_cur, vecF, hT, ones, pss)
                branch_AD(p, hT, w_in, vecF, ones, wts, pss, yaT, ydT, bzT)
                attn_prep(p, hT, w_in, vecT, rope, consts[:, 0, :], wts, pss, qT, kT_v, vv_v)
                rwkv_gemms(p, hT, w_in, lora_a, wts, pss, pc, czs, th)
            rwkv_pass1(p, pc, lora_b, th, vecT, consts, pss, bonus, None, QT, Yl, GH, seg_v)
        for a_, b_ in ((pk_v, g_v), (pk_k, g_k), (pk_s, g_s)):
            p.I("pool", "collective_compute", kind="AllGather", op=ALU.bypass, replica_groups=GROUPS,
                ins=[a_[:, :]], outs=[b_[:, :]])
        with p.scope():
            yT = p.sb("yT_s", [128, 4, 4, NT], BF16)
            p.dma("pool", yT[:, 0, :, :], V(yaT, yaT.h.rearrange("(j p) t -> p j t", p=128), None))
            p.dma("pool", yT[:, 3, :, :], V(ydT, ydT.h.rearrange("(j p) t -> p j t", p=128), None))
            attention(p, qT, None, None, bzT, consts, pss, yT, kv_src=(k_src, v_src))
            rwkv_pass2(p, QT, Yl, GH, None, bonus, czs, vecT, consts, pss, yT, compose=compose)
            with p.scope():
                hT = p.sb("hT", [128, 16, NX], BF16)
                rmsnorm_hT(p, xT_cur, vecF, hT, ones, pss)
                wts = [p.sb(f"wt{i}", [128, 16, 512], BF16) for i in range(2)]
                out_v = xT_nxt.view(xT_nxt.h[:, HALO:HALO + NT])
                merge_out(p, xT_cur, w_g, w_br, w_out, vecG, hT, wts, pss, yT, out_v, edge_d=(edge if l < L - 1 else None))
        if l < L - 1:
            p.I("pool", "collective_compute", kind="AllGather", op=ALU.bypass, replica_groups=GROUPS,
                ins=[edge[:, :]], outs=[edges[:, :]])
            with p.scope():
                eg = p.sb("hx_eg", [128, 4, 16, 32], F32)
                hl = p.sb("hx_l", [128, 16, 16], F32)
                hr = p.sb("hx_r", [128, 16, 16], F32)
                for j in range(4):
                    p.dma("sp", eg[:, j, :, :], V(edges, edges.h[j * 2048:(j + 1) * 2048, :].rearrange("(c p) n -> p c n", p=128), None))
                for j in range(4):
                    if j == 0:
                        p.I("dve", "tensor_scalar", out=hl[:, :, :], in0=eg[:, j, :, 16:32], scalar1=sel[:, 8 + j:9 + j], scalar2=None, op0=ALU.mult)
                        p.I("dve", "tensor_scalar", out=hr[:, :, :], in0=eg[:, j, :, 0:16], scalar1=sel[:, 12 + j:13 + j], scalar2=None, op0=ALU.mult)
                    else:
                        p.I("dve", "scalar_tensor_tensor", out=hl[:, :, :], in0=eg[:, j, :, 16:32], scalar=sel[:, 8 + j:9 + j],
                            in1=hl[:, :, :], op0=ALU.mult, op1=ALU.add)
                        p.I("dve", "scalar_tensor_tensor", out=hr[:, :, :], in0=eg[:, j, :, 0:16], scalar=sel[:, 12 + j:13 + j],
                            in1=hr[:, :, :], op0=ALU.mult, op1=ALU.add)
                p.dma("sp", V(xT_nxt, xT_nxt.h[:, 0:HALO].rearrange("(c p) n -> p c n", p=128), None), hl[:, :, :])
                p.dma("sp", V(xT_nxt, xT_nxt.h[:, HALO + NT:NX].rearrange("(c p) n -> p c n", p=128), None), hr[:, :, :])
    xT_fin = xTb[1 + (L - 1) % 2]
    p.dma("sp", vecF[:, :], vecF_all[L, :, :])
    with p.scope():
        xs = p.sb("fn_xs", [128, 16, NT], F32)
        sq = [p.sb(f"fn_sq{i}", [128, 512], F32) for i in range(2)]
        rstd = p.sb("fn_rstd", [128, 512], F32)
        ot = [p.sb(f"fn_ot{i}", [128, 512], F32) for i in range(2)]
        for cg in range(4):
            src = V(xT_fin, xT_fin.h[cg * 512:(cg + 1) * 512, HALO:HALO + NT].rearrange("(c p) n -> p c n", p=128), None)
            p.dma("sp", xs[:, cg * 4:(cg + 1) * 4, :], src)
        for tb in range(2):
            tsl = slice(tb * 512, (tb + 1) * 512)
            ps = pss[tb]
            for d in range(16):
                s = sq[d % 2]
                p.I("act", "activation", out=s[:, :], in_=xs[:, d, tsl], func=AF.Square)
                p.I("pe", "matmul", acc=(d > 0), out=ps[:, :], lhsT=ones, rhs=s[:, :], start=(d == 0), stop=(d == 15))
            rsqrt(p, rstd[:, :], ps[:, :], 1.0 / 2048, RMS_EPS)
            for d in range(16):
                o = ot[d % 2]
                p.I("dve", "scalar_tensor_tensor", out=o[:, :], in0=xs[:, d, tsl], scalar=vecF[:, VF_NORMG + d:VF_NORMG + d + 1],
                    in1=rstd[:, :], op0=ALU.mult, op1=ALU.mult)
                p.dma("sp", yT_out[d * 128:(d + 1) * 128, tsl], o[:, :], final=True)


def make_sel(c):
    m = c % 4
    s = np.zeros((128, 16), np.float32)
    for k in range(4):
        s[:, k] = 1.0 if k < m else 0.0
        s[:, 4 + k] = 1.0 if (3 - k) > m else 0.0
        s[:, 8 + k] = 1.0 if k == m - 1 else 0.0
        s[:, 12 + k] = 1.0 if k == m + 1 else 0.0
    return s


def fused_inputs(inp, L):
    loras = [make_lora(inp, l) for l in range(L)]
    shared = {
        "w_in": np.ascontiguousarray(inp['w_in'][:L]),
        "lora_a": np.stack([a for a, _ in loras]), "lora_b": np.stack([b for _, b in loras]),
        "w_br": np.ascontiguousarray(inp['w_branch'][:L]), "w_out": np.ascontiguousarray(inp['w_out'][:L]),
        "vecF": np.stack([make_vecF(inp, l) for l in range(L)] + [make_vecF(inp, 0, final=True)]),
        "vecT": np.stack([make_vecT(inp, l) for l in range(L)]),
        "vecG": np.stack([make_vecG(inp, l) for l in range(L)]),
        "consts": make_consts16(),
    }
    x = np.ascontiguousarray(inp['x'], dtype=np.float32)
    maps = []
    for c in range(8):
        d = dict(shared)
        d["x0T"] = make_xT(x, c)
        d["rope"] = make_rope(c)
        d["sel"] = make_sel(c)
        maps.append(d)
    return maps


_FUSED = {}


def kernel(**inputs):
    inp = {k: np.asarray(v) for k, v in inputs.items()}
    if "nc" not in _FUSED:
        p = Prog()
        build_fused(p, 4)
        _FUSED["nc"] = p.finalize()
    maps = fused_inputs(inp, 4)
    res = run_bass_kernel_spmd(_FUSED["nc"], maps, core_ids=list(range(8))).results
    out = np.empty((2, 4096, 2048), np.float32)
    for c in range(8):
        b, t0 = core_tok(c)
        out[b, t0:t0 + 1024, :] = res[c]['yT'].T
    return out
```

```python
import numpy as np
import concourse.bass as bass
import concourse.mybir as mybir
from concourse.bass_utils import run_bass_kernel_spmd

F32 = mybir.dt.float32
BF16 = mybir.dt.bfloat16
I32 = mybir.dt.int32
ALU = mybir.AluOpType
AF = mybir.ActivationFunctionType
AX = mybir.AxisListType

SEM_CAP = 4000
N_DMA_SEMS = 24


class V:
    __slots__ = ("t", "ap", "key")

    def __init__(self, t, ap, key):
        self.t, self.ap, self.key = t, ap, key

    def rr(self, pat, **kw):
        return V(self.t, self.ap.rearrange(pat, **kw), self.key)

    def bc(self, shape):
        return V(self.t, self.ap.to_broadcast(list(shape)), self.key)

    def __getitem__(self, idx):
        return V(self.t, self.ap[idx], self.key)

    def us(self, axis):
        return V(self.t, self.ap.unsqueeze(axis), self.key)

    def bitcast(self, dt):
        return V(self.t, self.ap.bitcast(dt), self.key)


class T:
    def __init__(self, h, name):
        self.h, self.name = h, name
        self.recs = {}
        self.is_psum = False

    def __getitem__(self, idx):
        return V(self, self.h[idx], None)

    def k(self, key):
        return _TK(self, key)

    def view(self, ap, name=None):
        t = T(ap, name or self.name)
        t.recs = self.recs
        t.is_psum = self.is_psum
        return t


class _TK:
    def __init__(self, t, key):
        self.t, self.key = t, key

    def __getitem__(self, idx):
        return V(self.t, self.t.h[idx], self.key)


class _Scope:
    def __init__(self, p):
        self.p = p

    def __enter__(self):
        self.n = len(self.p._ctx)
        return self

    def __exit__(self, *a):
        self.p.barrier()
        while len(self.p._ctx) > self.n:
            self.p._ctx.pop().__exit__(None, None, None)


class _Rec:
    __slots__ = ("w", "r")

    def __init__(self):
        self.w, self.r = None, []


OUT_NAMES = ("out", "accum_out", "ap")


class Prog:
    ENG = ("pe", "act", "dve", "pool", "sp")

    def __init__(self):
        self.nc = bass.Bass("TRN2", target_bir_lowering=False)
        nc = self.nc
        self.eng = {"pe": nc.tensor, "act": nc.scalar, "dve": nc.vector, "pool": nc.gpsimd, "sp": nc.sync}
        self.ops = []
        self._ctx = []
        self.n_names = 0
        self.out_dma_ops = []

    def _enter(self, cm):
        h = cm.__enter__()
        self._ctx.append(cm)
        return h

    def _uniq(self, name):
        self.n_names += 1
        return f"{name}_{self.n_names}"

    def sb(self, name, shape, dt=F32):
        name = self._uniq(name)
        return T(self._enter(self.nc.sbuf_tensor(name, list(shape), dt)), name)

    def ps(self, name, shape, dt=F32):
        name = self._uniq(name)
        t = T(self._enter(self.nc.psum_tensor(name, list(shape), dt)), name)
        t.is_psum = True
        return t

    def dram(self, name, shape, dt=F32, kind="Internal"):
        return T(self.nc.dram_tensor(name, list(shape), dt, kind=kind).ap(), name)

    def sem(self, name):
        return self._enter(self.nc.semaphore(name))

    def I(self, eng, method, acc=False, **kw):
        reads, writes = [], []
        for k, v in kw.items():
            if isinstance(v, V):
                (writes if k in OUT_NAMES else reads).append(v)
            elif k == "ins" and isinstance(v, list):
                reads.extend(v)
            elif k == "outs" and isinstance(v, list):
                writes.extend(v)
        if acc:
            for v in list(writes):
                reads.append(v)
        is_dma = method in ("dma_start", "collective_compute")
        op = dict(eng=eng, method=method, kw=kw, reads=reads, writes=writes, dma=is_dma,
                  idx=len(self.ops), mark=False, deps=[])
        self.ops.append(op)
        return op

    def barrier(self):
        last = {}
        dmas = []
        for op in self.ops:
            if op["dma"]:
                dmas.append(op["idx"])
            elif op["method"] is not None:
                last[op["eng"]] = op["idx"]
        deps = [self.ops[i] for i in (list(last.values()) + dmas[-6 * N_DMA_SEMS:])]
        for e in self.ENG:
            op = dict(eng=e, method=None, kw={}, reads=[], writes=[], dma=False,
                      idx=len(self.ops), mark=False, deps=[], xdeps=[d for d in deps])
            self.ops.append(op)

    def mark(self):
        return len(self.ops)

    def hoist(self, start, to):
        moved = self.ops[start:]
        del self.ops[start:]
        self.ops[to:to] = moved

    def scope(self):
        return _Scope(self)

    def dma(self, q, out, in_, final=False, **kw):
        op = self.I(q, "dma_start", out=out, in_=in_, **kw)
        if final:
            self.out_dma_ops.append(op)
        return op

    @staticmethod
    def _recs(v, create):
        t = v.t
        if v.key is None:
            ks = list(t.recs.keys())
            if None not in t.recs and create:
                t.recs[None] = _Rec()
                ks.append(None)
            return [t.recs[k] for k in ks]
        out = []
        if v.key not in t.recs and create:
            t.recs[v.key] = _Rec()
        if v.key in t.recs:
            out.append(t.recs[v.key])
        if None in t.recs:
            out.append(t.recs[None])
        return out

    def finalize(self):
        ops = self.ops
        for i, op in enumerate(ops):
            op["idx"] = i
        fin = dict(eng="pool", method=None, kw={}, reads=[], writes=[], dma=False, idx=len(ops),
                   mark=False, deps=[o["idx"] for o in self.out_dma_ops])
        for op in ops:
            raw, other = set(), set()
            for v in op["reads"]:
                for rec in self._recs(v, False):
                    if rec.w is not None:
                        raw.add(rec.w)
                    if v.t.is_psum:
                        other.update(rec.r)
            for v in op["writes"]:
                for rec in self._recs(v, False):
                    if rec.w is not None:
                        other.add(rec.w)
                    other.update(rec.r)
            deps = []
            for d in raw | other:
                if d == op["idx"]:
                    continue
                dop = ops[d]
                if dop["eng"] == op["eng"] and not dop["dma"] and not op["dma"]:
                    if op["eng"] == "pe":
                        continue
                    if d not in raw:
                        continue
                deps.append(d)
            op["deps"] = deps + [d["idx"] for d in op.get("xdeps", []) if not (d["eng"] == op["eng"] and not d["dma"]) and d["idx"] < op["idx"]]
            for v in op["reads"]:
                if v.key is None:
                    if None not in v.t.recs:
                        v.t.recs[None] = _Rec()
                    for rec in v.t.recs.values():
                        rec.r.append(op["idx"])
                else:
                    if v.key not in v.t.recs:
                        v.t.recs[v.key] = _Rec()
                        if None in v.t.recs:
                            v.t.recs[v.key].w = v.t.recs[None].w
                            v.t.recs[v.key].r = list(v.t.recs[None].r)
                    v.t.recs[v.key].r.append(op["idx"])
            for v in op["writes"]:
                if v.key is None:
                    if None not in v.t.recs:
                        v.t.recs[None] = _Rec()
                    for rec in v.t.recs.values():
                        rec.w = op["idx"]
                        rec.r = []
                else:
                    if v.key not in v.t.recs:
                        v.t.recs[v.key] = _Rec()
                    rec = v.t.recs[v.key]
                    rec.w = op["idx"]
                    rec.r = []
        allops = ops + [fin]
        for op in allops:
            for d in op["deps"]:
                ops[d]["mark"] = True
        for op in ops:
            if op["dma"]:
                op["mark"] = True
        nc = self.nc
        eng_sems = {e: [] for e in self.ENG}
        eng_cnt = {e: 0 for e in self.ENG}
        dma_sems = [self.sem(f"dq{i}") for i in range(N_DMA_SEMS)]
        dma_val = [0] * N_DMA_SEMS
        dma_rr = 0
        cc_sem, cc_val = [None], [0]
        for op in ops:
            if not op["mark"]:
                continue
            if op["method"] == "collective_compute":
                if cc_sem[0] is None:
                    cc_sem[0] = self.sem("ccsem")
                op["prev_ev"] = (("c", 0), cc_val[0])
                cc_val[0] += 1
                op["ev"] = (("c", 0), cc_val[0])
                op["inc"] = (cc_sem[0], 1)
            elif op["dma"]:
                s = dma_rr
                dma_rr = (dma_rr + 1) % N_DMA_SEMS
                op["prev_ev"] = (("d", s), dma_val[s])
                dma_val[s] += 16
                op["ev"] = (("d", s), dma_val[s])
                op["inc"] = (dma_sems[s], 16)
            else:
                e = op["eng"]
                if not eng_sems[e] or eng_cnt[e] >= SEM_CAP:
                    eng_sems[e].append(self.sem(f"s_{e}{len(eng_sems[e])}"))
                    eng_cnt[e] = 0
                eng_cnt[e] += 1
                si = len(eng_sems[e]) - 1
                op["ev"] = ((e, si), eng_cnt[e])
                op["inc"] = (eng_sems[e][si], 1)

        def semobj(sid):
            if sid[0] == "c":
                return cc_sem[0]
            return dma_sems[sid[1]] if sid[0] == "d" else eng_sems[sid[0]][sid[1]]

        waited = {e: {} for e in self.ENG}
        n_wait = 0
        for op in allops:
            e = op["eng"]
            E = self.eng[e]
            evs = [ops[d]["ev"] for d in op["deps"]]
            if op["dma"]:
                evs.append(op["prev_ev"])
            need = {}
            for sid, val in evs:
                if val <= 0:
                    continue
                if waited[e].get(sid, 0) >= val:
                    continue
                need[sid] = max(need.get(sid, 0), val)
            for sid, val in need.items():
                E.wait_ge(semobj(sid), val)
                waited[e][sid] = val
                n_wait += 1
            if op["method"] is None:
                continue
            kw = {k: (v.ap if isinstance(v, V) else ([x.ap for x in v] if k in ("ins", "outs") else v)) for k, v in op["kw"].items()}
            ins = getattr(E, op["method"])(**kw)
            if op["mark"]:
                ins.then_inc(*op["inc"])
        self.stats = dict(n_ops=len(ops), n_wait=n_wait,
                          n_sems=N_DMA_SEMS + sum(len(v) for v in eng_sems.values()))
        return nc

    def close(self):
        for cm in reversed(self._ctx):
            cm.__exit__(None, None, None)
        self._ctx = []


NT = 1024
HALO = 16
NX = NT + 2 * HALO
BLKS = [(0, 512), (512, 512), (1024, 32)]
RMS_EPS = 1e-6
LN_EPS = 1e-5
GN_EPS = 64e-5

VF_NORMG = 0
VF_CONVA = 16
VF_DWW = 28
VF_DWB = 152
VF_LNG = 156
VF_LNB = 160
VF_N = 164


def load_w(p, wt, wsrc, c0, ncols, q="pool"):
    for cg in range(4):
        src = V(wsrc, wsrc.h[cg * 512:(cg + 1) * 512, c0:c0 + ncols].rearrange("(c p) n -> p c n", p=128), None)
        p.dma(q, wt[:, cg * 4:(cg + 1) * 4, 0:ncols], src)


def rsqrt(p, out, in_, scale, eps):
    p.I("act", "activation", out=out, in_=in_, func=AF.Sqrt, bias=eps, scale=scale)
    p.I("dve", "reciprocal", out=out, in_=out)


def rmsnorm_hT(p, xT, vecF, hT, ones, pss, ncols=NX, blks=BLKS):
    with p.scope():
        xs = p.sb("xs", [128, 16, ncols], F32)
        sq = [p.sb(f"sq{i}", [128, 512], F32) for i in range(2)]
        rstd = p.sb("rstd", [128, 512], F32)
        for cg in range(4):
            src = V(xT, xT.h[cg * 512:(cg + 1) * 512, :].rearrange("(c p) n -> p c n", p=128), None)
            p.dma("sp", xs[:, cg * 4:(cg + 1) * 4, :], src)
        for bi, (b0, bn) in enumerate(blks):
            ps = pss[bi % 2]
            for d in range(16):
                s = sq[d % 2]
                p.I("act", "activation", out=s[:, 0:bn], in_=xs[:, d, b0:b0 + bn], func=AF.Square)
                p.I("pe", "matmul", acc=(d > 0), out=ps[:, 0:bn], lhsT=ones[:, :], rhs=s[:, 0:bn],
                    start=(d == 0), stop=(d == 15))
            rsqrt(p, rstd[:, 0:bn], ps[:, 0:bn], 1.0 / 2048, RMS_EPS)
            for d in range(16):
                e = "dve"
                p.I(e, "scalar_tensor_tensor", out=hT[:, d, b0:b0 + bn], in0=xs[:, d, b0:b0 + bn],
                    scalar=vecF[:, VF_NORMG + d:VF_NORMG + d + 1], in1=rstd[:, 0:bn], op0=ALU.mult, op1=ALU.mult)


def gemm_F(p, hT, wt, j, ps, b0, bn):
    for d in range(16):
        p.I("pe", "matmul", acc=(d > 0), out=ps[:, 0:bn], lhsT=wt[:, d, j * 128:(j + 1) * 128],
            rhs=hT[:, d, b0:b0 + bn], start=(d == 0), stop=(d == 15))


def branch_A(p, hT, w_in, vecF, ones, wts, pss, yaT_d):
    with p.scope():
        cgx = p.sb("cgx", [128, 4, NX], F32)
        ga = p.sb("ga", [128, 4, NX], F32)
        tmp = [p.sb(f"tmpF{i}", [128, 512], F32) for i in range(2)]
        wi = 0
        pi = 0
        marks = []
        for g in range(4):
            wt = wts[wi % 2]; wi += 1
            m0 = p.mark()
            load_w(p, wt, w_in, g * 512, 512)
            if g >= 1:
                p.hoist(m0, marks[-1])
            marks.append(p.mark())
            for j in range(4):
                for (b0, bn) in BLKS:
                    ps = pss[pi % 2]; pi += 1
                    gemm_F(p, hT, wt, j, ps, b0, bn)
                    if g == 0:
                        p.I("act", "activation", out=ga[:, j, b0:b0 + bn], in_=ps[:, 0:bn], func=AF.Copy)
                    elif g == 1:
                        p.I("act", "activation", out=cgx[:, j, b0:b0 + bn], in_=ps[:, 0:bn], func=AF.Copy)
                    elif g == 2:
                        p.I("dve", "tensor_tensor", out=cgx[:, j, b0:b0 + bn], in0=ps[:, 0:bn],
                            in1=cgx[:, j, b0:b0 + bn], op=ALU.mult)
                    else:
                        t = tmp[pi % 2]
                        p.I("act", "activation", out=t[:, 0:bn], in_=ps[:, 0:bn], func=AF.Silu)
                        p.I("dve", "tensor_tensor", out=ga[:, j, b0:b0 + bn], in0=t[:, 0:bn],
                            in1=ga[:, j, b0:b0 + bn], op=ALU.mult)
        ya = p.sb("ya", [128, 4, NT], F32)
        for j in range(4):
            e = "dve"
            w = lambda k: vecF[:, VF_CONVA + j * 3 + k:VF_CONVA + j * 3 + k + 1]
            p.I(e, "tensor_scalar", out=ya[:, j, :], in0=cgx[:, j, HALO - 1:HALO - 1 + NT], scalar1=w(0), scalar2=None,
                op0=ALU.mult)
            for k in (1, 2):
                p.I(e, "scalar_tensor_tensor", out=ya[:, j, :], in0=cgx[:, j, HALO - 1 + k:HALO - 1 + k + NT],
                    scalar=w(k), in1=ya[:, j, :], op0=ALU.mult, op1=ALU.add)
            p.I("pool", "tensor_tensor", out=ya[:, j, :], in0=ya[:, j, :], in1=ga[:, j, HALO:HALO + NT], op=ALU.mult)
            p.dma("sp", yaT_d[j * 128:(j + 1) * 128, :], ya[:, j, :], final=True)


def branch_D_gemms(p, hT, w_in, wts, pss, bzT_d):
    D = dict(u=p.sb("u", [128, 4, NX], F32), sz=p.sb("sz", [128, 4, NT], F32), acc=p.sb("acc", [128, 4, NT], F32),
             tmp=[p.sb(f"tmpD{i}", [128, 512], F32) for i in range(2)], mean=p.sb("mean", [128, 512], F32),
             rs=p.sb("rs", [128, 512], F32))
    u, sz, tmp = D["u"], D["sz"], D["tmp"]
    wi = 0
    pi = 0
    D0 = 2048 + 1280 + 2048
    marks = []
    for g in range(2):
        wt = wts[wi % 2]; wi += 1
        m0 = p.mark()
        load_w(p, wt, w_in, D0 + g * 512, 512)
        if g >= 1:
            p.hoist(m0, marks[-1])
        marks.append(p.mark())
        for j in range(4):
            for (b0, bn) in BLKS:
                ps = pss[pi % 2]; pi += 1
                gemm_F(p, hT, wt, j, ps, b0, bn)
                if g == 0:
                    p.I("act", "activation", out=u[:, j, b0:b0 + bn], in_=ps[:, 0:bn], func=AF.Copy)
                else:
                    t = tmp[pi % 2]
                    p.I("act", "activation", out=t[:, 0:bn], in_=ps[:, 0:bn], func=AF.Sigmoid)
                    p.I("dve", "tensor_tensor", out=u[:, j, b0:b0 + bn], in0=t[:, 0:bn],
                        in1=u[:, j, b0:b0 + bn], op=ALU.mult)
    wt = wts[wi % 2]; wi += 1
    m0 = p.mark()
    load_w(p, wt, w_in, D0 + 1024, 512)
    p.hoist(m0, marks[-1])
    marks.append(p.mark())
    for j in range(4):
        for tb in range(2):
            ps = pss[pi % 2]; pi += 1
            gemm_F(p, hT, wt, j, ps, HALO + tb * 512, 512)
            p.I("act", "activation", out=sz[:, j, tb * 512:(tb + 1) * 512], in_=ps[:, :], func=AF.Silu)
    wt = wts[wi % 2]; wi += 1
    m0 = p.mark()
    load_w(p, wt, w_in, 2048 + 768, 512)
    p.hoist(m0, marks[-1])
    for j in range(4):
        for tb in range(2):
            ps = pss[pi % 2]; pi += 1
            t = tmp[pi % 2]
            gemm_F(p, hT, wt, j, ps, HALO + tb * 512, 512)
            p.I("act", "activation", out=t[:, :], in_=ps[:, :], func=AF.Silu)
            p.dma("sp", bzT_d[j * 128:(j + 1) * 128, tb * 512:(tb + 1) * 512], t[:, :], final=True)
    return D


def branch_D_conv(p, D, vecF):
    u, acc = D["u"], D["acc"]
    for j in range(4):
        w = lambda k: vecF[:, VF_DWW + j * 31 + k:VF_DWW + j * 31 + k + 1]
        p.I("dve", "tensor_scalar", out=acc[:, j, :], in0=u[:, j, HALO - 15:HALO - 15 + NT], scalar1=w(0),
            scalar2=vecF[:, VF_DWB + j:VF_DWB + j + 1], op0=ALU.mult, op1=ALU.add)
        for k in range(1, 31):
            p.I("dve", "scalar_tensor_tensor", out=acc[:, j, :], in0=u[:, j, HALO - 15 + k:HALO - 15 + k + NT],
                scalar=w(k), in1=acc[:, j, :], op0=ALU.mult, op1=ALU.add)


def branch_D_finish(p, D, vecF, ones, pss, ydT_d):
    sz, acc, tmp, mean, rs = D["sz"], D["acc"], D["tmp"], D["mean"], D["rs"]
    for tb in range(2):
        sl = slice(tb * 512, (tb + 1) * 512)
        ps1, ps2 = pss[0], pss[1]
        for j in range(4):
            p.I("pe", "matmul", acc=(j > 0), out=ps1[:, :], lhsT=ones[:, :], rhs=acc[:, j, sl], start=(j == 0), stop=(j == 3))
        for j in range(4):
            t = tmp[j % 2]
            p.I("act", "activation", out=t[:, :], in_=acc[:, j, sl], func=AF.Square)
            p.I("pe", "matmul", acc=(j > 0), out=ps2[:, :], lhsT=ones[:, :], rhs=t[:, :], start=(j == 0), stop=(j == 3))
        p.I("dve", "tensor_scalar", out=mean[:, :], in0=ps1[:, :], scalar1=1.0 / 512, scalar2=None, op0=ALU.mult)
        p.I("dve", "tensor_tensor", out=rs[:, :], in0=mean[:, :], in1=mean[:, :], op=ALU.mult)
        p.I("dve", "scalar_tensor_tensor", out=rs[:, :], in0=ps2[:, :], scalar=1.0 / 512, in1=rs[:, :],
            op0=ALU.mult, op1=ALU.subtract)
        rsqrt(p, rs[:, :], rs[:, :], 1.0, LN_EPS)
        for j in range(4):
            e = "dve" if j % 2 == 0 else "pool"
            p.I(e, "tensor_tensor", out=acc[:, j, sl], in0=acc[:, j, sl], in1=mean[:, :], op=ALU.subtract)
            p.I(e, "tensor_tensor", out=acc[:, j, sl], in0=acc[:, j, sl], in1=rs[:, :], op=ALU.mult)
            p.I("act", "activation", out=acc[:, j, sl], in_=acc[:, j, sl], func=AF.Silu,
                bias=vecF[:, VF_LNB + j:VF_LNB + j + 1], scale=vecF[:, VF_LNG + j:VF_LNG + j + 1])
            p.I(e, "tensor_tensor", out=acc[:, j, sl], in0=acc[:, j, sl], in1=sz[:, j, sl], op=ALU.mult)
    for j in range(4):
        p.dma("sp", ydT_d[j * 128:(j + 1) * 128, :], acc[:, j, :], final=True)


def gemm_T(p, hT, wt, ncols, ps, tok0):
    for d in range(16):
        p.I("pe", "matmul", acc=(d > 0), out=ps[:, 0:ncols], lhsT=hT[:, d, tok0:tok0 + 128],
            rhs=wt[:, d, 0:ncols], start=(d == 0), stop=(d == 15))


VT_MU = 0
VT_W0 = 3072
VT_A0 = 4096
VT_KK = 5120
VT_KA = 5632
VT_RK = 6144
VT_GNG = 6656
VT_GNB = 7168
VT_QG = 7680
VT_KG = 7744
VT_N = 7808


def qk_norm_rope(p, src, nh, g_off, vecT, rope_tt, ident, pst, outT, scale, wk):
    n = nh * 64
    sq, ss, t1, t2, qn, qr = wk["sq"], wk["ss"], wk["t1"], wk["t2"], wk["qn"], wk["qr"]
    p.I("act", "activation", out=sq[:, 0:n], in_=src, func=AF.Square)
    p.I("dve", "tensor_reduce", out=ss[:, 0:nh], in_=sq[:, 0:n].rr("p (h k) -> p h k", k=64), axis=AX.X, op=ALU.add)
    rsqrt(p, ss[:, 0:nh], ss[:, 0:nh], 1.0 / 64, RMS_EPS)
    if scale != 1.0:
        p.I("dve", "tensor_scalar", out=ss[:, 0:nh], in0=ss[:, 0:nh], scalar1=scale, scalar2=None, op0=ALU.mult)
    v3 = lambda t: t[:, 0:n].rr("p (h k) -> p h k", k=64)
    p.I("dve", "tensor_tensor", out=v3(qn), in0=src.rr("p (h k) -> p h k", k=64),
        in1=ss[:, 0:nh].us(2).bc([128, nh, 64]), op=ALU.mult)
    p.I("dve", "tensor_tensor", out=v3(qn), in0=v3(qn),
        in1=vecT[:, g_off:g_off + 64].us(1).bc([128, nh, 64]), op=ALU.mult)
    v5 = lambda t: t[:, 0:n].rr("p (h a b k) -> p h a b k", a=2, b=2, k=16)
    x0 = v5(qn)[:, :, :, 0, :]
    x1 = v5(qn)[:, :, :, 1, :]
    o0 = v5(qr)[:, :, :, 0, :]
    o1 = v5(qr)[:, :, :, 1, :]
    cs = rope_tt[:, 0:32].rr("p (a k) -> p a k", k=16).us(1).bc([128, nh, 2, 16])
    sn = rope_tt[:, 32:64].rr("p (a k) -> p a k", k=16).us(1).bc([128, nh, 2, 16])
    h4 = lambda t: t[:, 0:n // 2].rr("p (h a k) -> p h a k", a=2, k=16)
    p.I("dve", "tensor_tensor", out=h4(t1), in0=x0, in1=cs, op=ALU.mult)
    p.I("pool", "tensor_tensor", out=h4(t2), in0=x1, in1=sn, op=ALU.mult)
    p.I("dve", "tensor_tensor", out=o0, in0=h4(t1), in1=h4(t2), op=ALU.subtract)
    p.I("pool", "tensor_tensor", out=h4(t2), in0=x1, in1=cs, op=ALU.mult)
    p.I("dve", "tensor_tensor", out=h4(t1), in0=x0, in1=sn, op=ALU.mult)
    p.I("dve", "tensor_tensor", out=o1, in0=h4(t1), in1=h4(t2), op=ALU.add)
    for h0 in range(0, nh, 4):
        hn = min(4, nh - h0)
        for h in range(h0, h0 + hn):
            p.I("pe", "transpose", out=pst[0:64, (h - h0) * 128:(h - h0 + 1) * 128], in_=qr[:, h * 64:(h + 1) * 64],
                identity=ident)
        p.I("act", "activation", out=outT[:, h0:h0 + hn, :], in_=pst[0:64, 0:hn * 128].rr("p (h t) -> p h t", t=128),
            func=AF.Copy)


def attn_prep(p, hT, w_in, vecT, rope, ident, wts, pss, qT_d, kT_d, v_d):
    with p.scope():
        wk = dict(sq=p.sb("aq_sq", [128, 512], F32), ss=p.sb("aq_ss", [128, 8], F32),
                  t1=p.sb("aq_t1", [128, 256], F32), t2=p.sb("aq_t2", [128, 256], F32),
                  qn=p.sb("aq_qn", [128, 512], F32), qr=p.sb("aq_qr", [128, 512], F32))
        qs = [p.sb(f"aq_qs{i}", [128, 512], F32) for i in range(2)]
        kvs = [p.sb(f"aq_kvs{i}", [128, 256], F32) for i in range(2)]
        qT = [p.sb(f"aq_qT{i}", [64, 8, 128], F32) for i in range(2)]
        kT = [p.sb(f"aq_kT{i}", [64, 2, 128], F32) for i in range(2)]
        wq, wkv = wts
        load_w(p, wq, w_in, 2048, 512)
        load_w(p, wkv, w_in, 2560, 256)

        def gemms(tt):
            tok0 = HALO + tt * 128
            ps = pss[tt % 2]
            gemm_T(p, hT, wq, 512, ps, tok0)
            p.I("act", "activation", out=qs[tt % 2][:, :], in_=ps[:, :], func=AF.Copy)
            ps = pss[4 + tt % 2]
            gemm_T(p, hT, wkv, 256, ps, tok0)
            p.I("act", "activation", out=kvs[tt % 2][:, :], in_=ps[:, 0:256], func=AF.Copy)
            p.dma("sp", v_d[tt * 128:(tt + 1) * 128, :], kvs[tt % 2][:, 128:256], final=True)

        def post(tt):
            qk_norm_rope(p, qs[tt % 2][:, :], 8, VT_QG, vecT, rope[:, tt, :], ident, pss[2 + tt % 2], qT[tt % 2], 0.125, wk)
            p.dma("sp", qT_d[:, :, tt * 128:(tt + 1) * 128], qT[tt % 2][:, :, :], final=True)
            qk_norm_rope(p, kvs[tt % 2][:, 0:128], 2, VT_KG, vecT, rope[:, tt, :], ident, pss[6 + tt % 2], kT[tt % 2], 1.0, wk)
            p.dma("sp", kT_d[:, :, tt * 128:(tt + 1) * 128], kT[tt % 2][:, :, :], final=True)

        gemms(0)
        for tt in range(8):
            if tt + 1 < 8:
                gemms(tt + 1)
            post(tt)


C_R0 = 2048 + 1280
NCONST = 16


def rwkv_gemms(p, hT, w_in, lora_a_d, wts, pss, pc_d, czs_d, th):
    with p.scope():
        ev = [p.sb(f"rg_ev{i}", [128, 512], F32) for i in range(3)]
        n = 0
        marks = []
        for q in range(4):
            wt = wts[q % 2]
            m0 = p.mark()
            load_w(p, wt, w_in, C_R0 + q * 512, 512)
            if q >= 1:
                p.hoist(m0, marks[-1])
            marks.append(p.mark())
            for tt in range(8):
                for win, sh in enumerate((0, -1, 1)):
                    if q == 3 and win > 0:
                        continue
                    ps = pss[n % 4]
                    e = ev[n % 3]
                    n += 1
                    gemm_T(p, hT, wt, 512, ps, HALO + tt * 128 + sh)
                    if q == 3:
                        p.I("act", "activation", out=e[:, :], in_=ps[:, :], func=AF.Silu)
                        p.dma("sp", czs_d[tt * 128:(tt + 1) * 128, :], e[:, :], final=True)
                    else:
                        p.I("act", "activation", out=e[:, :], in_=ps[:, :], func=AF.Copy)
                        p.dma("sp", pc_d[q * 3 + win, tt * 128:(tt + 1) * 128, :], e[:, :])
        wt = wts[0]
        m0 = p.mark()
        load_w(p, wt, lora_a_d, 0, 384)
        p.hoist(m0, marks[-1])
        for lo in range(4):
            for tb in range(2):
                ps = pss[4 + (lo * 2 + tb) % 2]
                for d in range(16):
                    p.I("pe", "matmul", acc=(d > 0), out=ps[0:96, :], lhsT=wt[:, d, lo * 96:(lo + 1) * 96],
                        rhs=hT[:, d, HALO + tb * 512:HALO + (tb + 1) * 512], start=(d == 0), stop=(d == 15))
                p.I("act", "activation", out=th[:, lo, tb * 512:(tb + 1) * 512], in_=ps[0:96, :],
                    func=(AF.Tanh if lo < 2 else AF.Copy))


def rwkv_pass1(p, pc_d, lora_b_d, th, vecT, consts, pss, bonus_d, v2_d, QT_d, Yl_d, GH_d, segT_d):
    ident = consts[:, 0, :]
    ones = consts[:, 5, :]
    with p.scope():
        lb = p.sb("rw_lb", [96, 4, 512], F32)
        p.dma("sp", lb[:, :, :], lora_b_d[:, :, :])
        S = p.sb("rw_S", [128, 17, 1024], F32)
        names = ["r2", "k2", "v2", "lw", "aa", "kk", "kt", "bb", "cum", "wt", "Bt", "Kt", "Rt", "Bh", "Kh", "t1", "t2"]
        sl = {n: i for i, n in enumerate(names)}
        W = lambda n: S.k(n)[:, sl[n], :]
        We = lambda n, e: S.k(n)[:, sl[n], e * 512:(e + 1) * 512]
        X = p.sb("rw_X", [128, 16, 128], F32)
        rkv = p.sb("rw_rkv", [128, 9, 512], F32)
        ssq = p.sb("rw_ssq", [128, 16], F32)
        NU = 4
        T5 = [p.sb(f"rw_T5_{i}", [128, 5, 128], F32) for i in range(NU)]
        NP = [p.sb(f"rw_NP_{i}", [128, 2, 2, 128], F32) for i in range(NU)]
        TR = [p.sb(f"rw_TR_{i}", [128, 4, 128], F32) for i in range(NU)]
        for t_ in TR:
            p.I("pool", "memset", ap=t_[:, :, :], constant=0.0)
        OUTQ = [p.sb(f"rw_oq_{i}", [64, 128], F32) for i in range(NU)]
        OUTY = [p.sb(f"rw_oy_{i}", [128, 64], F32) for i in range(NU)]
        OUTG = [p.sb(f"rw_og_{i}", [64, 128], F32) for i in range(NU)]
        dgW = p.sb("rw_dgW", [64, 16, 64], F32)
        omka = p.sb("rw_omka", [128, 512], F32)
        vt = lambda off, n=512: vecT[:, off:off + n]
        p.I("dve", "tensor_scalar", out=omka[:, :], in0=vt(VT_KA), scalar1=-1.0, scalar2=1.0, op0=ALU.mult, op1=ALU.add)
        import os
        for tt in range(8):
            tsl = slice(tt * 128, (tt + 1) * 128)
            p.dma("sp", rkv[:, :, :], V(pc_d.t if isinstance(pc_d, V) else pc_d, pc_d.h[0:9, tsl, :].rearrange("q t c -> t q c"), None))
            for e in range(2):
                for qi, nm in enumerate(("r2", "k2", "v2")):
                    cur = rkv[:, qi * 3, :]
                    sh = rkv[:, qi * 3 + 1 + e, :]
                    eng = "dve" if (qi + e) % 2 == 0 else "pool"
                    p.I(eng, "tensor_tensor", out=We(nm, e), in0=sh, in1=cur, op=ALU.subtract)
                    p.I(eng, "tensor_tensor", out=We(nm, e), in0=We(nm, e), in1=vt(VT_MU + (e * 3 + qi) * 512), op=ALU.mult)
                    p.I(eng, "tensor_tensor", out=We(nm, e), in0=We(nm, e), in1=cur, op=ALU.add)
            for e in range(2):
                for kind in range(2):
                    lo = kind * 2 + e
                    ps = pss[lo % 2]
                    p.I("pe", "matmul", out=ps[:, :], lhsT=th[:, lo, tsl], rhs=lb[:, lo, :], start=True, stop=True)
                    dst = We("lw" if kind == 0 else "aa", e)
                    p.I("dve", "tensor_tensor", out=dst, in0=ps[:, :], in1=vt((VT_W0 if kind == 0 else VT_A0) + e * 512), op=ALU.add)
                    p.I("act", "activation", out=dst, in_=dst, func=AF.Sigmoid)
            p.I("pool", "tensor_scalar", out=W("lw"), in0=W("lw"), scalar1=-0.6065306597126334, scalar2=None, op0=ALU.mult)
            for e in range(2):
                p.I("dve", "tensor_tensor", out=We("kk", e), in0=We("k2", e), in1=vt(VT_KK), op=ALU.mult)
                p.I("pool", "tensor_tensor", out=We("kt", e), in0=We("aa", e), in1=vt(VT_KA), op=ALU.mult)
                p.I("pool", "tensor_tensor", out=We("kt", e), in0=We("kt", e), in1=omka[:, :], op=ALU.add)
                p.I("pool", "tensor_tensor", out=We("kt", e), in0=We("kt", e), in1=We("k2", e), op=ALU.mult)
            p.I("act", "activation", out=W("t1"), in_=W("kk"), func=AF.Square)
            p.I("dve", "tensor_reduce", out=ssq[:, :], in_=W("t1").rr("p (u k) -> p u k", k=64), axis=AX.X, op=ALU.add)
            p.I("dve", "tensor_scalar", out=ssq[:, :], in0=ssq[:, :], scalar1=1e-24, scalar2=None, op0=ALU.max)
            rsqrt(p, ssq[:, :], ssq[:, :], 1.0, 0.0)
            p.I("dve", "tensor_tensor", out=W("kk").rr("p (u k) -> p u k", k=64), in0=W("kk").rr("p (u k) -> p u k", k=64),
                in1=ssq[:, :].us(2).bc([128, 16, 64]), op=ALU.mult)
            p.I("pool", "tensor_tensor", out=W("bb"), in0=W("kk"), in1=W("aa"), op=ALU.mult)
            for e in range(2):
                p.I("dve", "tensor_tensor", out=We("t1", e), in0=We("r2", e), in1=We("kt", e), op=ALU.mult)
                p.I("dve", "tensor_tensor", out=We("t1", e), in0=We("t1", e), in1=vt(VT_RK), op=ALU.mult)
            p.I("dve", "tensor_reduce", out=ssq[:, :], in_=W("t1").rr("p (u k) -> p u k", k=64), axis=AX.X, op=ALU.add)
            p.I("dve", "tensor_tensor", out=W("t2").rr("p (u k) -> p u k", k=64), in0=W("v2").rr("p (u k) -> p u k", k=64),
                in1=ssq[:, :].us(2).bc([128, 16, 64]), op=ALU.mult)
            for e in range(2):
                p.dma("sp", bonus_d[e, tsl, :], We("t2", e), final=True)
            for e in range(2):
                psc, pst_ = pss[2 + e], pss[4 + e]
                p.I("pe", "matmul", out=psc[:, :], lhsT=consts[:, 2 if e == 0 else 1, :], rhs=We("lw", e), start=True, stop=True)
                p.I("pe", "matmul", out=pst_[:, :], lhsT=ones, rhs=We("lw", e), start=True, stop=True)
                p.I("act", "activation", out=We("cum", e), in_=psc[:, :], func=AF.Copy)
                p.I("act", "activation", out=We("wt", e), in_=psc[:, :], func=AF.Exp)
                p.I("dve", "tensor_tensor", out=We("Rt", e), in0=We("r2", e), in1=We("wt", e), op=ALU.mult)
                p.I("act", "activation", out=We("wt", e), in_=psc[:, :], func=AF.Exp, scale=-1.0)
                p.I("dve", "tensor_tensor", out=We("Bt", e), in0=We("bb", e), in1=We("wt", e), op=ALU.mult)
                p.I("pool", "tensor_tensor", out=We("Kt", e), in0=We("kt", e), in1=We("wt", e), op=ALU.mult)
                p.I("dve", "tensor_tensor", out=We("t1", e), in0=pst_[:, :], in1=We("cum", e), op=ALU.subtract)
                p.I("act", "activation", out=We("t1", e), in_=We("t1", e), func=AF.Exp)
                p.I("dve", "tensor_tensor", out=We("Bh", e), in0=We("bb", e), in1=We("t1", e), op=ALU.mult)
                p.I("pool", "tensor_tensor", out=We("Kh", e), in0=We("kt", e), in1=We("t1", e), op=ALU.mult)
                p.I("dve", "tensor_tensor", out=We("t2", e), in0=We("cum", e), in1=We("lw", e), op=ALU.subtract)
                p.I("act", "activation", out=We("t2", e), in_=We("t2", e), func=AF.Exp)
                p.I("dve", "scalar_tensor_tensor", out=X[:, e * 8:(e + 1) * 8, 0:64], in0=We("kk", e).rr("p (u k) -> p u k", k=64),
                    scalar=-1.0, in1=We("t2", e).rr("p (u k) -> p u k", k=64), op0=ALU.mult, op1=ALU.mult)
                p.I("act", "activation", out=We("t2", e)[0:64, :], in_=pst_[0:64, :], func=AF.Exp)
                p.I("dve", "tensor_tensor", out=dgW[:, e * 8:(e + 1) * 8, :], in0=We("t2", e)[0:64, :].rr("p (u k) -> p u k", k=64),
                    in1=consts[0:64, 0, 0:64].us(1).bc([64, 8, 64]), op=ALU.mult)
            def unit(u, slot):
                e, h = divmod(u, 8)
                cs = slice(u * 64, (u + 1) * 64)
                t5, npw, tr = T5[slot], NP[slot], TR[slot]
                psa, psb = pss[slot * 2], pss[slot * 2 + 1]
                Vv = W("v2")[:, cs]
                for i, src in enumerate((X.k(u)[:, u, 0:64], W("Bt")[:, cs], W("Kt")[:, cs], W("Rt")[:, cs])):
                    p.I("pe", "transpose", out=psa[0:64, i * 128:(i + 1) * 128], in_=src, identity=ident)
                p.I("act", "activation", out=tr[0:64, :, :], in_=psa[0:64, :].rr("p (i t) -> p i t", t=128), func=AF.Copy)
                yield
                AT, BT, KT, RT = (tr[:, i, :] for i in range(4))
                mm = lambda out, l, r: p.I("pe", "matmul", out=out, lhsT=l, rhs=r, start=True, stop=True)
                mm(psb[:, 0:128], AT, BT)
                mm(psb[:, 128:256], BT, AT)
                mm(psb[:, 256:384], KT, AT)
                p.I("dve", "tensor_tensor", out=t5[:, 0:3, :], in0=psb[:, 0:384].rr("p (i t) -> p i t", t=128),
                    in1=consts[:, 6 + e * 5:9 + e * 5, :], op=ALU.mult)
                mm(psa[:, 0:128], KT, RT)
                mm(psa[:, 128:256], BT, RT)
                p.I("dve", "tensor_tensor", out=t5[:, 3:5, :], in0=psa[:, 0:256].rr("p (i t) -> p i t", t=128),
                    in1=consts[:, 9 + e * 5:11 + e * 5, :], op=ALU.mult)
                yield
                mm(psb[:, 0:64], t5[:, 2, :], Vv)
                p.I("act", "activation", out=X.k(u)[:, u, 64:128], in_=psb[:, 0:64], func=AF.Copy)
                p.I("pool", "tensor_copy", out=npw[:, 0, :, :], in_=t5[:, 0:2, :])
                yield
                for lvl in range(7):
                    cur = npw[:, lvl % 2, :, :]
                    nxt = npw[:, (lvl + 1) % 2, :, :]
                    ps = psa if lvl % 2 == 0 else psb
                    mm(ps[:, 0:128], cur[:, 1, :], X.k(u)[:, u, :])
                    if lvl < 6:
                        mm(ps[:, 128:256], cur[:, 1, :], cur[:, 0, :])
                        mm(ps[:, 256:384], cur[:, 0, :], cur[:, 1, :])
                    p.I("dve", "tensor_tensor", out=X.k(u)[:, u, :], in0=ps[:, 0:128], in1=X.k(u)[:, u, :], op=ALU.add)
                    if lvl < 6:
                        p.I("act", "activation", out=nxt, in_=ps[:, 128:384].rr("p (i t) -> p i t", t=128), func=AF.Copy)
                    yield
                P_, U_ = X.k(u)[:, u, 0:64], X.k(u)[:, u, 64:128]
                mm(psa[0:64, 0:128], P_, t5[:, 4, :])
                p.I("pe", "matmul", out=psa[:, 128:192], lhsT=t5[:, 3, :], rhs=Vv, start=True, stop=False)
                p.I("pe", "matmul", acc=True, out=psa[:, 128:192], lhsT=t5[:, 4, :], rhs=U_, start=False, stop=True)
                mm(psa[0:64, 192:256], P_, W("Bh")[:, cs])
                p.I("pe", "matmul", out=psa[0:64, 256:320], lhsT=W("Bh")[:, cs], rhs=U_, start=True, stop=False)
                p.I("pe", "matmul", acc=True, out=psa[0:64, 256:320], lhsT=W("Kh")[:, cs], rhs=Vv, start=False, stop=True)
                oq, oy, og = OUTQ[slot], OUTY[slot], OUTG[slot]
                p.I("dve", "tensor_tensor", out=oq[:, :], in0=psa[0:64, 0:128], in1=tr[0:64, 3, :], op=ALU.add)
                p.I("act", "activation", out=oy[:, :], in_=psa[:, 128:192], func=AF.Copy)
                p.I("dve", "tensor_tensor", out=og[:, 0:64], in0=psa[0:64, 192:256], in1=dgW[:, u, :], op=ALU.add)
                p.I("act", "activation", out=og[:, 64:128], in_=psa[0:64, 256:320], func=AF.Copy)
                p.dma("sp", QT_d[u, tt, :, :], oq[:, :], final=True)
                p.dma("sp", Yl_d[u, tt, :, :], oy[:, :], final=True)
                p.dma("sp", GH_d[u, tt, :, :], og[:, :], final=True)
                yield

            for u0 in range(0, 16, NU):
                gens = [unit(u0 + i, i) for i in range(NU)]
                while gens:
                    for g in list(gens):
                        try:
                            next(g)
                        except StopIteration:
                            gens.remove(g)
    import os
    if 1:
      with p.scope():
        st = [p.sb(f"rw_st{i}", [128, 16, 128], F32) for i in range(2)]
        gh = [p.sb(f"rw_gh{i}", [128, 8, 128], F32) for i in range(2)]
        for t_ in st + gh:
            p.I("pool", "memset", ap=t_[:, :, :], constant=0.0)
        p.I("dve", "tensor_copy", out=st[0][0:64, :, 64:128], in_=consts[0:64, 0, 0:64].us(1).bc([64, 16, 64]))
        for e in range(2):
            for step in range(8):
                c = step if e == 0 else 7 - step
                g = gh[step % 2]
                src = V(GH_d, GH_d.h[e * 8:(e + 1) * 8, c, :, :].rearrange("u k n -> k u n"), None)
                p.dma("sp", g[0:64, :, :], src)
                a, b = st[step % 2], st[(step + 1) % 2]
                for half in range(2):
                    ps = pss[half]
                    for i in range(4):
                        u = e * 8 + half * 4 + i
                        p.I("pe", "matmul", out=ps[0:64, i * 128:(i + 1) * 128], lhsT=g[:, half * 4 + i, 0:64], rhs=a[:, u, :],
                            start=True, stop=True)
                    us = slice(e * 8 + half * 4, e * 8 + half * 4 + 4)
                    pv = ps[0:64, :].rr("p (i t) -> p i t", t=128)
                    p.I("dve", "tensor_tensor", out=b[0:64, us, 0:64], in0=pv[:, :, 0:64], in1=g[0:64, half * 4:half * 4 + 4, 64:128], op=ALU.add)
                    p.I("act", "activation", out=b[0:64, us, 64:128], in_=pv[:, :, 64:128], func=AF.Copy)
        fin = st[0]
        for half in range(4):
            ps = pss[2 + half % 2]
            for i in range(4):
                u = half * 4 + i
                p.I("pe", "transpose", out=ps[0:64, i * 128:(i + 1) * 128], in_=fin[:, u, 64:128], identity=consts[:, 0, :])
            p.I("act", "activation", out=st[1][0:64, half * 4:half * 4 + 4, 64:128], in_=ps[0:64, :].rr("p (i t) -> p i t", t=128)[:, :, 0:64], func=AF.Copy)
            p.I("dve", "tensor_copy", out=st[1][0:64, half * 4:half * 4 + 4, 0:64], in_=fin[0:64, half * 4:half * 4 + 4, 0:64])
        p.dma("sp", V(segT_d, segT_d.h.rearrange("u k n -> k u n"), None), st[1][0:64, :, :], final=True)


def build_stage_A(p, parts=('AD', 'attn', 'rgemm', 'rw1')):
    xT = p.dram("xT", [2048, NX], F32, kind="ExternalInput")
    w_in = p.dram("w_inA", [2048, 6912], F32, kind="ExternalInput")
    lora_a = p.dram("lora_a", [2048, 384], F32, kind="ExternalInput")
    lora_b = p.dram("lora_b", [96, 4, 512], F32, kind="ExternalInput")
    vecF_d = p.dram("vecF", [128, VF_N], F32, kind="ExternalInput")
    vecT_d = p.dram("vecT", [128, VT_N], F32, kind="ExternalInput")
    rope_d = p.dram("rope", [128, 8, 64], F32, kind="ExternalInput")
    consts_d = p.dram("consts", [128, NCONST, 128], F32, kind="ExternalInput")
    yaT = p.dram("yaT", [512, NT], F32, kind="ExternalOutput")
    ydT = p.dram("ydT", [512, NT], F32, kind="ExternalOutput")
    bzT = p.dram("bzT", [512, NT], F32, kind="ExternalOutput")
    qT = p.dram("qT", [64, 8, NT], F32, kind="ExternalOutput")
    kT = p.dram("kT", [64, 2, NT], F32, kind="ExternalOutput")
    vv = p.dram("vv", [NT, 128], F32, kind="ExternalOutput")
    czs = p.dram("czs", [NT, 512], F32, kind="ExternalOutput")
    bonus = p.dram("bonus", [2, NT, 512], F32, kind="ExternalOutput")
    QT = p.dram("rQT", [16, 8, 64, 128], F32, kind="ExternalOutput")
    Yl = p.dram("rYl", [16, 8, 128, 64], F32, kind="ExternalOutput")
    GH = p.dram("rGH", [16, 8, 64, 128], F32, kind="ExternalOutput")
    segT = p.dram("rsegT", [16, 64, 128], F32, kind="ExternalOutput")
    pc = p.dram("pc", [9, NT, 512], F32, kind="Internal")
    vecF = p.sb("vecF_s", [128, VF_N], F32)
    vecT = p.sb("vecT_s", [128, VT_N], F32)
    rope = p.sb("rope_s", [128, 8, 64], F32)
    consts = p.sb("consts_s", [128, NCONST, 128], F32)
    th = p.sb("th_s", [96, 4, NT], F32)
    pss = [p.ps(f"ps{i}", [128, 512], F32) for i in range(8)]
    p.dma("sp", vecF[:, :], vecF_d[:, :])
    p.dma("sp", vecT[:, :], vecT_d[:, :])
    p.dma("sp", rope[:, :, :], rope_d[:, :, :])
    p.dma("sp", consts[:, :, :], consts_d[:, :, :])
    ones = consts[:, 5, :]
    with p.scope():
        hT = p.sb("hT", [128, 16, NX], BF16)
        wts = [p.sb(f"wt{i}", [128, 16, 512], BF16) for i in range(2)]
        rmsnorm_hT(p, xT, vecF, hT, ones, pss)
        if 'AD' in parts:
            branch_A(p, hT, w_in, vecF, ones, wts, pss, yaT)
            D = branch_D_gemms(p, hT, w_in, wts, pss, bzT)
            branch_D_conv(p, D, vecF)
            branch_D_finish(p, D, vecF, ones, pss, ydT)
        if 'attn' in parts:
            attn_prep(p, hT, w_in, vecT, rope, consts[:, 0, :], wts, pss, qT, kT, vv)
        if 'rgemm' in parts:
            rwkv_gemms(p, hT, w_in, lora_a, wts, pss, pc, czs, th)
    if 'rw1' in parts:
        rwkv_pass1(p, pc, lora_b, th, vecT, consts, pss, bonus, None, QT, Yl, GH, segT)


G0 = 6912


def load_cast(p, dst, src_t, src_ap, q="pool"):
    p.dma(q, dst, V(src_t, src_ap, None))


def attention(p, qT_d, kTf_d, vf_d, bzT_d, consts, pss, yT, kv_src=None):
    ones_bf = None
    with p.scope():
        kT = p.sb("at_kT", [64, 2, 4096], BF16)
        qT = p.sb("at_qT", [64, 8, NT], BF16)
        vz = p.sb("at_vz", [128, 32, 2, 192], BF16)
        onesb = p.sb("at_ones", [128, 128], BF16)
        bz = p.sb("at_bz", [128, 4, NT], F32)
        PT = [p.sb(f"at_PT{i}", [128, 512], BF16) for i in range(3)]
        rden = p.sb("at_rden", [128, 512], F32)
        p.I("pool", "memset", ap=vz[:, :, :, :], constant=0.0)
        p.I("pool", "memset", ap=onesb[:, :], constant=1.0)
        if kv_src is None:
            k_src = lambda g, j: kTf_d[:, g, j * 1024:(j + 1) * 1024]
            v_src = lambda g, j: V(vf_d, vf_d.h[j * 1024:(j + 1) * 1024, g * 64:(g + 1) * 64].rearrange("(t p) d -> p t d", p=128), None)
        else:
            k_src, v_src = kv_src
        for g in range(2):
            for j in range(4):
                p.dma("pool", kT[:, g, j * 1024:(j + 1) * 1024], k_src(g, j))
        for h0 in range(0, 8, 2):
            p.dma("pool", qT[:, h0:h0 + 2, :], qT_d[:, h0:h0 + 2, :])
        for g in range(2):
            for t4 in range(4):
                p.dma("pool", vz[:, t4 * 8:(t4 + 1) * 8, g, 64:128], v_src(g, t4))
        p.dma("sp", bz[:, :, :], V(bzT_d, bzT_d.h.rearrange("(j p) t -> p j t", p=128), None))
        n = 0
        for h in range(8):
            g = h // 4
            po = (h % 2) * 64
            lo = 64 if h % 2 == 0 else 0
            for qb in range(2):
                qs = slice(qb * 512, (qb + 1) * 512)
                pso, psd = pss[4 + (n % 2)], pss[6 + (n % 2)]
                n += 1
                def score(kt):
                    p.I("pe", "matmul", out=pss[kt % 4][:, :], lhsT=kT[:, g, kt * 128:(kt + 1) * 128], rhs=qT[:, h, qs],
                        start=True, stop=True)
                    p.I("act", "activation", out=PT[kt % 3][:, :], in_=pss[kt % 4][:, :], func=AF.Exp)

                score(0)
                score(1)
                for kt in range(32):
                    pt = PT[kt % 3]
                    p.I("pe", "matmul", acc=(kt > 0), out=pso[:, :], lhsT=vz[:, kt, g, lo:lo + 128], rhs=pt[:, :],
                        start=(kt == 0), stop=(kt == 31))
                    p.I("pe", "matmul", acc=(kt > 0), out=psd[:, :], lhsT=onesb[:, :], rhs=pt[:, :],
                        start=(kt == 0), stop=(kt == 31))
                    if kt + 2 < 32:
                        score(kt + 2)
                ps_ = slice(po, po + 64)
                p.I("dve", "reciprocal", out=rden[ps_, :], in_=psd[ps_, :])
                p.I("dve", "tensor_tensor", out=rden[ps_, :], in0=rden[ps_, :], in1=bz[ps_, h // 2, qs], op=ALU.mult)
                p.I("dve", "tensor_tensor", out=yT[ps_, 1, h // 2, qs], in0=pso[ps_, :], in1=rden[ps_, :], op=ALU.mult)


def compose_host_slots(p, ST, sg, pss, segs_d):
    cur = 0
    for s in range(3):
        p.dma("sp", sg[0:64, :, :], V(segs_d, segs_d.h[s].rearrange("u k n -> k u n"), None))
        a, b = ST[cur], ST[1 - cur]
        for half in range(2):
            ps = pss[half]
            for i in range(8):
                u = half * 8 + i
                p.I("pe", "matmul", out=ps[0:64, i * 64:(i + 1) * 64], lhsT=sg[:, u, 64:128], rhs=a[:, u, :],
                    start=True, stop=True)
            us = slice(half * 8, half * 8 + 8)
            p.I("dve", "tensor_tensor", out=b[0:64, us, :], in0=ps[0:64, :].rr("p (i t) -> p i t", t=64),
                in1=sg[0:64, us, 0:64], op=ALU.add)
        cur = 1 - cur
    return cur


def rwkv_pass2(p, QT_d, Yl_d, GH_d, segs_d, bonus_d, czs_d, vecT, consts, pss, yT, compose=None):
    ident = consts[:, 0, :]
    with p.scope():
        ST = [p.sb(f"r2_ST{i}", [128, 16, 64], F32) for i in range(2)]
        sg = p.sb("r2_sg", [128, 16, 128], F32)
        qt = [p.sb(f"r2_qt{i}", [128, 16, 128], F32) for i in range(2)]
        gh = [p.sb(f"r2_gh{i}", [128, 16, 128], F32) for i in range(2)]
        yl = [p.sb(f"r2_yl{i}", [128, 16, 64], F32) for i in range(2)]
        yn = p.sb("r2_yn", [128, 8, 2, 512], F32)
        for t_ in ST + qt + gh + [sg]:
            p.I("pool", "memset", ap=t_[:, :, :], constant=0.0)
        if compose is not None:
            cur = compose(p, ST, sg, pss)
        else:
            cur = compose_host_slots(p, ST, sg, pss, segs_d)
        for s in range(8):
            q_, g_, y_ = qt[s % 2], gh[s % 2], yl[s % 2]
            for e in range(2):
                c = s if e == 0 else 7 - s
                us = slice(e * 8, e * 8 + 8)
                p.dma("sp", q_[0:64, us, :], V(QT_d, QT_d.h[e * 8:(e + 1) * 8, c].rearrange("u k n -> k u n"), None))
                p.dma("sp", g_[0:64, us, :], V(GH_d, GH_d.h[e * 8:(e + 1) * 8, c].rearrange("u k n -> k u n"), None))
                p.dma("sp", y_[:, us, :], V(Yl_d, Yl_d.h[e * 8:(e + 1) * 8, c].rearrange("u t n -> t u n"), None))
            a, b = ST[cur], ST[1 - cur]
            for e in range(2):
                c = s if e == 0 else 7 - s
                psy, pss_ = pss[2 + e], pss[4 + e]
                for i in range(8):
                    u = e * 8 + i
                    p.I("pe", "matmul", out=psy[:, i * 64:(i + 1) * 64], lhsT=q_[:, u, :], rhs=a[:, u, :], start=True, stop=True)
                    p.I("pe", "matmul", out=pss_[0:64, i * 64:(i + 1) * 64], lhsT=g_[:, u, 0:64], rhs=a[:, u, :], start=True, stop=True)
                us = slice(e * 8, e * 8 + 8)
                p.I("dve", "tensor_tensor", out=yn[:, c, e, :].rr("p (u k) -> p u k", k=64), in0=psy[:, :].rr("p (u k) -> p u k", k=64),
                    in1=y_[:, us, :], op=ALU.add)
                p.I("dve", "tensor_tensor", out=b[0:64, us, :], in0=pss_[0:64, :].rr("p (i t) -> p i t", t=64),
                    in1=g_[0:64, us, 64:128], op=ALU.add)
            cur = 1 - cur
        st1 = p.sb("r2_s1", [128, 16], F32)
        st2 = p.sb("r2_s2", [128, 16], F32)
        sq = p.sb("r2_sq", [128, 2, 512], F32)
        bon = [p.sb(f"r2_bon{i}", [128, 2, 512], F32) for i in range(2)]
        cz = [p.sb(f"r2_cz{i}", [128, 512], F32) for i in range(2)]
        yc = [p.sb(f"r2_yc{i}", [128, 512], F32) for i in range(2)]
        for c in range(8):
            tsl = slice(c * 128, (c + 1) * 128)
            bo, cz_, yc_ = bon[c % 2], cz[c % 2], yc[c % 2]
            p.dma("sp", bo[:, :, :], V(bonus_d, bonus_d.h[:, tsl, :].rearrange("e t c -> t e c"), None))
            p.dma("sp", cz_[:, :], czs_d[tsl, :])
            y2 = yn[:, c, :, :]
            y3 = y2.rr("p e (h k) -> p (e h) k", k=64)
            p.I("dve", "tensor_reduce", out=st1[:, :], in_=y3, axis=AX.X, op=ALU.add)
            p.I("dve", "tensor_scalar", out=st1[:, :], in0=st1[:, :], scalar1=1.0 / 64, scalar2=None, op0=ALU.mult)
            p.I("dve", "tensor_tensor", out=y3, in0=y3, in1=st1[:, :].us(2).bc([128, 16, 64]), op=ALU.subtract)
            p.I("act", "activation", out=sq[:, :, :], in_=y2, func=AF.Square)
            p.I("dve", "tensor_reduce", out=st2[:, :], in_=sq[:, :, :].rr("p e (h k) -> p (e h) k", k=64), axis=AX.X, op=ALU.add)
            rsqrt(p, st2[:, :], st2[:, :], 1.0 / 64, GN_EPS)
            p.I("dve", "tensor_tensor", out=y3, in0=y3, in1=st2[:, :].us(2).bc([128, 16, 64]), op=ALU.mult)
            p.I("pool", "tensor_tensor", out=y2, in0=y2, in1=vecT[:, VT_GNG:VT_GNG + 512].us(1).bc([128, 2, 512]), op=ALU.mult)
            p.I("pool", "tensor_tensor", out=y2, in0=y2, in1=vecT[:, VT_GNB:VT_GNB + 512].us(1).bc([128, 2, 512]), op=ALU.add)
            p.I("dve", "tensor_tensor", out=y2, in0=y2, in1=bo[:, :, :], op=ALU.add)
            p.I("dve", "tensor_tensor", out=yc_[:, :], in0=yn[:, c, 0, :], in1=yn[:, c, 1, :], op=ALU.add)
            p.I("pool", "tensor_tensor", out=yc_[:, :], in0=yc_[:, :], in1=cz_[:, :], op=ALU.mult)
            ps = pss[6 + c % 2]
            for j in range(4):
                p.I("pe", "transpose", out=ps[:, j * 128:(j + 1) * 128], in_=yc_[:, j * 128:(j + 1) * 128], identity=ident)
            p.I("act", "activation", out=yT[:, 2, :, tsl], in_=ps[:, :].rr("p (j t) -> p j t", t=128), func=AF.Copy)


def merge_out(p, xT_d, w_g_d, w_br_d, w_out_d, vecG, hT, wts, pss, yT, out_d, edge_d=None):
    with p.scope():
        mT = p.sb("mo_mT", [128, 16, NT], BF16)
        mg = p.sb("mo_mg", [128, 4, NT], F32)
        wbr = [p.sb(f"mo_wbr{i}", [128, 4, 512], BF16) for i in range(2)]
        gt = [p.sb(f"mo_g{i}", [128, 512], F32) for i in range(2)]
        n = 0
        marks = []
        for dg in range(4):
            for br in range(4):
                wt = wts[n % 2]
                wb = wbr[n % 2]
                n += 1
                m0 = p.mark()
                load_w(p, wt, w_g_d, br * 2048 + dg * 512, 512)
                p.dma("pool", wb[:, :, :], V(w_br_d, w_br_d.h[br, :, dg * 512:(dg + 1) * 512].rearrange("(j p) n -> p j n", p=128), None))
                if marks:
                    p.hoist(m0, marks[-1])
                marks.append(p.mark())
                for dci in range(4):
                    dc = dg * 4 + dci
                    for tb in range(2):
                        tsl = slice(tb * 512, (tb + 1) * 512)
                        psg, psb = pss[(dci * 2 + tb) % 4], pss[4 + (dci * 2 + tb) % 4]
                        gemm_F(p, hT, wt, dci, psg, HALO + tb * 512, 512)
                        for j in range(4):
                            p.I("pe", "matmul", acc=(j > 0), out=psb[:, :], lhsT=wb[:, j, dci * 128:(dci + 1) * 128],
                                rhs=yT[:, br, j, tsl], start=(j == 0), stop=(j == 3))
                        g = gt[(dci * 2 + tb) % 2]
                        p.I("act", "activation", out=g[:, :], in_=psg[:, :], func=AF.Sigmoid,
                            bias=vecG[:, br * 16 + dc:br * 16 + dc + 1], scale=1.0)
                        if br == 0:
                            p.I("dve", "tensor_tensor", out=mg[:, dci, tsl], in0=psb[:, :], in1=g[:, :], op=ALU.mult)
                        else:
                            p.I("dve", "tensor_tensor", out=g[:, :], in0=psb[:, :], in1=g[:, :], op=ALU.mult)
                            if br < 3:
                                p.I("pool", "tensor_tensor", out=mg[:, dci, tsl], in0=mg[:, dci, tsl], in1=g[:, :], op=ALU.add)
                            else:
                                p.I("pool", "tensor_tensor", out=mT[:, dc, tsl], in0=mg[:, dci, tsl], in1=g[:, :], op=ALU.add)
        xr = [p.sb(f"mo_xr{i}", [128, 512], F32) for i in range(2)]
        ot = [p.sb(f"mo_ot{i}", [128, 512], F32) for i in range(2)]
        n = 0
        for og in range(4):
            wt = wts[og % 2]
            m0 = p.mark()
            load_w(p, wt, w_out_d, og * 512, 512)
            p.hoist(m0, marks[-1])
            marks.append(p.mark())
            for oci in range(4):
                oc = og * 4 + oci
                for tb in range(2):
                    tsl = slice(tb * 512, (tb + 1) * 512)
                    ps = pss[n % 4]
                    x_, o_ = xr[n % 2], ot[n % 2]
                    n += 1
                    p.dma("sp", x_[:, :], xT_d[oc * 128:(oc + 1) * 128, HALO + tb * 512:HALO + (tb + 1) * 512])
                    for d in range(16):
                        p.I("pe", "matmul", acc=(d > 0), out=ps[:, :], lhsT=wt[:, d, oci * 128:(oci + 1) * 128],
                            rhs=mT[:, d, tsl], start=(d == 0), stop=(d == 15))
                    p.I("dve", "tensor_tensor", out=o_[:, :], in0=ps[:, :], in1=x_[:, :], op=ALU.add)
                    p.dma("sp", out_d[oc * 128:(oc + 1) * 128, tsl], o_[:, :], final=True)
                    if edge_d is not None:
                        if tb == 0:
                            p.dma("sp", edge_d[oc * 128:(oc + 1) * 128, 0:16], o_[:, 0:16])
                        else:
                            p.dma("sp", edge_d[oc * 128:(oc + 1) * 128, 16:32], o_[:, 496:512])


def build_stage_B(p, parts=("attn", "rw2", "merge")):
    xT = p.dram("xT", [2048, NX], F32, kind="ExternalInput")
    w_g = p.dram("w_g", [2048, 8192], F32, kind="ExternalInput")
    w_br = p.dram("w_br", [4, 512, 2048], F32, kind="ExternalInput")
    w_out = p.dram("w_out", [2048, 2048], F32, kind="ExternalInput")
    vecF_d = p.dram("vecF", [128, VF_N], F32, kind="ExternalInput")
    vecT_d = p.dram("vecT", [128, VT_N], F32, kind="ExternalInput")
    vecG_d = p.dram("vecG", [128, 64], F32, kind="ExternalInput")
    consts_d = p.dram("consts", [128, NCONST, 128], F32, kind="ExternalInput")
    yaT = p.dram("yaT", [512, NT], F32, kind="ExternalInput")
    ydT = p.dram("ydT", [512, NT], F32, kind="ExternalInput")
    bzT = p.dram("bzT", [512, NT], F32, kind="ExternalInput")
    qT = p.dram("qT", [64, 8, NT], F32, kind="ExternalInput")
    kTf = p.dram("kTf", [64, 2, 4096], F32, kind="ExternalInput")
    vf = p.dram("vf", [4096, 128], F32, kind="ExternalInput")
    czs = p.dram("czs", [NT, 512], F32, kind="ExternalInput")
    bonus = p.dram("bonus", [2, NT, 512], F32, kind="ExternalInput")
    QT = p.dram("rQT", [16, 8, 64, 128], F32, kind="ExternalInput")
    Yl = p.dram("rYl", [16, 8, 128, 64], F32, kind="ExternalInput")
    GH = p.dram("rGH", [16, 8, 64, 128], F32, kind="ExternalInput")
    segs = p.dram("segs", [3, 16, 64, 128], F32, kind="ExternalInput")
    out = p.dram("xoT", [2048, NT], F32, kind="ExternalOutput")
    vecF = p.sb("vecF_s", [128, VF_N], F32)
    vecT = p.sb("vecT_s", [128, VT_N], F32)
    vecG = p.sb("vecG_s", [128, 64], F32)
    consts = p.sb("consts_s", [128, NCONST, 128], F32)
    yT = p.sb("yT_s", [128, 4, 4, NT], BF16)
    pss = [p.ps(f"ps{i}", [128, 512], F32) for i in range(8)]
    p.dma("sp", vecF[:, :], vecF_d[:, :])
    p.dma("sp", vecT[:, :], vecT_d[:, :])
    p.dma("sp", vecG[:, :], vecG_d[:, :])
    p.dma("sp", consts[:, :, :], consts_d[:, :, :])
    p.dma("pool", yT[:, 0, :, :], V(yaT, yaT.h.rearrange("(j p) t -> p j t", p=128), None))
    p.dma("pool", yT[:, 3, :, :], V(ydT, ydT.h.rearrange("(j p) t -> p j t", p=128), None))
    if "attn" in parts:
        attention(p, qT, kTf, vf, bzT, consts, pss, yT)
    if "rw2" in parts:
        rwkv_pass2(p, QT, Yl, GH, segs, bonus, czs, vecT, consts, pss, yT)
    if "merge" in parts:
        with p.scope():
            hT = p.sb("hT", [128, 16, NX], BF16)
            rmsnorm_hT(p, xT, vecF, hT, consts[:, 5, :], pss)
            wts = [p.sb(f"wt{i}", [128, 16, 512], BF16) for i in range(2)]
            merge_out(p, xT, w_g, w_br, w_out, vecG, hT, wts, pss, yT, out)
    else:
        dbg = p.dram("dbg_yT", [128, 16, NT], F32, kind="ExternalOutput")
        p.dma("pool", dbg[:, :, :], yT[:, :, :, :].rr("p b j t -> p (b j) t"), final=True)


def build_stage_C(p):
    xT = p.dram("xT", [2048, NT], F32, kind="ExternalInput")
    vecF_d = p.dram("vecF", [128, VF_N], F32, kind="ExternalInput")
    out = p.dram("yT", [2048, NT], F32, kind="ExternalOutput")
    vecF = p.sb("vecF_s", [128, VF_N], F32)
    ones = p.sb("ones_s", [128, 128], F32)
    xs = p.sb("xs", [128, 16, NT], F32)
    sq = [p.sb(f"sq{i}", [128, 512], F32) for i in range(2)]
    rstd = p.sb("rstd", [128, 512], F32)
    ot = [p.sb(f"ot{i}", [128, 512], F32) for i in range(2)]
    pss = [p.ps(f"ps{i}", [128, 512], F32) for i in range(2)]
    p.dma("sp", vecF[:, :], vecF_d[:, :])
    p.I("pool", "memset", ap=ones[:, :], constant=1.0)
    for cg in range(4):
        src = V(xT, xT.h[cg * 512:(cg + 1) * 512, :].rearrange("(c p) n -> p c n", p=128), None)
        p.dma("sp", xs[:, cg * 4:(cg + 1) * 4, :], src)
    for tb in range(2):
        tsl = slice(tb * 512, (tb + 1) * 512)
        ps = pss[tb]
        for d in range(16):
            s = sq[d % 2]
            p.I("act", "activation", out=s[:, :], in_=xs[:, d, tsl], func=AF.Square)
            p.I("pe", "matmul", acc=(d > 0), out=ps[:, :], lhsT=ones[:, :], rhs=s[:, :], start=(d == 0), stop=(d == 15))
        rsqrt(p, rstd[:, :], ps[:, :], 1.0 / 2048, RMS_EPS)
        for d in range(16):
            o = ot[d % 2]
            p.I("dve", "scalar_tensor_tensor", out=o[:, :], in0=xs[:, d, tsl], scalar=vecF[:, VF_NORMG + d:VF_NORMG + d + 1],
                in1=rstd[:, :], op0=ALU.mult, op1=ALU.mult)
            p.dma("sp", out[d * 128:(d + 1) * 128, tsl], o[:, :], final=True)


def core_tok(c):
    return c // 4, (c % 4) * 1024


def make_xT(x, c, halo=16):
    b, t0 = core_tok(c)
    S = x.shape[1]
    out = np.zeros((2048, 1024 + 2 * halo), np.float32)
    lo, hi = t0 - halo, t0 + 1024 + halo
    slo, shi = max(lo, 0), min(hi, S)
    out[:, slo - lo: shi - lo] = x[b, slo:shi, :].T
    return out


def make_vecF(inp, l, final=False):
    v = np.zeros((128, 164), np.float32)
    if final:
        v[:, 0:16] = np.asarray(inp['final_norm_g'], np.float32).reshape(16, 128).T
        return v
    v[:, 0:16] = inp['norm_g'][l].reshape(16, 128).T
    v[:, 16:28] = inp['conv_a_w'][l].reshape(3, 4, 128).transpose(2, 1, 0).reshape(128, 12)
    v[:, 28:152] = inp['dw_w'][l].reshape(31, 4, 128).transpose(2, 1, 0).reshape(128, 124)
    v[:, 152:156] = inp['dw_b'][l].reshape(4, 128).T
    v[:, 156:160] = inp['ln_g'][l].reshape(4, 128).T
    v[:, 160:164] = inp['ln_b'][l].reshape(4, 128).T
    return v


def make_consts16():
    c = np.zeros((128, 16, 128), np.float32)
    i = np.arange(128)
    I = np.eye(128, dtype=np.float32)
    LI = (i[None, :] <= i[:, None]).astype(np.float32)
    UI = (i[None, :] >= i[:, None]).astype(np.float32)
    LS = (i[None, :] < i[:, None]).astype(np.float32)
    US = (i[None, :] > i[:, None]).astype(np.float32)
    for k, m in enumerate([I, LI, UI, LS, US, np.ones((128, 128), np.float32), LS, US, US, UI, UI, US, LS, LS, LI, LI]):
        c[:, k] = m
    return c


def make_vecT(inp, l):
    parts = [inp['mu_rkv'][l].reshape(-1), inp['w0'][l].reshape(-1), inp['a0'][l].reshape(-1), inp['k_k'][l], inp['k_a'][l],
             inp['r_k'][l].reshape(-1), inp['gn_g'][l], inp['gn_b'][l], inp['q_norm_g'][l], inp['k_norm_g'][l]]
    v = np.concatenate([np.asarray(x, np.float32).reshape(-1) for x in parts])
    return np.ascontiguousarray(np.broadcast_to(v[None, :], (128, v.size)))


def make_lora(inp, l):
    la = np.concatenate([inp['w_lora_a'][l][0], inp['w_lora_a'][l][1], inp['a_lora_a'][l][0], inp['a_lora_a'][l][1]], axis=1)
    lb = np.stack([inp['w_lora_b'][l][0], inp['w_lora_b'][l][1], inp['a_lora_b'][l][0], inp['a_lora_b'][l][1]], axis=1)
    return np.ascontiguousarray(la), np.ascontiguousarray(lb)


def make_rope(c):
    b, t0 = core_tok(c)
    t = t0 + np.arange(1024)
    row = (t // 64).astype(np.float32)
    col = (t % 64).astype(np.float32)
    inv = (10000.0 ** (-np.arange(16, dtype=np.float32) / 16)).astype(np.float32)
    ar = row[:, None] * inv
    ac = col[:, None] * inv
    tab = np.concatenate([np.cos(ar), np.cos(ac), np.sin(ar), np.sin(ac)], axis=1).astype(np.float32)
    return np.ascontiguousarray(tab.reshape(8, 128, 64).transpose(1, 0, 2))


def make_vecG(inp, l):
    bg = inp['b_gate'][l]
    return np.ascontiguousarray(bg.reshape(4, 16, 128).transpose(2, 0, 1).reshape(128, 64))


def stageA_inputs(inp, l, x, c, shared):
    d = dict(shared)
    d["xT"] = make_xT(x, c)
    d["rope"] = make_rope(c)
    return d


def stageB_inputs(inp, l, x, c, A, shared, xTs):
    m = c % 4
    base = (c // 4) * 4
    kTf = np.concatenate([A[base + i]['kT'] for i in range(4)], axis=2)
    vf = np.concatenate([A[base + i]['vv'] for i in range(4)], axis=0)
    segs = np.zeros((3, 16, 64, 128), np.float32)
    for s in range(3):
        src0 = m - 3 + s
        if src0 >= 0:
            segs[s, 0:8] = A[base + src0]['rsegT'][0:8]
        src1 = m + 3 - s
        if src1 <= 3:
            segs[s, 8:16] = A[base + src1]['rsegT'][8:16]
    r = A[c]
    d = dict(shared)
    d.update({"xT": xTs[c], "kTf": kTf, "vf": vf, "segs": segs})
    for k in ("yaT", "ydT", "bzT", "qT", "czs", "bonus", "rQT", "rYl", "rGH"):
        d[k] = r[k]
    return d


_PROGS = {}


def _prog(name, builder):
    if name not in _PROGS:
        p = Prog()
        builder(p)
        _PROGS[name] = p.finalize()
    return _PROGS[name]


def kernel_unfused(**inputs):
    inp = {k: np.asarray(v) for k, v in inputs.items()}
    x = np.ascontiguousarray(inp['x'], dtype=np.float32)
    ncA = _prog("A", build_stage_A)
    ncB = _prog("B", build_stage_B)
    ncC = _prog("C", build_stage_C)
    cores = list(range(8))
    consts = make_consts16()
    for l in range(4):
        la, lb = make_lora(inp, l)
        vecF, vecT = make_vecF(inp, l), make_vecT(inp, l)
        sharedA = {"w_inA": np.ascontiguousarray(inp['w_in'][l][:, :6912]), "lora_a": la, "lora_b": lb,
                   "vecF": vecF, "vecT": vecT, "consts": consts}
        xTs = [make_xT(x, c) for c in cores]
        in_maps = []
        for c in cores:
            d = dict(sharedA)
            d["xT"] = xTs[c]
            d["rope"] = make_rope(c)
            in_maps.append(d)
        A = run_bass_kernel_spmd(ncA, in_maps, core_ids=cores).results
        sharedB = {"w_g": np.ascontiguousarray(inp['w_in'][l][:, 6912:]), "w_br": np.ascontiguousarray(inp['w_branch'][l]),
                   "w_out": np.ascontiguousarray(inp['w_out'][l]), "vecF": vecF, "vecT": vecT, "vecG": make_vecG(inp, l),
                   "consts": consts}
        in_maps = [stageB_inputs(inp, l, x, c, A, sharedB, xTs) for c in cores]
        B = run_bass_kernel_spmd(ncB, in_maps, core_ids=cores).results
        xn = np.empty_like(x)
        for c in cores:
            b, t0 = core_tok(c)
            xn[b, t0:t0 + 1024, :] = B[c]['xoT'].T
        x = xn
    vecF = make_vecF(inp, 0, final=True)
    in_maps = []
    for c in cores:
        b, t0 = core_tok(c)
        in_maps.append({"xT": np.ascontiguousarray(x[b, t0:t0 + 1024, :].T), "vecF": vecF})
    C = run_bass_kernel_spmd(ncC, in_maps, core_ids=cores).results
    out = np.empty_like(x)
    for c in cores:
        b, t0 = core_tok(c)
        out[b, t0:t0 + 1024, :] = C[c]['yT'].T
    return out


GROUPS = [[0, 1, 2, 3], [4, 5, 6, 7]]
PKR = 3072


def build_fused(p, n_layers=4):
    L = n_layers
    x0T = p.dram("x0T", [2048, NX], F32, kind="ExternalInput")
    w_in_all = p.dram("w_in", [L, 2048, 15104], F32, kind="ExternalInput")
    lora_a_all = p.dram("lora_a", [L, 2048, 384], F32, kind="ExternalInput")
    lora_b_all = p.dram("lora_b", [L, 96, 4, 512], F32, kind="ExternalInput")
    w_br_all = p.dram("w_br", [L, 4, 512, 2048], F32, kind="ExternalInput")
    w_out_all = p.dram("w_out", [L, 2048, 2048], F32, kind="ExternalInput")
    vecF_all = p.dram("vecF", [L + 1, 128, VF_N], F32, kind="ExternalInput")
    vecT_all = p.dram("vecT", [L, 128, VT_N], F32, kind="ExternalInput")
    vecG_all = p.dram("vecG", [L, 128, 64], F32, kind="ExternalInput")
    rope_d = p.dram("rope", [128, 8, 64], F32, kind="ExternalInput")
    consts_d = p.dram("consts", [128, NCONST, 128], F32, kind="ExternalInput")
    sel_d = p.dram("sel", [128, 16], F32, kind="ExternalInput")
    yT_out = p.dram("yT", [2048, NT], F32, kind="ExternalOutput")
    xTb = [x0T, p.dram("xTs1", [2048, NX], F32), p.dram("xTs2", [2048, NX], F32)]
    yaT = p.dram("s_yaT", [512, NT], F32)
    ydT = p.dram("s_ydT", [512, NT], F32)
    bzT = p.dram("s_bzT", [512, NT], F32)
    qT = p.dram("s_qT", [64, 8, NT], F32)
    czs = p.dram("s_czs", [NT, 512], F32)
    bonus = p.dram("s_bonus", [2, NT, 512], F32)
    QT = p.dram("s_rQT", [16, 8, 64, 128], F32)
    Yl = p.dram("s_rYl", [16, 8, 128, 64], F32)
    GH = p.dram("s_rGH", [16, 8, 64, 128], F32)
    pc = p.dram("s_pc", [9, NT, 512], F32)
    pk_v = p.dram("s_pkv", [1024, 128], F32)
    pk_k = p.dram("s_pkk", [1024, 128], F32)
    pk_s = p.dram("s_pks", [1024, 128], F32)
    g_v = p.dram("s_gv", [4096, 128], F32)
    g_k = p.dram("s_gk", [4096, 128], F32)
    g_s = p.dram("s_gs", [4096, 128], F32)
    edge = p.dram("s_edge", [2048, 32], F32)
    edges = p.dram("s_edges", [4 * 2048, 32], F32)
    vv_v = pk_v
    kT_v = pk_k.view(pk_k.h.rearrange("(k g a) b -> k g (a b)", k=64, g=2, a=8))
    seg_v = pk_s.view(pk_s.h.rearrange("(u k) n -> u k n", u=16))
    vecF = p.sb("vecF_s", [128, VF_N], F32)
    vecT = p.sb("vecT_s", [128, VT_N], F32)
    vecG = p.sb("vecG_s", [128, 64], F32)
    rope = p.sb("rope_s", [128, 8, 64], F32)
    consts = p.sb("consts_s", [128, NCONST, 128], F32)
    sel = p.sb("sel_s", [128, 16], F32)
    pss = [p.ps(f"ps{i}", [128, 512], F32) for i in range(8)]
    p.dma("sp", rope[:, :, :], rope_d[:, :, :])
    p.dma("sp", consts[:, :, :], consts_d[:, :, :])
    p.dma("sp", sel[:, :], sel_d[:, :])
    ones = consts[:, 5, :]

    def k_src(g, j):
        return V(g_k, g_k.h[j * 1024:(j + 1) * 1024, :].rearrange("(k g a) b -> k g (a b)", k=64, g=2, a=8)[:, g, :], None)

    def v_src(g, j):
        return V(g_v, g_v.h[j * 1024:(j + 1) * 1024, g * 64:(g + 1) * 64].rearrange("(t p) d -> p t d", p=128), None)

    def seg_src(j, us):
        return V(g_s, g_s.h[j * 1024:(j + 1) * 1024, :].rearrange("(u k) n -> k u n", u=16)[:, us, :], None)

    def compose(p, ST, sg, pss):
        cur = 0
        with p.scope():
            tmp = p.sb("r2_ctmp", [128, 16, 64], F32)
            for s in range(4):
                p.dma("sp", sg[0:64, 0:8, :], seg_src(s, slice(0, 8)))
                p.dma("sp", sg[0:64, 8:16, :], seg_src(3 - s, slice(8, 16)))
                a, b = ST[cur], ST[1 - cur]
                for e in range(2):
                    ps = pss[e]
                    for i in range(8):
                        u = e * 8 + i
                        p.I("pe", "matmul", out=ps[0:64, i * 64:(i + 1) * 64], lhsT=sg[:, u, 64:128], rhs=a[:, u, :],
                            start=True, stop=True)
                    us = slice(e * 8, e * 8 + 8)
                    p.I("dve", "tensor_tensor", out=tmp[0:64, us, :], in0=ps[0:64, :].rr("p (i t) -> p i t", t=64),
                        in1=sg[0:64, us, 0:64], op=ALU.add)
                    p.I("pool", "tensor_tensor", out=tmp[0:64, us, :], in0=tmp[0:64, us, :], in1=a[0:64, us, :], op=ALU.subtract)
                    col = e * 4 + s
                    p.I("dve", "scalar_tensor_tensor", out=b[0:64, us, :], in0=tmp[0:64, us, :], scalar=sel[0:64, col:col + 1],
                        in1=a[0:64, us, :], op0=ALU.mult, op1=ALU.add)
                cur = 1 - cur
        return cur

    for l in range(L):
        xT_cur, xT_nxt = xTb[0 if l == 0 else 1 + (l - 1) % 2], xTb[1 + l % 2]
        w_in = w_in_all.view(w_in_all.h[l])
        w_g = w_in_all.view(w_in_all.h[l][:, 6912:])
        lora_a = lora_a_all.view(lora_a_all.h[l])
        lora_b = lora_b_all.view(lora_b_all.h[l])
        w_br = w_br_all.view(w_br_all.h[l])
        w_out = w_out_all.view(w_out_all.h[l])
        p.dma("sp", vecF[:, :], vecF_all[l, :, :])
        p.dma("sp", vecT[:, :], vecT_all[l, :, :])
        p.dma("sp", vecG[:, :], vecG_all[l, :, :])
        with p.scope():
            th = p.sb("th_s", [96, 4, NT], F32)
            with p.scope():
                hT = p.sb("hT", [128, 16, NX], BF16)
                wts = [p.sb(f"wt{i}", [128, 16, 512], BF16) for i in range(2)]
                rmsnorm_hT(p, xT_cur, vecF, hT, ones, pss)
                branch_A(p, hT, w_in, vecF, ones, wts, pss, yaT)
                D = branch_D_gemms(p, hT, w_in, wts, pss, bzT)
                attn_prep(p, hT, w_in, vecT, rope, consts[:, 0, :], wts, pss, qT, kT_v, vv_v)
                branch_D_conv(p, D, vecF)
                rwkv_gemms(p, hT, w_in, lora_a, wts, pss, pc, czs, th)
                for a_, b_ in ((pk_v, g_v), (pk_k, g_k)):
                    p.I("pool", "collective_compute", kind="AllGather", op=ALU.bypass, replica_groups=GROUPS,
                        ins=[a_[:, :]], outs=[b_[:, :]])
                branch_D_finish(p, D, vecF, ones, pss, ydT)
            rwkv_pass1(p, pc, lora_b, th, vecT, consts, pss, bonus, None, QT, Yl, GH, seg_v)
        p.I("pool", "collective_compute", kind="AllGather", op=ALU.bypass, replica_groups=GROUPS,
            ins=[pk_s[:, :]], outs=[g_s[:, :]])
        with p.scope():
            yT = p.sb("yT_s", [128, 4, 4, NT], BF16)
            p.dma("pool", yT[:, 0, :, :], V(yaT, yaT.h.rearrange("(j p) t -> p j t", p=128), None))
            p.dma("pool", yT[:, 3, :, :], V(ydT, ydT.h.rearrange("(j p) t -> p j t", p=128), None))
            attention(p, qT, None, None, bzT, consts, pss, yT, kv_src=(k_src, v_src))
            rwkv_pass2(p, QT, Yl, GH, None, bonus, czs, vecT, consts, pss, yT, compose=compose)
            with p.scope():
                hT = p.sb("hT", [128, 16, NX], BF16)
                rmsnorm_hT(p, xT_cur, vecF, hT, ones, pss)
                wts = [p.sb(f"wt{i}", [128, 16, 512], BF16) for i in range(2)]
                out_v = xT_nxt.view(xT_nxt.h[:, HALO:HALO + NT])
                merge_out(p, xT_cur, w_g, w_br, w_out, vecG, hT, wts, pss, yT, out_v, edge_d=(edge if l < L - 1 else None))
        if l < L - 1:
            p.I("pool", "collective_compute", kind="AllGather", op=ALU.bypass, replica_groups=GROUPS,
                ins=[edge[:, :]], outs=[edges[:, :]])
            with p.scope():
                eg = p.sb("hx_eg", [128, 4, 16, 32], F32)
                hl = p.sb("hx_l", [128, 16, 16], F32)
                hr = p.sb("hx_r", [128, 16, 16], F32)
                for j in range(4):
                    p.dma("sp", eg[:, j, :, :], V(edges, edges.h[j * 2048:(j + 1) * 2048, :].rearrange("(c p) n -> p c n", p=128), None))
                for j in range(4):
                    if j == 0:
                        p.I("dve", "tensor_scalar", out=hl[:, :, :], in0=eg[:, j, :, 16:32], scalar1=sel[:, 8 + j:9 + j], scalar2=None, op0=ALU.mult)
                        p.I("dve", "tensor_scalar", out=hr[:, :, :], in0=eg[:, j, :, 0:16], scalar1=sel[:, 12 + j:13 + j], scalar2=None, op0=ALU.mult)
                    else:
                        p.I("dve", "scalar_tensor_tensor", out=hl[:, :, :], in0=eg[:, j, :, 16:32], scalar=sel[:, 8 + j:9 + j],
                            in1=hl[:, :, :], op0=ALU.mult, op1=ALU.add)
                        p.I("dve", "scalar_tensor_tensor", out=hr[:, :, :], in0=eg[:, j, :, 0:16], scalar=sel[:, 12 + j:13 + j],
                            in1=hr[:, :, :], op0=ALU.mult, op1=ALU.add)
                p.dma("sp", V(xT_nxt, xT_nxt.h[:, 0:HALO].rearrange("(c p) n -> p c n", p=128), None), hl[:, :, :])
                p.dma("sp", V(xT_nxt, xT_nxt.h[:, HALO + NT:NX].rearrange("(c p) n -> p c n", p=128), None), hr[:, :, :])
    xT_fin = xTb[1 + (L - 1) % 2]
    p.dma("sp", vecF[:, :], vecF_all[L, :, :])
    with p.scope():
        xs = p.sb("fn_xs", [128, 16, NT], F32)
        sq = [p.sb(f"fn_sq{i}", [128, 512], F32) for i in range(2)]
        rstd = p.sb("fn_rstd", [128, 512], F32)
        ot = [p.sb(f"fn_ot{i}", [128, 512], F32) for i in range(2)]
        for cg in range(4):
            src = V(xT_fin, xT_fin.h[cg * 512:(cg + 1) * 512, HALO:HALO + NT].rearrange("(c p) n -> p c n", p=128), None)
            p.dma("sp", xs[:, cg * 4:(cg + 1) * 4, :], src)
        for tb in range(2):
            tsl = slice(tb * 512, (tb + 1) * 512)
            ps = pss[tb]
            for d in range(16):
                s = sq[d % 2]
                p.I("act", "activation", out=s[:, :], in_=xs[:, d, tsl], func=AF.Square)
                p.I("pe", "matmul", acc=(d > 0), out=ps[:, :], lhsT=ones, rhs=s[:, :], start=(d == 0), stop=(d == 15))
            rsqrt(p, rstd[:, :], ps[:, :], 1.0 / 2048, RMS_EPS)
            for d in range(16):
                o = ot[d % 2]
                p.I("dve", "scalar_tensor_tensor", out=o[:, :], in0=xs[:, d, tsl], scalar=vecF[:, VF_NORMG + d:VF_NORMG + d + 1],
                    in1=rstd[:, :], op0=ALU.mult, op1=ALU.mult)
                p.dma("sp", yT_out[d * 128:(d + 1) * 128, tsl], o[:, :], final=True)


def make_sel(c):
    m = c % 4
    s = np.zeros((128, 16), np.float32)
    for k in range(4):
        s[:, k] = 1.0 if k < m else 0.0
        s[:, 4 + k] = 1.0 if (3 - k) > m else 0.0
        s[:, 8 + k] = 1.0 if k == m - 1 else 0.0
        s[:, 12 + k] = 1.0 if k == m + 1 else 0.0
    return s


def fused_inputs(inp, L):
    loras = [make_lora(inp, l) for l in range(L)]
    shared = {
        "w_in": np.ascontiguousarray(inp['w_in'][:L]),
        "lora_a": np.stack([a for a, _ in loras]), "lora_b": np.stack([b for _, b in loras]),
        "w_br": np.ascontiguousarray(inp['w_branch'][:L]), "w_out": np.ascontiguousarray(inp['w_out'][:L]),
        "vecF": np.stack([make_vecF(inp, l) for l in range(L)] + [make_vecF(inp, 0, final=True)]),
        "vecT": np.stack([make_vecT(inp, l) for l in range(L)]),
        "vecG": np.stack([make_vecG(inp, l) for l in range(L)]),
        "consts": make_consts16(),
    }
    x = np.ascontiguousarray(inp['x'], dtype=np.float32)
    maps = []
    for c in range(8):
        d = dict(shared)
        d["x0T"] = make_xT(x, c)
        d["rope"] = make_rope(c)
        d["sel"] = make_sel(c)
        maps.append(d)
    return maps


_FUSED = {}


def kernel(**inputs):
    inp = {k: np.asarray(v) for k, v in inputs.items()}
    if "nc" not in _FUSED:
        p = Prog()
        build_fused(p, 4)
        _FUSED["nc"] = p.finalize()
    maps = fused_inputs(inp, 4)
    res = run_bass_kernel_spmd(_FUSED["nc"], maps, core_ids=list(range(8))).results
    out = np.empty((2, 4096, 2048), np.float32)
    for c in range(8):
        b, t0 = core_tok(c)
        out[b, t0:t0 + 1024, :] = res[c]['yT'].T
    return out
```
